# Optimizing a Trainium2 kernel written in Bass

```python
import math
import jax, jax.numpy as jnp
from jax import lax
import numpy as np

D_MODEL = 2048
BATCH = 4
SEQ = 2048
DEPTH = 1

CHUNK = 64
N_META = 16
Q_BLOCK = 128
ATT_HEADS = 8
ATT_QK_DIM = 64
ATT_V_DIM = 128
DN_HEADS = 8
DN_K_DIM = 128
DN_V_DIM = 128
DN_CONV = 4
D_FF = 5632
FFN_CONV = 3
ROPE_THETA = 10000.0
LN_EPS = 1e-5
RMS_EPS = 1e-6

ATT_QK_W = ATT_HEADS * 2 * ATT_QK_DIM
ATT_V_W = ATT_HEADS * ATT_V_DIM
DN_QK_W = DN_HEADS * DN_K_DIM
DN_V_W = DN_HEADS * DN_V_DIM
MIX_WIDTH = ATT_V_W + DN_V_W
SPLITS = (ATT_QK_W, ATT_QK_W, ATT_V_W, DN_QK_W, DN_QK_W, DN_V_W, DN_V_W, DN_HEADS, DN_HEADS)
IN_COLS = 3 * 1024 + 4 * 1024 + 2 * DN_HEADS

kernel_name = "hybrid_diffattn_gdn_convffn_deepnorm"


def layer_norm(x, g, b):
    xf = x.astype(jnp.float32)
    mu = jnp.mean(xf, axis=-1, keepdims=True)
    var = jnp.mean(jnp.square(xf - mu), axis=-1, keepdims=True)
    return ((xf - mu) * lax.rsqrt(var + LN_EPS) * g.astype(jnp.float32) + b.astype(jnp.float32)).astype(x.dtype)


def rms_norm(x, w):
    xf = x.astype(jnp.float32)
    return (xf * lax.rsqrt(jnp.mean(jnp.square(xf), axis=-1, keepdims=True) + RMS_EPS) * w.astype(jnp.float32)).astype(x.dtype)


def l2_normalize(x):
    return x * lax.rsqrt(jnp.sum(jnp.square(x), axis=-1, keepdims=True) + RMS_EPS)


def causal_dwconv(x, w):
    k_w = w.shape[0]
    n = x.shape[1]
    xp = jnp.pad(x, ((0, 0), (k_w - 1, 0), (0, 0)))
    y = xp[:, 0:n] * w[0]
    for j in range(1, k_w):
        y = y + xp[:, j:j + n] * w[j]
    return y


def rope(x, pos):
    d = x.shape[-1]
    inv_freq = ROPE_THETA ** (-jnp.arange(0, d, 2, dtype=jnp.float32) / d)
    ang = pos[:, None] * inv_freq[None, :]
    cos = jnp.cos(ang)[None, :, None, None, :]
    sin = jnp.sin(ang)[None, :, None, None, :]
    xf = x.astype(jnp.float32)
    x1, x2 = xf[..., : d // 2], xf[..., d // 2:]
    return jnp.concatenate([x1 * cos - x2 * sin, x2 * cos + x1 * sin], axis=-1)


def diff_attention(q, k, v, lam, lam_init, norm_w, cid):
    b_, n, h, _, dq = q.shape
    scale = dq ** -0.5
    n_real = n - N_META
    bounds = [(0, N_META)] + [(N_META + i * Q_BLOCK, N_META + (i + 1) * Q_BLOCK) for i in range(n_real // Q_BLOCK)]
    vf = v.astype(jnp.float32)
    outs = []
    for s, e in bounds:
        scores = jnp.einsum('bqhcd,bkhcd->bhcqk', q[:, s:e], k[:, :e]) * scale
        mask = cid[None, :e] <= cid[s:e, None]
        p = jax.nn.softmax(jnp.where(mask, scores, -jnp.inf), axis=-1)
        a = p[:, :, 0] - lam * p[:, :, 1]
        outs.append(jnp.einsum('bhqk,bkhe->bqhe', a, vf[:, :e]))
    o = jnp.concatenate(outs, axis=1)
    o = rms_norm(o, norm_w) * (1.0 - lam_init)
    return o.reshape(b_, n, h * o.shape[-1])


def gated_delta_rule(q, k, v, beta, g):
    b_, n, h, dk = q.shape
    dv = v.shape[-1]
    pad = (-n) % CHUNK
    n_p = n + pad
    nc = n_p // CHUNK

    def front_pad(t):
        return jnp.pad(t, ((0, 0), (pad, 0)) + ((0, 0),) * (t.ndim - 2))

    def to_chunks(t):
        t = jnp.swapaxes(front_pad(t), 1, 2)
        return t.reshape(t.shape[:2] + (nc, CHUNK) + t.shape[3:])

    q, k, v, beta, g = (to_chunks(t) for t in (q, k, v, beta, g))
    gc = jnp.cumsum(g, axis=-1)
    idx = jnp.arange(CHUNK)
    tril = idx[:, None] >= idx[None, :]
    strict = idx[:, None] > idx[None, :]
    decay = jnp.exp(jnp.where(tril, gc[..., :, None] - gc[..., None, :], -jnp.inf))
    kb = k * beta[..., None]
    vb = v * beta[..., None]
    m = jnp.where(strict, jnp.einsum('bhncd,bhnsd->bhncs', kb, k) * decay, 0.0)
    eye = jnp.eye(CHUNK, dtype=jnp.float32)
    rhs = jnp.concatenate([vb, kb * jnp.exp(gc)[..., None]], axis=-1)
    sol = lax.linalg.triangular_solve(eye + m, rhs, left_side=True, lower=True)
    u, w = sol[..., :dv], sol[..., dv:]
    qk = jnp.einsum('bhncd,bhnsd->bhncs', q, k) * decay
    q_dec = q * jnp.exp(gc)[..., None]
    k_dec = k * jnp.exp(gc[..., -1:] - gc)[..., None]
    g_last = jnp.exp(gc[..., -1])
    xs = tuple(jnp.moveaxis(t, 2, 0) for t in (u, w, qk, q_dec, k_dec, g_last))

    def step(state, inp):
        u_c, w_c, qk_c, qd_c, kd_c, gl_c = inp
        v_new = u_c - jnp.einsum('bhck,bhkv->bhcv', w_c, state)
        o_c = jnp.einsum('bhck,bhkv->bhcv', qd_c, state) + jnp.einsum('bhcs,bhsv->bhcv', qk_c, v_new)
        state = state * gl_c[..., None, None] + jnp.einsum('bhck,bhcv->bhkv', kd_c, v_new)
        return state, o_c

    s0 = jnp.zeros((b_, h, dk, dv), jnp.float32)
    _, o = lax.scan(step, s0, xs)
    o = o.transpose(1, 0, 3, 2, 4).reshape(b_, n_p, h, dv)
    return o[:, pad:]


def hybrid_mixer(x, w_in, conv_w, a_log, dt_bias, lq1, lk1, lq2, lk2, diff_norm_w, delta_norm_w, w_out, layer_idx, cid, pos):
    b_, n, _ = x.shape
    points, acc = [], 0
    for s in SPLITS[:-1]:
        acc += s
        points.append(acc)
    aq, ak, av, dq, dk, dv, dz, db, da = jnp.split(x @ w_in, points, axis=-1)

    aq = rope(aq.reshape(b_, n, ATT_HEADS, 2, ATT_QK_DIM), pos)
    ak = rope(ak.reshape(b_, n, ATT_HEADS, 2, ATT_QK_DIM), pos)
    av = av.reshape(b_, n, ATT_HEADS, ATT_V_DIM)
    lam_init = 0.8 - 0.6 * math.exp(-0.3 * layer_idx)
    f32 = jnp.float32
    lam = (jnp.exp(jnp.sum(lq1.astype(f32) * lk1.astype(f32))) - jnp.exp(jnp.sum(lq2.astype(f32) * lk2.astype(f32))) + lam_init)
    o_att = diff_attention(aq, ak, av, lam, lam_init, diff_norm_w, cid)

    qkv = jax.nn.silu(causal_dwconv(jnp.concatenate([dq, dk, dv], axis=-1), conv_w)).astype(f32)
    dq, dk, dv = jnp.split(qkv, [DN_QK_W, 2 * DN_QK_W], axis=-1)
    dq = l2_normalize(dq.reshape(b_, n, DN_HEADS, DN_K_DIM)) * (DN_K_DIM ** -0.5)
    dk = l2_normalize(dk.reshape(b_, n, DN_HEADS, DN_K_DIM))
    dv = dv.reshape(b_, n, DN_HEADS, DN_V_DIM)
    beta = jax.nn.sigmoid(db.astype(f32))
    g = -jnp.exp(a_log.astype(f32)) * jax.nn.softplus(da.astype(f32) + dt_bias.astype(f32))
    o_dn = gated_delta_rule(dq, dk, dv, beta, g)
    o_dn = rms_norm(o_dn, delta_norm_w) * jax.nn.silu(dz.reshape(b_, n, DN_HEADS, DN_V_DIM).astype(f32))
    o_dn = o_dn.reshape(b_, n, DN_V_W)

    o = jnp.concatenate([o_att, o_dn], axis=-1).astype(x.dtype)
    return o @ w_out


def conv_ffn(x, w_gate, w_up, conv_w, conv_b, w_down):
    h = causal_dwconv(x @ w_gate, conv_w) + conv_b
    return (jax.nn.silu(h) * (x @ w_up)) @ w_down


def setup_inputs(seed: int = 0) -> dict:
    key = jax.random.key(seed)
    ks = jax.random.split(key, 24)
    beta_dn = (8.0 * DEPTH) ** -0.25
    nrm = jax.random.normal
    dt = jnp.exp(jax.random.uniform(ks[4], (DEPTH, DN_HEADS)) * (math.log(0.1) - math.log(0.001)) + math.log(0.001))
    return {
        "x": nrm(ks[0], (BATCH, SEQ, D_MODEL), jnp.float32),
        "meta_tokens": nrm(ks[1], (N_META, D_MODEL), jnp.float32),
        "w_in": nrm(ks[2], (DEPTH, D_MODEL, IN_COLS), jnp.float32) * D_MODEL ** -0.5,
        "conv_qkv_w": nrm(ks[3], (DEPTH, DN_CONV, 2 * DN_QK_W + DN_V_W), jnp.float32) * DN_CONV ** -0.5,
        "a_log": jnp.log(jax.random.uniform(ks[5], (DEPTH, DN_HEADS), minval=1.0, maxval=16.0)),
        "dt_bias": dt + jnp.log(-jnp.expm1(-dt)),
        "lambda_q1": nrm(ks[6], (DEPTH, ATT_QK_DIM), jnp.float32) * 0.1,
        "lambda_k1": nrm(ks[7], (DEPTH, ATT_QK_DIM), jnp.float32) * 0.1,
        "lambda_q2": nrm(ks[8], (DEPTH, ATT_QK_DIM), jnp.float32) * 0.1,
        "lambda_k2": nrm(ks[9], (DEPTH, ATT_QK_DIM), jnp.float32) * 0.1,
        "diff_norm_w": 1.0 + 0.01 * nrm(ks[10], (DEPTH, ATT_V_DIM), jnp.float32),
        "delta_norm_w": 1.0 + 0.01 * nrm(ks[11], (DEPTH, DN_V_DIM), jnp.float32),
        "w_out": nrm(ks[12], (DEPTH, MIX_WIDTH, D_MODEL), jnp.float32) * MIX_WIDTH ** -0.5 * beta_dn,
        "ln1_g": 1.0 + 0.01 * nrm(ks[13], (DEPTH, D_MODEL), jnp.float32),
        "ln1_b": 0.01 * nrm(ks[14], (DEPTH, D_MODEL), jnp.float32),
        "ffn_w_gate": nrm(ks[15], (DEPTH, D_MODEL, D_FF), jnp.float32) * D_MODEL ** -0.5,
        "ffn_w_up": nrm(ks[16], (DEPTH, D_MODEL, D_FF), jnp.float32) * D_MODEL ** -0.5,
        "ffn_conv_w": nrm(ks[17], (DEPTH, FFN_CONV, D_FF), jnp.float32) * FFN_CONV ** -0.5,
        "ffn_conv_b": 0.01 * nrm(ks[18], (DEPTH, D_FF), jnp.float32),
        "ffn_w_down": nrm(ks[19], (DEPTH, D_FF, D_MODEL), jnp.float32) * D_FF ** -0.5 * beta_dn,
        "ln2_g": 1.0 + 0.01 * nrm(ks[20], (DEPTH, D_MODEL), jnp.float32),
        "ln2_b": 0.01 * nrm(ks[21], (DEPTH, D_MODEL), jnp.float32),
    }


def reference(x, meta_tokens, w_in, conv_qkv_w, a_log, dt_bias, lambda_q1, lambda_k1, lambda_q2, lambda_k2,
              diff_norm_w, delta_norm_w, w_out, ln1_g, ln1_b, ffn_w_gate, ffn_w_up, ffn_conv_w, ffn_conv_b,
              ffn_w_down, ln2_g, ln2_b):
    b_, n, d = x.shape
    meta = jnp.broadcast_to(meta_tokens[None].astype(x.dtype), (b_, N_META, d))
    h = jnp.concatenate([meta, x], axis=1)
    pos = jnp.arange(n + N_META, dtype=jnp.float32)
    cid = jnp.concatenate([jnp.full((N_META,), -1, jnp.int32), jnp.arange(n, dtype=jnp.int32) // CHUNK])
    alpha = (2.0 * DEPTH) ** 0.25
    for l in range(DEPTH):
        mix = hybrid_mixer(h, w_in[l], conv_qkv_w[l], a_log[l], dt_bias[l], lambda_q1[l], lambda_k1[l],
                           lambda_q2[l], lambda_k2[l], diff_norm_w[l], delta_norm_w[l], w_out[l], l, cid, pos)
        h = layer_norm(alpha * h + mix, ln1_g[l], ln1_b[l])
        ffn = conv_ffn(h, ffn_w_gate[l], ffn_w_up[l], ffn_conv_w[l], ffn_conv_b[l], ffn_w_down[l])
        h = layer_norm(alpha * h + ffn, ln2_g[l], ln2_b[l])
    return h[:, N_META:]
```

```python
import math
import numpy as np
from contextlib import ExitStack
import concourse.bass as bass
import concourse.mybir as mybir
from concourse.bass_utils import run_bass_kernel_spmd

F32 = mybir.dt.float32
BF16 = mybir.dt.bfloat16
AF = mybir.ActivationFunctionType
ALU = mybir.AluOpType

D = 2048
T = 2064
NMETA = 16
DFF = 5632
NFF = 44
NTOK = 1026
ALPHA = 2.0 ** 0.25
LAM_INIT = 0.8 - 0.6 * math.exp(0.0)
LN_EPS = 1e-5
RMS_EPS = 1e-6
BIG = 30000.0


class Prog:
    def __init__(self, nc, stack):
        self.nc = nc
        self.stack = stack
        self.sems = {}
        self.engh = {"pe": nc.tensor, "act": nc.scalar, "dve": nc.vector, "pool": nc.gpsimd, "sp": nc.sync}
        self.cnt = {}
        self.seen = {e: {} for e in self.engh}
        self.last_w = {}
        self.readers = {}
        self.out_tokens = []
        self.nops = 0
        import os as _os
        self.same = _os.environ.get("SAME", "1") == "1"

    def _sem(self, k):
        if k not in self.sems:
            self.sems[k] = self.stack.enter_context(self.nc.semaphore("s_" + str(k)))
        return self.sems[k]

    def op(self, eng, fn, reads=(), writes=(), dma=None, out=False):
        deps = {}
        for r in reads:
            t = self.last_w.get(r)
            if t is not None and deps.get(t[0], 0) < t[1]:
                deps[t[0]] = t[1]
        for w in writes:
            t = self.last_w.get(w)
            if t is not None and deps.get(t[0], 0) < t[1]:
                deps[t[0]] = t[1]
            for k, v in self.readers.get(w, {}).items():
                if deps.get(k, 0) < v:
                    deps[k] = v
        key, amt = (eng, 1) if dma is None else (dma, 16)
        self.cnt[key] = self.cnt.get(key, 0) + amt
        tok = (key, self.cnt[key])
        e = self.engh[eng]
        for k, v in deps.items():
            if k == eng and (eng == "pe" or not self.same):
                continue
            if self.seen[eng].get(k, 0) < v:
                self.seen[eng][k] = v
                e.wait_ge(self._sem(k), v)
        fn(e).then_inc(self._sem(key), amt)
        self.nops += 1
        for r in reads:
            d = self.readers.setdefault(r, {})
            if d.get(key, 0) < tok[1]:
                d[key] = tok[1]
        for w in writes:
            self.last_w[w] = tok
            self.readers[w] = {}
        if out:
            self.out_tokens.append(tok)
        return tok

    def join(self, names):
        toks = [self.last_w[n] for n in names]
        k = toks[0][0]
        assert all(t[0] == k for t in toks)
        m = max(t[1] for t in toks)
        for n in names:
            self.last_w[n] = (k, m)

    def barrier(self):
        for eng, e in self.engh.items():
            for k, v in self.cnt.items():
                if self.seen[eng].get(k, 0) < v:
                    self.seen[eng][k] = v
                    e.wait_ge(self._sem(k), v)

    def finish(self):
        fin = {}
        for k, v in self.out_tokens:
            fin[k] = max(fin.get(k, 0), v)
        for k, v in fin.items():
            self.engh["sp"].wait_ge(self._sem(k), v)


def build(stages=("att", "gdn", "post"), att_heads=range(8), dn_heads=range(8), oscr_input=False, dbg=False, lvl=9):
    nc = bass.Bass("TRN2", target_bir_lowering=False)

    def din(name, shape, dt=F32):
        return nc.dram_tensor(name, list(shape), dt, kind="ExternalInput").ap()

    hT_d = din("hT", [D, T])
    hrow_d = din("hrow", [NTOK, D])
    cst_d = din("cst", [128, 8, 128])
    rope_d = din("rope", [2, 128, T])
    waq_d = din("waq", [8, 128, 16, 128])
    wak_d = din("wak", [8, 128, 16, 128])
    wav_d = din("wav", [2, 128, 16, 512])
    wdq_d = din("wdq", [8, 128, 16, 128])
    wdk_d = din("wdk", [8, 128, 16, 128])
    wdv_d = din("wdv", [8, 128, 16, 128])
    wdz_d = din("wdz", [8, 128, 16, 128])
    wba_d = din("wba", [128, 16, 16])
    cw_d = din("cw", [128, 3, 8, 4])
    sm_d = din("sm", [128, 4 * 64 + 8 + 8 + 1 + 128])
    wout_d = din("wout", [128, 16, D])
    lnp_d = din("lnp", [4, 128, D])
    wg_d = din("wg", [NFF, 128, 16, 128])
    wu_d = din("wu", [NFF, 128, 16, 128])
    wd_d = din("wd", [16, 128, NFF, 128])
    fcw_d = din("fcw", [128, NFF, 4])
    out_d = nc.dram_tensor("out", [1024, D], F32, kind="ExternalOutput").ap()
    if oscr_input:
        oscr_d = din("oscr", [16, 128, T], BF16)
    else:
        oscr_d = nc.dram_tensor("oscr", [16, 128, T], BF16, kind="ExternalOutput" if dbg else "Internal").ap()
    h1s_d = nc.dram_tensor("h1s", [1024, D], F32, kind="ExternalOutput" if dbg else "Internal").ap()

    with ExitStack() as top:
        P = Prog(nc, top)

        def sbt(st, name, shape, dt=F32):
            return st.enter_context(nc.sbuf_tensor("sb_" + name, list(shape), dt))

        pb = [top.enter_context(nc.psum_tensor(f"pb{i}", [128, 512], F32)) for i in range(7)]
        pbh = top.enter_context(nc.psum_tensor("pbh", [128, 1024], BF16))

        cst = sbt(top, "cst", [128, 8, 128])
        P.op("sp", lambda e: e.dma_start(out=cst[:], in_=cst_d), writes=["cst"], dma="d_cst0")
        ident = cst[:, 0, :]
        identb = sbt(top, "identb", [128, 128], BF16)
        pswapb = sbt(top, "pswapb", [128, 128], BF16)
        onesb = sbt(top, "onesb", [128, 128], BF16)
        P.op("dve", lambda e: e.tensor_copy(out=identb[:], in_=cst[:, 0, :]), reads=["cst"], writes=["identb"])
        P.op("dve", lambda e: e.tensor_copy(out=pswapb[:], in_=cst[:, 1, :]), reads=["cst"], writes=["pswapb"])
        P.op("dve", lambda e: e.tensor_copy(out=onesb[:], in_=cst[:, 6, :]), reads=["cst"], writes=["onesb"])
        sm = sbt(top, "sm", [128, 401])
        P.op("sp", lambda e: e.dma_start(out=sm[:], in_=sm_d), writes=["sm"], dma="d_cst1")

        if "att" in stages or "gdn" in stages:
            with ExitStack() as mix:
                xT = sbt(mix, "xT", [128, 16, T], BF16)
                for c in range(16):
                    P.op("pool", lambda e, c=c: e.dma_start(out=xT[:, c, :], in_=hT_d[c * 128:(c + 1) * 128, :]),
                         writes=[f"xT{c}"], dma=f"d_xT{c // 4}")
                XT = [f"xT{c}" for c in range(16)]
                for g4 in range(4):
                    P.join(XT[4 * g4:4 * g4 + 4])
                if "att" in stages:
                    with ExitStack() as st:
                        emit_attention(nc, P, st, sbt, pb, pbh, xT, XT, cst, identb, pswapb, onesb, sm, rope_d,
                                       waq_d, wak_d, wav_d, oscr_d, list(att_heads), lvl=lvl)
                    P.barrier()
                if "gdn" in stages:
                    with ExitStack() as st:
                        emit_gdn(nc, P, st, sbt, pb, pbh, xT, XT, cst, sm, wdq_d, wdk_d, wdv_d, wdz_d, wba_d, cw_d,
                                 oscr_d, list(dn_heads))
                    P.barrier()
            P.barrier()

        if "post" in stages:
            with ExitStack() as st:
                emit_post(nc, P, st, sbt, pb, pbh, cst, identb, oscr_d, hrow_d, wout_d, lnp_d, wg_d, wu_d, wd_d, fcw_d,
                          h1s_d, out_d, oscr_input)
        P.barrier()
        P.finish()
    return nc


class Region:
    def __init__(self, tile, nwords):
        self.t = tile; self.n = nwords; self.off = 0

    def reset(self):
        self.off = 0

    def take(self, nelem, dt=F32):
        words = nelem if dt == F32 else (nelem + 1) // 2
        ap = self.t[:, self.off:self.off + words]
        self.off += words
        assert self.off <= self.n, (self.off, self.n)
        return ap if dt == F32 else ap.bitcast(dt)


def emit_post(nc, P, st, sbt, pb, pbh, cst, identb, oscr_d, hrow_d, wout_d, lnp_d, wg_d, wu_d, wd_d, fcw_d,
              h1s_d, out_d, oscr_input):
    ident = cst[:, 0, :]
    pid = nc.sync.partition_id()
    col0 = (pid % 2) * 1024 + 14
    ttiles = [(0, 2)] + [(2 + 128 * i, 128) for i in range(8)]

    RA = Region(sbt(st, "RA", [128, 22528]), 22528)
    RB = Region(sbt(st, "RB", [128, 16384]), 16384)
    RC = Region(sbt(st, "RC", [128, 8208]), 8208)
    fcw = sbt(st, "fcw", [128, NFF, 4])
    P.op("sp", lambda e: e.dma_start(out=fcw[:], in_=fcw_d), writes=["fcw"], dma="d_fcw")
    stats = sbt(st, "stats", [128, 4, 6]); mv = sbt(st, "mv", [128, 2]); rstd = sbt(st, "rstd", [128, 1])
    eps = sbt(st, "eps", [128, 1])
    P.op("dve", lambda e: e.memset(eps[:], LN_EPS), writes=["eps"])
    h1T = RC.take(16 * NTOK, BF16).rearrange("p (c t) -> p c t", c=16)

    def layer_norm_tile(y, n, gsb, bsb, tag):
        yv = y[0:n, :].rearrange("p (c f) -> p c f", c=4)
        for c in range(4):
            P.op("dve", lambda e, c=c: e.bn_stats(out=stats[0:n, c, :], in_=yv[:, c, :]), reads=[tag], writes=["stats"])
        P.op("dve", lambda e: e.bn_aggr(out=mv[0:n, :], in_=stats[0:n, :, :].rearrange("p c s -> p (c s)")),
             reads=["stats"], writes=["mv"])
        P.op("act", lambda e: e.activation(out=rstd[0:n, :], in_=mv[0:n, 1:2], func=AF.Sqrt, bias=eps[0:n, :]),
             reads=["mv", "eps"], writes=["rstd"])
        P.op("dve", lambda e: e.reciprocal(out=rstd[0:n, :], in_=rstd[0:n, :]), reads=["rstd"], writes=["rstd"])
        P.op("dve", lambda e: e.tensor_scalar(out=y[0:n, :], in0=y[0:n, :], scalar1=mv[0:n, 0:1], scalar2=rstd[0:n, 0:1],
                                              op0=ALU.subtract, op1=ALU.mult), reads=[tag, "mv", "rstd"], writes=[tag])
        P.op("pool", lambda e: e.tensor_tensor(out=y[0:n, :], in0=y[0:n, :], in1=gsb[0:n, :], op=ALU.mult),
             reads=[tag, "lng"], writes=[tag])
        P.op("pool", lambda e: e.tensor_tensor(out=y[0:n, :], in0=y[0:n, :], in1=bsb[0:n, :], op=ALU.add),
             reads=[tag, "lnb"], writes=[tag])

    RA.reset(); RB.reset()
    oTm = RA.take(16 * NTOK, BF16).rearrange("p (h t) -> p h t", h=16)
    lng = RA.take(D); lnb = RA.take(D)
    ybuf = [RA.take(D), RA.take(D)]
    h1b = [RA.take(D, BF16), RA.take(D, BF16)]
    wout = RB.take(16 * D, BF16).rearrange("p (h d) -> p h d", h=16)
    OTM = [f"oTm{hd}" for hd in range(16)]
    WOUT = [f"wout{hd}" for hd in range(16)]
    for hd in range(16):
        P.op("sp", lambda e, hd=hd: e.dma_start(out=oTm[:, hd, :], in_=oscr_d[hd, :, bass.ds(col0, NTOK)]),
             reads=[f"oscr{hd}"], writes=[OTM[hd]], dma="d_oTm")
    P.join(OTM)
    for hd in range(16):
        P.op("pool", lambda e, hd=hd: e.dma_start(out=wout[:, hd, :], in_=wout_d[:, hd, :]),
             writes=[WOUT[hd]], dma="d_wout")
    P.join(WOUT)
    P.op("sp", lambda e: e.dma_start(out=lng, in_=lnp_d[0]), writes=["lng"], dma="d_lng")
    P.op("sp", lambda e: e.dma_start(out=lnb, in_=lnp_d[1]), writes=["lnb"], dma="d_lnb")
    for ti, (r0, n) in enumerate(ttiles):
        b = ti % 2
        y = ybuf[b]; ytag = f"y{b}"
        P.op("sp", lambda e, y=y, r0=r0, n=n: e.dma_start(out=y[0:n, :], in_=hrow_d[r0:r0 + n, :]),
             writes=[ytag], dma=f"d_y{b}")
        for db in range(4):
            acc = pb[db]
            for hd in range(16):
                P.op("pe", lambda e, acc=acc, hd=hd, r0=r0, n=n, db=db: e.matmul(
                    acc[0:n, :], oTm[:, hd, r0:r0 + n], wout[:, hd, db * 512:(db + 1) * 512],
                    start=(hd == 0), stop=(hd == 15)), reads=[OTM[hd], WOUT[hd]], writes=[f"pb{db}"])
            P.op("dve", lambda e, acc=acc, y=y, n=n, db=db: e.scalar_tensor_tensor(
                out=y[0:n, db * 512:(db + 1) * 512], in0=y[0:n, db * 512:(db + 1) * 512], scalar=ALPHA,
                in1=acc[0:n, :], op0=ALU.mult, op1=ALU.add), reads=[ytag, f"pb{db}"], writes=[ytag])
        layer_norm_tile(y, n, lng, lnb, ytag)
        hb = h1b[b]
        P.op("act", lambda e, hb=hb, y=y, n=n: e.copy(out=hb[0:n, :], in_=y[0:n, :]), reads=[ytag], writes=[f"h1b{b}"])
        if ti > 0:
            P.op("sp", lambda e, y=y, ti=ti: e.dma_start(out=h1s_d[(ti - 1) * 128:ti * 128, :], in_=y[:, :]),
                 reads=[ytag], writes=["h1s"], dma="d_h1s", out=True)
        for cg in range(2):
            for cc in range(8):
                c = cg * 8 + cc
                P.op("pe", lambda e, hb=hb, n=n, c=c, cc=cc: e.transpose(
                    pbh[:, cc * 128:cc * 128 + n], hb[0:n, c * 128:(c + 1) * 128], identb[0:n, 0:n]),
                    reads=[f"h1b{b}", "identb"], writes=["pbh"])
            src = pbh[:, :].rearrange("p (c t) -> p c t", c=8)
            if cg == 0:
                P.op("act", lambda e, src=src, cg=cg, r0=r0, n=n: e.copy(
                    out=h1T[:, cg * 8:(cg + 1) * 8, r0:r0 + n], in_=src[:, :, 0:n]), reads=["pbh"], writes=["h1T"])
            else:
                P.op("dve", lambda e, src=src, cg=cg, r0=r0, n=n: e.tensor_copy(
                    out=h1T[:, cg * 8:(cg + 1) * 8, r0:r0 + n], in_=src[:, :, 0:n]), reads=["pbh"], writes=["h1T"])
    P.barrier()

    RA.reset(); RB.reset()
    hidT = RA.take(NFF * 1024, BF16).rearrange("p (f t) -> p f t", f=NFF)
    NWB = 3
    wgb = [RB.take(16 * 128, BF16).rearrange("p (c n) -> p c n", c=16) for i in range(NWB)]
    wub = [RB.take(16 * 128, BF16).rearrange("p (c n) -> p c n", c=16) for i in range(NWB)]
    gsb = [RB.take(NTOK) for i in range(2)]
    cacc = [RB.take(1024) for i in range(2)]
    sil = [RB.take(1024) for i in range(2)]

    def load_w(f):
        s = f % NWB
        P.op("pool", lambda e: e.dma_start(out=wgb[s], in_=wg_d[f]), writes=[f"wgb{s}"], dma=f"d_wg{s}")
        P.op("pool", lambda e: e.dma_start(out=wub[s], in_=wu_d[f]), writes=[f"wub{s}"], dma=f"d_wu{s}")

    load_w(0); load_w(1)
    for f in range(NFF):
        s = f % NWB; b = f % 2
        if f + 2 < NFF:
            load_w(f + 2)
        for half in range(2):
            for c in range(16):
                P.op("pe", lambda e, c=c, half=half, s=s: e.matmul(
                    pb[half][:, :], wgb[s][:, c, :], h1T[:, c, 2 + half * 512:2 + (half + 1) * 512],
                    start=(c == 0), stop=(c == 15)), reads=[f"wgb{s}", "h1T"], writes=[f"pb{half}"])
        for c in range(16):
            P.op("pe", lambda e, c=c, s=s: e.matmul(pb[4][:, 0:2], wgb[s][:, c, :], h1T[:, c, 0:2],
                                                    start=(c == 0), stop=(c == 15)),
                 reads=[f"wgb{s}", "h1T"], writes=["pb4"])
        for half in range(2):
            for c in range(16):
                P.op("pe", lambda e, c=c, half=half, s=s: e.matmul(
                    pb[2 + half][:, :], wub[s][:, c, :], h1T[:, c, 2 + half * 512:2 + (half + 1) * 512],
                    start=(c == 0), stop=(c == 15)), reads=[f"wub{s}", "h1T"], writes=[f"pb{2 + half}"])
        g = gsb[b]; ca = cacc[b]; sl = sil[b]
        P.op("act", lambda e, g=g: e.copy(out=g[:, 0:2], in_=pb[4][:, 0:2]), reads=["pb4"], writes=[f"gsb{b}"])
        for half in range(2):
            P.op("act", lambda e, g=g, half=half: e.copy(out=g[:, 2 + half * 512:2 + (half + 1) * 512], in_=pb[half][:, :]),
                 reads=[f"pb{half}"], writes=[f"gsb{b}"])
        P.op("dve", lambda e, g=g, ca=ca, f=f: e.tensor_scalar(
            out=ca[:, :], in0=g[:, 2:1026], scalar1=fcw[:, f, 2:3], scalar2=fcw[:, f, 3:4], op0=ALU.mult, op1=ALU.add),
            reads=[f"gsb{b}", "fcw"], writes=[f"cacc{b}"])
        P.op("dve", lambda e, g=g, ca=ca, f=f: e.scalar_tensor_tensor(
            out=ca[:, :], in0=g[:, 1:1025], scalar=fcw[:, f, 1:2], in1=ca[:, :], op0=ALU.mult, op1=ALU.add),
            reads=[f"gsb{b}", "fcw", f"cacc{b}"], writes=[f"cacc{b}"])
        P.op("dve", lambda e, g=g, ca=ca, f=f: e.scalar_tensor_tensor(
            out=ca[:, :], in0=g[:, 0:1024], scalar=fcw[:, f, 0:1], in1=ca[:, :], op0=ALU.mult, op1=ALU.add),
            reads=[f"gsb{b}", "fcw", f"cacc{b}"], writes=[f"cacc{b}"])
        P.op("act", lambda e, ca=ca, sl=sl: e.activation(out=sl[:, :], in_=ca[:, :], func=AF.Silu),
             reads=[f"cacc{b}"], writes=[f"sil{b}"])
        for half in range(2):
            P.op("dve", lambda e, sl=sl, half=half, f=f: e.tensor_tensor(
                out=hidT[:, f, half * 512:(half + 1) * 512], in0=sl[:, half * 512:(half + 1) * 512],
                in1=pb[2 + half][:, :], op=ALU.mult), reads=[f"sil{b}", f"pb{2 + half}"], writes=[f"hid{f}"])
    P.barrier()
    HID = [f"hid{f}" for f in range(NFF)]

    RB.reset(); RC.reset()
    y2 = RB.take(8 * D).rearrange("p (t d) -> p t d", t=8)
    Y2 = [f"y2_{tt}" for tt in range(8)]
    for tt in range(8):
        P.op("sp", lambda e, tt=tt: e.dma_start(out=y2[:, tt, :], in_=h1s_d[tt * 128:(tt + 1) * 128, :]),
             reads=["h1s"], writes=[Y2[tt]], dma="d_y2")
    P.join(Y2)
    wdb = [RC.take(NFF * 128, BF16).rearrange("p (f n) -> p f n", f=NFF) for i in range(2)]
    fsb = [RC.take(1024) for i in range(2)]

    def load_wd(dt):
        s = dt % 2
        P.op("pool", lambda e: e.dma_start(out=wdb[s], in_=wd_d[dt]), writes=[f"wdb{s}"], dma=f"d_wd{s}")

    load_wd(0)
    for dt in range(16):
        s = dt % 2
        if dt + 1 < 16:
            load_wd(dt + 1)
        for half in range(2):
            for f in range(NFF):
                P.op("pe", lambda e, half=half, f=f, s=s: e.matmul(
                    pb[half][:, :], wdb[s][:, f, :], hidT[:, f, half * 512:(half + 1) * 512],
                    start=(f == 0), stop=(f == NFF - 1)), reads=[f"wdb{s}", HID[f]], writes=[f"pb{half}"])
        fs = fsb[s]
        for half in range(2):
            P.op("act", lambda e, fs=fs, half=half: e.copy(out=fs[:, half * 512:(half + 1) * 512], in_=pb[half][:, :]),
                 reads=[f"pb{half}"], writes=[f"fsb{s}"])
        for tg in range(2):
            bank = pb[2 + tg]
            for k in range(4):
                tt = tg * 4 + k
                P.op("pe", lambda e, bank=bank, k=k, tt=tt, fs=fs: e.transpose(
                    bank[:, k * 128:(k + 1) * 128], fs[:, tt * 128:(tt + 1) * 128], ident),
                    reads=[f"fsb{s}", "cst"], writes=[f"pb{2 + tg}"])
            src = bank[:, :].rearrange("p (k n) -> p k n", k=4)
            P.op("dve", lambda e, src=src, tg=tg, dt=dt: e.scalar_tensor_tensor(
                out=y2[:, tg * 4:(tg + 1) * 4, dt * 128:(dt + 1) * 128], in0=y2[:, tg * 4:(tg + 1) * 4, dt * 128:(dt + 1) * 128],
                scalar=ALPHA, in1=src, op0=ALU.mult, op1=ALU.add),
                reads=[f"pb{2 + tg}"] + Y2[tg * 4:(tg + 1) * 4], writes=Y2[tg * 4:(tg + 1) * 4])
    P.barrier()
    RA.reset()
    lng = RA.take(D); lnb = RA.take(D)
    P.op("sp", lambda e: e.dma_start(out=lng, in_=lnp_d[2]), writes=["lng"], dma="d_lng")
    P.op("sp", lambda e: e.dma_start(out=lnb, in_=lnp_d[3]), writes=["lnb"], dma="d_lnb")
    for tt in range(8):
        layer_norm_tile(y2[:, tt, :], 128, lng, lnb, Y2[tt])
        P.op("sp", lambda e, tt=tt: e.dma_start(out=out_d[tt * 128:(tt + 1) * 128, :], in_=y2[:, tt, :]),
             reads=[Y2[tt]], dma="d_out", out=True)


def emit_attention(nc, P, st, sbt, pb, pbh, xT, XT, cst, identb, pswapb, onesb, sm, rope_d, waq_d, wak_d, wav_d,
                   oscr_d, heads, lvl=9):
    AXX = mybir.AxisListType.X
    onesf = cst[:, 6, :]
    cosT = sbt(st, "cosT", [128, T]); sinT = sbt(st, "sinT", [128, T])
    P.op("sp", lambda e: e.dma_start(out=cosT[:], in_=rope_d[0]), writes=["cosT"], dma="d_cos")
    P.op("sp", lambda e: e.dma_start(out=sinT[:], in_=rope_d[1]), writes=["sinT"], dma="d_sin")
    prod = sbt(st, "lprod", [128, 2, 64]); ls = sbt(st, "lsum", [128, 2]); neglam = sbt(st, "neglam", [128, 1])
    nws = sbt(st, "nws", [128, 1]); epsr = sbt(st, "epsr", [128, 1])
    P.op("dve", lambda e: e.tensor_tensor(out=prod[:, 0, :], in0=sm[:, 0:64], in1=sm[:, 64:128], op=ALU.mult), reads=["sm"], writes=["lprod"])
    P.op("dve", lambda e: e.tensor_tensor(out=prod[:, 1, :], in0=sm[:, 128:192], in1=sm[:, 192:256], op=ALU.mult), reads=["sm", "lprod"], writes=["lprod"])
    P.op("dve", lambda e: e.reduce_sum(out=ls[:, :], in_=prod[:, :, :], axis=AXX), reads=["lprod"], writes=["lsum"])
    P.op("act", lambda e: e.activation(out=ls[:, :], in_=ls[:, :], func=AF.Exp), reads=["lsum"], writes=["lsum"])
    P.op("dve", lambda e: e.tensor_tensor(out=neglam[:, :], in0=ls[:, 1:2], in1=ls[:, 0:1], op=ALU.subtract), reads=["lsum"], writes=["neglam"])
    P.op("dve", lambda e: e.tensor_scalar(out=neglam[:, :], in0=neglam[:, :], scalar1=-LAM_INIT, scalar2=None, op0=ALU.add), reads=["neglam"], writes=["neglam"])
    P.op("dve", lambda e: e.tensor_scalar(out=nws[:, :], in0=sm[:, 272:273], scalar1=1.0 - LAM_INIT, scalar2=None, op0=ALU.mult), reads=["sm"], writes=["nws"])
    P.op("dve", lambda e: e.memset(epsr[:], RMS_EPS), writes=["epsr"])

    wv = sbt(st, "wv", [128, 16, 512], BF16)
    vtm = sbt(st, "vtm", [128, 17, 512], BF16)
    wqb = [sbt(st, f"wqb{i}", [128, 16, 128], BF16) for i in range(2)]
    wkb = [sbt(st, f"wkb{i}", [128, 16, 128], BF16) for i in range(2)]
    qT = sbt(st, "qT", [128, T], BF16); kT = sbt(st, "kT", [128, T], BF16)
    qb = [sbt(st, f"qb{i}", [128, 512], BF16) for i in range(2)]
    t1 = [sbt(st, f"t1_{i}", [128, 512]) for i in range(2)]
    t2 = [sbt(st, f"t2_{i}", [128, 512]) for i in range(2)]
    ptb = [sbt(st, f"ptb{i}", [128, 2, 256], BF16) for i in range(3)]
    rZ = sbt(st, "rZ", [128, 2, 256]); Aa = sbt(st, "Aa", [128, 2, 256])
    od = sbt(st, "od", [128, 256]); sq = sbt(st, "sq", [128, 256]); rs = sbt(st, "rs", [128, 256])
    oTb = [sbt(st, f"oTb{i}", [128, T], BF16) for i in range(2)]
    P.op("pool", lambda e: e.memset(vtm[:], 0.0), writes=["vtm"])

    blocks = [(0, 16)] + [(16 + 512 * j, 512) for j in range(4)]
    ttiles = [(0, 16)] + [(16 + 128 * j, 128) for j in range(16)]
    cnt = {"acc": 0, "qb": 0, "st": 0, "pt": 0}
    cur_group = [None]

    def load_qk(hi):
        h = heads[hi]; par = hi % 2
        P.op("pool", lambda e: e.dma_start(out=wqb[par][:], in_=waq_d[h]), writes=[f"wqb{par}"], dma=f"d_wq{par}")
        P.op("pool", lambda e: e.dma_start(out=wkb[par][:], in_=wak_d[h]), writes=[f"wkb{par}"], dma=f"d_wk{par}")

    qTb = [qT, sbt(st, "qT2", [128, T], BF16)]; kTb = [kT, sbt(st, "kT2", [128, T], BF16)]

    def run(fg, bgl=()):
        fg = list(fg)
        while fg:
            nxt = []
            for g_ in fg:
                try:
                    next(g_); nxt.append(g_)
                except StopIteration:
                    pass
            fg = nxt
            for g_ in list(bgl):
                try:
                    next(g_)
                except StopIteration:
                    bgl.remove(g_)

    def vproj(hi):
        g = heads[hi] // 4
        if cur_group[0] == g:
            return
        cur_group[0] = g
        P.op("pool", lambda e: e.dma_start(out=wv[:], in_=wav_d[g]), writes=["wv"], dma="d_wv")
        for tt, (c0, n) in enumerate(ttiles):
            bi = 4 + cnt["acc"] % 2; cnt["acc"] += 1
            acc = pb[bi]
            for c in range(16):
                P.op("pe", lambda e, c=c: e.matmul(acc[0:n, :], xT[:, c, c0:c0 + n], wv[:, c, :], start=(c == 0), stop=(c == 15)),
                     reads=[XT[c], "wv"], writes=[f"pb{bi}"])
            P.op("act", lambda e: e.copy(out=vtm[0:n, tt, :], in_=acc[0:n, :]), reads=[f"pb{bi}"], writes=["vtm"])

    def proj_chain(hi):
        par = hi % 2
        for which, wsb, wtag, dst, dtag in (("q", wqb[par], f"wqb{par}", qTb[par], f"qT{par}"), ("k", wkb[par], f"wkb{par}", kTb[par], f"kT{par}")):
            for (c0, n) in blocks:
                bi = 4 + cnt["acc"] % 2; cnt["acc"] += 1
                acc = pb[bi]
                qi = cnt["qb"] % 2; cnt["qb"] += 1
                for c in range(16):
                    P.op("pe", lambda e, c=c: e.matmul(acc[:, 0:n], wsb[:, c, :], xT[:, c, c0:c0 + n], start=(c == 0), stop=(c == 15)),
                         reads=[XT[c], wtag], writes=[f"pb{bi}"])
                P.op("act", lambda e: e.copy(out=qb[qi][:, 0:n], in_=acc[:, 0:n]), reads=[f"pb{bi}"], writes=[f"qb{qi}"])
                P.op("dve", lambda e: e.tensor_tensor(out=t2[qi][:, 0:n], in0=acc[:, 0:n], in1=cosT[:, c0:c0 + n], op=ALU.mult),
                     reads=[f"pb{bi}", "cosT", f"qb{qi}"], writes=[f"t2_{qi}"])
                yield
                P.op("pe", lambda e: e.matmul(pb[6][:, 0:n], pswapb[:, :], qb[qi][:, 0:n], start=True, stop=True), reads=["pswapb", f"qb{qi}"], writes=["pb6"])
                P.op("dve", lambda e: e.tensor_tensor(out=t1[qi][:, 0:n], in0=pb[6][:, 0:n], in1=sinT[:, c0:c0 + n], op=ALU.mult),
                     reads=["pb6", "sinT"], writes=[f"t1_{qi}"])
                yield
                P.op("pool", lambda e: e.tensor_tensor(out=dst[:, c0:c0 + n], in0=t1[qi][:, 0:n], in1=t2[qi][:, 0:n], op=ALU.add),
                     reads=[f"t1_{qi}", f"t2_{qi}"], writes=[dtag])
                yield

    def core_chain(hi):
        h = heads[hi]; par = hi % 2
        hc = (h % 4) * 128
        qTc = qTb[par]; kTc = kTb[par]; qtag = f"qT{par}"; ktag = f"kT{par}"
        ob = oTb[par]; otag = f"oTb{par}"
        groups = [(0, 16, [(0, 16, 0, 0, None)])]
        for gq in range(8):
            kts = [(0, 16, 0, 0, None)]
            for j in range(2 * gq + 2):
                qlo = 0 if j <= 2 * gq else 128
                dl = 0 if j == 2 * gq else (128 if j == 2 * gq + 1 else None)
                kts.append((16 + 128 * j, 128, j + 1, qlo, dl))
            groups.append((16 + 256 * gq, 256, kts))
        pending = [None]
        for (qc0, nq, kts) in groups:
            O = pb[2][:, 0:2 * nq].rearrange("p (c n) -> p c n", c=2)
            Z = pb[3][:, 0:2 * nq].rearrange("p (c n) -> p c n", c=2)
            slots = {}

            def emit_scores(ki, qc0=qc0, nq=nq, kts=kts, slots=slots):
                kc0, nk, vt, qlo, dl = kts[ki]
                si = cnt["st"] % 2; cnt["st"] += 1
                pi = cnt["pt"] % 3; cnt["pt"] += 1
                slots[ki] = pi
                sbk = (0, 1) if si == 0 else (4, 5)
                PT = ptb[pi]
                for cc in range(2):
                    P.op("pe", lambda e, cc=cc: e.matmul(
                        pb[sbk[cc]][0:nk, qlo:nq], kTc[cc * 64:(cc + 1) * 64, kc0:kc0 + nk], qTc[cc * 64:(cc + 1) * 64, qc0 + qlo:qc0 + nq],
                        start=True, stop=True), reads=[ktag, qtag], writes=[f"pb{sbk[cc]}"])
                for cc in range(2):
                    P.op("act", lambda e, cc=cc: e.activation(
                        out=PT[0:nk, cc, qlo:nq], in_=pb[sbk[cc]][0:nk, qlo:nq], func=AF.Exp, scale=0.125), reads=[f"pb{sbk[cc]}"], writes=[f"ptb{pi}"])
                if dl is not None:
                    P.op("dve", lambda e: e.memset(PT[64:128, :, dl:dl + 64], 0.0), reads=[f"ptb{pi}"], writes=[f"ptb{pi}"])

            def emit_av(ki, qc0=qc0, nq=nq, kts=kts, slots=slots, O=O, Z=Z):
                kc0, nk, vt, qlo, dl = kts[ki]
                pi = slots[ki]; PT = ptb[pi]
                first = ki == 0; last = ki == len(kts) - 1
                lastflat = ki == len(kts) - 2
                if qlo == 0 and nq == 256:
                    PTf = PT[0:nk, :, :].rearrange("p c n -> p (c n)")
                    P.op("pe", lambda e: e.matmul(pb[2][:, 0:512], vtm[0:nk, vt, hc:hc + 128], PTf, start=first, stop=lastflat),
                         reads=["vtm", f"ptb{pi}"], writes=["pb2"])
                    P.op("pe", lambda e: e.matmul(pb[3][:, 0:512], onesb[0:nk, :], PTf, start=first, stop=lastflat),
                         reads=["onesb", f"ptb{pi}"], writes=["pb3"])
                else:
                    for cc in range(2):
                        P.op("pe", lambda e, cc=cc: e.matmul(
                            O[:, cc, qlo:nq], vtm[0:nk, vt, hc:hc + 128], PT[0:nk, cc, qlo:nq], start=(first and cc == 0), stop=(last and cc == 1),
                            skip_group_check=True), reads=["vtm", f"ptb{pi}"], writes=["pb2"])
                        P.op("pe", lambda e, cc=cc: e.matmul(
                            Z[:, cc, qlo:nq], onesb[0:nk, :], PT[0:nk, cc, qlo:nq], start=(first and cc == 0), stop=(last and cc == 1),
                            skip_group_check=True), reads=["onesb", f"ptb{pi}"], writes=["pb3"])

            nk_ = len(kts)
            emit_scores(0)
            if nk_ > 1:
                emit_scores(1)
            if pending[0] is not None:
                pending[0](); pending[0] = None
            for ki in range(nk_):
                if ki + 2 < nk_:
                    emit_scores(ki + 2)
                emit_av(ki)
                yield
            P.op("dve", lambda e, Z=Z, nq=nq: e.reciprocal(out=rZ[:, :, 0:nq], in_=Z), reads=["pb3"], writes=["rZ"])
            P.op("dve", lambda e, O=O, nq=nq: e.tensor_tensor(out=Aa[:, :, 0:nq], in0=O, in1=rZ[:, :, 0:nq], op=ALU.mult),
                 reads=["pb2", "rZ"], writes=["Aa"])
            P.op("dve", lambda e, nq=nq: e.scalar_tensor_tensor(out=od[:, 0:nq], in0=Aa[:, 1, 0:nq], scalar=neglam[:, 0:1], in1=Aa[:, 0, 0:nq],
                                                                 op0=ALU.mult, op1=ALU.add), reads=["Aa", "neglam"], writes=["od"])
            P.op("dve", lambda e, nq=nq: e.tensor_tensor(out=sq[:, 0:nq], in0=od[:, 0:nq], in1=od[:, 0:nq], op=ALU.mult), reads=["od"], writes=["sq"])

            def post2(nq=nq, qc0=qc0, ob=ob, otag=otag):
                P.op("pe", lambda e: e.matmul(pb[6][:, 0:nq], onesf, sq[:, 0:nq], start=True, stop=True), reads=["cst", "sq"], writes=["pb6"])
                P.op("act", lambda e: e.activation(out=rs[:, 0:nq], in_=pb[6][:, 0:nq], func=AF.Ln, bias=epsr[:, 0:1], scale=1.0 / 128),
                     reads=["pb6", "epsr"], writes=["rs"])
                P.op("act", lambda e: e.activation(out=rs[:, 0:nq], in_=rs[:, 0:nq], func=AF.Exp, scale=-0.5), reads=["rs"], writes=["rs"])
                P.op("dve", lambda e: e.scalar_tensor_tensor(out=ob[:, qc0:qc0 + nq], in0=od[:, 0:nq], scalar=nws[:, 0:1], in1=rs[:, 0:nq],
                                                             op0=ALU.mult, op1=ALU.mult), reads=["od", "nws", "rs"], writes=[otag])
            pending[0] = post2
            yield
        if pending[0] is not None:
            pending[0](); pending[0] = None
        P.op("sp", lambda e: e.dma_start(out=oscr_d[h], in_=ob[:, :]), reads=[otag], writes=[f"oscr{h}"], dma=f"d_oscr{par}", out=True)
        yield

    load_qk(0)
    vproj(0)
    run([proj_chain(0)])
    for hi in range(len(heads)):
        if hi + 1 < len(heads):
            load_qk(hi + 1)
        bgl = [proj_chain(hi + 1)] if hi + 1 < len(heads) else []
        run([core_chain(hi)], bgl)
        run(bgl)
        if hi + 1 < len(heads):
            vproj(hi + 1)


def emit_gdn(nc, P, st, sbt, pb, pbh, xT, XT, cst, sm, wdq_d, wdk_d, wdv_d, wdz_d, wba_d, cw_d, oscr_d, heads, GS=2):
    ident = cst[:, 0, :]; U = cst[:, 2, :]; Bm = cst[:, 3, :]; NM1 = cst[:, 4, :]; NM2 = cst[:, 5, :]; onesf = cst[:, 6, :]
    ttiles = [(0, 16)] + [(16 + 128 * j, 128) for j in range(16)]
    rot = {"b": 0, "u": 0}

    def nb():
        i = rot["b"] % 3; rot["b"] += 1
        return i, pb[i], f"pb{i}"

    def uid(p):
        rot["u"] += 1
        return f"{p}{rot['u']}"

    def evac(eng, out_ap, in_ap, reads, writes):
        if eng == "act":
            P.op("act", lambda e: e.copy(out=out_ap, in_=in_ap), reads=reads, writes=writes)
        else:
            P.op("dve", lambda e: e.tensor_copy(out=out_ap, in_=in_ap), reads=reads, writes=writes)

    wba = sbt(st, "wba", [128, 16, 16], BF16)
    P.op("pool", lambda e: e.dma_start(out=wba[:], in_=wba_d), writes=["wba"], dma="d_wba")
    cw = sbt(st, "cw", [128, 3, 8, 4])
    P.op("sp", lambda e: e.dma_start(out=cw[:], in_=cw_d), writes=["cw"], dma="d_cw")
    ba = sbt(st, "ba", [128, 17, 16]); beta = sbt(st, "beta", [128, 17, 8]); gg = sbt(st, "gg", [128, 17, 8])
    gcl = sbt(st, "gcl", [128, 17, 16]); bg = sbt(st, "bg", [128, 17, 8]); kdsc = sbt(st, "kdsc", [128, 17, 8])
    nea = sbt(st, "nea", [128, 8]); tA = sbt(st, "tA", [128, 17]); tB = sbt(st, "tB", [128, 17]); tC = sbt(st, "tC", [128, 17, 8])
    epsr = sbt(st, "epsg", [128, 1])
    P.op("dve", lambda e: e.memset(epsr[:], RMS_EPS), writes=["epsg"])
    P.op("dve", lambda e: e.memset(ba[:], 0.0), writes=["ba"])
    P.op("dve", lambda e: e.memset(gcl[:], 0.0), writes=["gcl"])
    for tt, (c0, n) in enumerate(ttiles):
        bi, bank, btag = nb()
        for c in range(16):
            P.op("pe", lambda e, bank=bank, c=c, c0=c0, n=n: e.matmul(bank[0:n, 0:16], xT[:, c, c0:c0 + n], wba[:, c, :],
                                                                     start=(c == 0), stop=(c == 15)), reads=[XT[c], "wba"], writes=[btag])
        evac("act", ba[0:n, tt, :], bank[0:n, 0:16], [btag], ["ba"])
    P.op("act", lambda e: e.activation(out=beta[:, :, :], in_=ba[:, :, 0:8], func=AF.Sigmoid), reads=["ba"], writes=["beta"])
    P.op("act", lambda e: e.activation(out=nea[:, :], in_=sm[:, 256:264], func=AF.Exp), reads=["sm"], writes=["nea"])
    P.op("dve", lambda e: e.tensor_scalar(out=nea[:, :], in0=nea[:, :], scalar1=-1.0, scalar2=None, op0=ALU.mult), reads=["nea"], writes=["nea"])
    for h in range(8):
        P.op("dve", lambda e, h=h: e.tensor_scalar(out=tA[:, :], in0=ba[:, :, 8 + h], scalar1=sm[:, 264 + h:265 + h], scalar2=None, op0=ALU.add),
             reads=["ba", "sm"], writes=["tA"])
        P.op("act", lambda e: e.activation(out=tB[:, :], in_=tA[:, :], func=AF.Abs), reads=["tA"], writes=["tB"])
        P.op("act", lambda e: e.activation(out=tB[:, :], in_=tB[:, :], func=AF.Exp, scale=-1.0), reads=["tB"], writes=["tB"])
        P.op("act", lambda e: e.activation(out=tB[:, :], in_=tB[:, :], func=AF.Ln, bias=1.0), reads=["tB"], writes=["tB"])
        P.op("dve", lambda e: e.scalar_tensor_tensor(out=tA[:, :], in0=tA[:, :], scalar=0.0, in1=tB[:, :], op0=ALU.max, op1=ALU.add),
             reads=["tA", "tB"], writes=["tA"])
        P.op("dve", lambda e, h=h: e.tensor_scalar(out=gg[:, :, h], in0=tA[:, :], scalar1=nea[:, h:h + 1], scalar2=None, op0=ALU.mult),
             reads=["tA", "nea"], writes=["gg"])
    P.op("dve", lambda e: e.tensor_scalar(out=gg[:, 0, :], in0=gg[:, 0, :], scalar1=U[:, 15:16], scalar2=None, op0=ALU.mult), reads=["gg", "cst"], writes=["gg"])
    for tt, (c0, n) in enumerate(ttiles):
        bi, bank, btag = nb()
        P.op("pe", lambda e, bank=bank, n=n, tt=tt: e.matmul(bank[0:n, 0:8], U[0:n, 0:n], gg[0:n, tt, :], start=True, stop=True),
             reads=["cst", "gg"], writes=[btag])
        P.op("pe", lambda e, bank=bank, n=n, tt=tt: e.matmul(bank[0:n, 8:16], Bm[0:n, 0:n], gg[0:n, tt, :], start=True, stop=True),
             reads=["cst", "gg"], writes=[btag])
        evac("act", gcl[0:n, tt, :], bank[0:n, 0:16], [btag], ["gcl"])
    P.op("act", lambda e: e.activation(out=tC[:, :, :], in_=gcl[:, :, 0:8], func=AF.Exp), reads=["gcl"], writes=["tC"])
    P.op("dve", lambda e: e.tensor_tensor(out=bg[:, :, :], in0=beta[:, :, :], in1=tC[:, :, :], op=ALU.mult), reads=["beta", "tC"], writes=["bg"])
    P.op("dve", lambda e: e.tensor_tensor(out=tC[:, :, :], in0=gcl[:, :, 8:16], in1=gcl[:, :, 0:8], op=ALU.subtract), reads=["gcl", "tC"], writes=["tC"])
    P.op("act", lambda e: e.activation(out=kdsc[:, :, :], in_=tC[:, :, :], func=AF.Exp), reads=["tC"], writes=["kdsc"])

    def mk(name, shape, dt=F32):
        return [sbt(st, f"{name}_{i}", shape, dt) for i in range(GS)]

    wts = {k: [sbt(st, f"g{k}_{i}", [128, 16, 128], BF16) for i in range(GS)] for k in ("q", "k", "v", "z")}
    pre = {k: mk("pre" + k, [128, 515]) for k in ("q", "k", "v")}
    cvo = {k: [mk(f"cvo{k}{p}", [128, 512]) for p in range(2)] for k in ("q", "k", "v")}
    cacc = mk("gcacc", [128, 512]); rn = mk("grn", [128, 512])
    S = mk("S", [128, 128])
    gb = mk("gb", [128, 128]); xx = mk("xx", [128, 128]); a1 = mk("a1", [128, 128]); a2 = mk("a2", [128, 128])
    Dst = mk("Dst", [128, 128]); DTt = mk("DTt", [128, 128]); egrow = [mk(f"egrow{p}", [128, 128]) for p in range(2)]
    Mm = mk("Mm", [128, 128]); Nm = mk("Nm", [128, 128])
    YQ = [mk(f"YQ{j}", [128, 256]) for j in range(2)]
    QT = [mk(f"QT{j}", [128, 128]) for j in range(2)]
    kbg = mk("kbg", [128, 128]); kdec = [mk(f"kdec{p}", [128, 128], BF16) for p in range(2)]; vb = mk("vb", [128, 128]); uu = [mk(f"uu{p}", [128, 128]) for p in range(2)]
    wT = [mk(f"wT{p}", [128, 128], BF16) for p in range(2)]; qkT = [mk(f"qkT{p}", [128, 128], BF16) for p in range(2)]
    qdT = [mk(f"qdT{p}", [128, 128], BF16) for p in range(2)]; szT = [mk(f"szT{p}", [128, 512]) for p in range(2)]
    vnew = mk("vnew", [128, 128], BF16); odn = mk("odn", [128, 128]); t3 = mk("t3", [128, 128]); Sb = mk("Sb", [128, 128], BF16)
    st6 = mk("st6", [128, 6]); mvv = mk("mvv", [128, 2]); msq = mk("msq", [128, 1])
    oTs = [mk(f"oTs{j}", [128, 128], BF16) for j in range(2)]
    wsrc = {"q": wdq_d, "k": wdk_d, "v": wdv_d, "z": wdz_d}

    groups = [heads[i:i + GS] for i in range(0, len(heads), GS)]

    def load_weights(gi):
        for i, h in enumerate(groups[gi]):
            for k in ("q", "k", "v", "z"):
                P.op("pool", lambda e, i=i, h=h, k=k: e.dma_start(out=wts[k][i][:], in_=wsrc[k][h]),
                     writes=[f"gw{k}{i}"], dma=f"d_gw{k}{i}")

    import os as _os
    GL = int(_os.environ.get("GDN_LVL", "9"))
    if GL < 1:
        return
    blocks = [(0, 16, [0])] + [(16 + 512 * j, 512, [1 + 4 * j + t for t in range(4)]) for j in range(4)]
    import itertools as _it

    def blk_chain(i, h, bidx):
        c0, n, tiles_ = blocks[bidx]; bp = bidx % 2
        for pi, k in enumerate(("q", "k", "v")):
            w = wts[k][i]; wtag = f"gw{k}{i}"
            pr = pre[k][i]; ptag = f"pre{k}{i}"; co = cvo[k][bp][i]; ctag = f"cvo{k}{bp}{i}"
            nprev = blocks[bidx - 1][1] if bidx > 0 else 0
            bi, bank, btag = nb()
            for c in range(16):
                P.op("pe", lambda e, c=c: e.matmul(bank[:, 0:n], w[:, c, :], xT[:, c, c0:c0 + n], start=(c == 0), stop=(c == 15)), reads=[XT[c], wtag], writes=[btag])
            if bidx == 0:
                P.op("pool", lambda e: e.memset(pr[:, 0:3], 0.0), writes=[ptag])
            else:
                P.op("pool", lambda e: e.tensor_copy(out=pr[:, 0:3], in_=pr[:, nprev:nprev + 3]), reads=[ptag], writes=[ptag])
            evac("act", pr[:, 3:3 + n], bank[:, 0:n], [btag], [ptag])
            yield
            ca = cacc[i]; catag = f"gcacc{i}"
            P.op("dve", lambda e: e.tensor_scalar(out=ca[:, 0:n], in0=pr[:, 3:3 + n], scalar1=cw[:, pi, h, 3:4], scalar2=None, op0=ALU.mult), reads=[ptag, "cw"], writes=[catag])
            yield
            for j in range(3):
                P.op("dve", lambda e, j=j: e.scalar_tensor_tensor(out=ca[:, 0:n], in0=pr[:, j:j + n], scalar=cw[:, pi, h, j:j + 1], in1=ca[:, 0:n], op0=ALU.mult, op1=ALU.add),
                     reads=[ptag, "cw", catag], writes=[catag])
                yield
            P.op("act", lambda e: e.activation(out=co[:, 0:n], in_=ca[:, 0:n], func=AF.Silu), reads=[catag], writes=[ctag])
            yield
            if k in ("q", "k"):
                r_ = rn[i]
                P.op("dve", lambda e: e.tensor_tensor(out=ca[:, 0:n], in0=co[:, 0:n], in1=co[:, 0:n], op=ALU.mult), reads=[ctag], writes=[catag])
                yield
                b2, bank2, btag2 = nb()
                P.op("pe", lambda e: e.matmul(bank2[:, 0:n], onesf, ca[:, 0:n], start=True, stop=True), reads=["cst", catag], writes=[btag2])
                P.op("act", lambda e: e.activation(out=r_[:, 0:n], in_=bank2[:, 0:n], func=AF.Ln, bias=epsr[:, 0:1]), reads=[btag2, "epsg"], writes=[f"grn{i}"])
                P.op("act", lambda e: e.activation(out=r_[:, 0:n], in_=r_[:, 0:n], func=AF.Exp, scale=-0.5), reads=[f"grn{i}"], writes=[f"grn{i}"])
                yield
                if k == "k":
                    P.op("dve", lambda e: e.tensor_tensor(out=co[:, 0:n], in0=co[:, 0:n], in1=r_[:, 0:n], op=ALU.mult), reads=[ctag, f"grn{i}"], writes=[ctag])
                else:
                    P.op("dve", lambda e: e.scalar_tensor_tensor(out=co[:, 0:n], in0=co[:, 0:n], scalar=128.0 ** -0.5, in1=r_[:, 0:n], op0=ALU.mult, op1=ALU.mult),
                         reads=[ctag, f"grn{i}"], writes=[ctag])
                yield
        wz = wts["z"][i]
        bz, bankz, btz = nb()
        for c in range(16):
            P.op("pe", lambda e, c=c: e.matmul(bankz[:, 0:n], wz[:, c, :], xT[:, c, c0:c0 + n], start=(c == 0), stop=(c == 15)), reads=[XT[c], f"gwz{i}"], writes=[btz])
        P.op("act", lambda e: e.activation(out=szT[bp][i][:, 0:n], in_=bankz[:, 0:n], func=AF.Silu), reads=[btz], writes=[f"szT{bp}{i}"])
        yield

    def run(fg, bg=()):
        fg = list(fg)
        while fg:
            nxt = []
            for g in fg:
                try:
                    next(g); nxt.append(g)
                except StopIteration:
                    pass
            fg = nxt
            for g in list(bg):
                try:
                    next(g)
                except StopIteration:
                    bg.remove(g)

    for gi, hg in enumerate(groups):
        load_weights(gi)
        for i, h in enumerate(hg):
            P.op("dve", lambda e, i=i: e.memset(S[i][:], 0.0), writes=[f"S{i}"])
            P.op("dve", lambda e, i=i: e.memset(Sb[i][:], 0.0), writes=[f"Sb{i}"])
        run([blk_chain(i, h, 0) for i, h in enumerate(hg)])
        for bidx, (c0, n, tiles) in enumerate(blocks):
            bp = bidx % 2
            bgen = [blk_chain(i, h, bidx + 1) for i, h in enumerate(hg)] if bidx + 1 < len(blocks) else []

            import itertools as _it

            def tile_params(tl):
                tt = tiles[tl]; lo = 128 * tl; nt = 16 if tt == 0 else 128
                return tt, lo, nt, c0 + lo, ([(0, 16)] if tt == 0 else [(0, 64), (64, 64)]), tl % 2

            def pre_chain(i, h, tl):
                tt, lo, nt, tc0, chunks, pp = tile_params(tl)
                kc = cvo["k"][bp][i][:, lo:lo + nt]; qc = cvo["q"][bp][i][:, lo:lo + nt]; vc = cvo["v"][bp][i][:, lo:lo + nt]
                KT = f"cvok{bp}{i}"; QTg = f"cvoq{bp}{i}"; VT = f"cvov{bp}{i}"
                eg = egrow[pp][i]; egt = f"egrow{pp}{i}"
                P.op("dve", lambda e: e.tensor_scalar(out=gb[i][0:nt, :], in0=onesf[0:nt, :], scalar1=gg[0:nt, tt, h:h + 1], scalar2=None, op0=ALU.mult),
                     reads=["cst", "gg"], writes=[f"gb{i}"])
                b1, bk1, bt1 = nb()
                P.op("pe", lambda e: e.matmul(bk1[:, 0:nt], gb[i][0:nt, :], U[0:nt, 0:nt], start=True, stop=True), reads=[f"gb{i}", "cst"], writes=[bt1])
                P.op("dve", lambda e: e.tensor_scalar(out=xx[i][0:nt, 0:nt], in0=bk1[0:nt, 0:nt], scalar1=gcl[0:nt, tt, h:h + 1], scalar2=None, op0=ALU.subtract),
                     reads=[bt1, "gcl"], writes=[f"xx{i}"])
                P.op("act", lambda e: e.activation(out=eg[:, 0:nt], in_=bk1[:, 0:nt], func=AF.Exp), reads=[bt1, f"xx{i}"], writes=[egt])
                yield
                P.op("pool", lambda e: e.tensor_tensor(out=a1[i][0:nt, 0:nt], in0=xx[i][0:nt, 0:nt], in1=NM1[0:nt, 0:nt], op=ALU.add), reads=[f"xx{i}", "cst"], writes=[f"a1{i}"])
                P.op("pool", lambda e: e.tensor_tensor(out=a2[i][0:nt, 0:nt], in0=xx[i][0:nt, 0:nt], in1=NM2[0:nt, 0:nt], op=ALU.subtract), reads=[f"xx{i}", "cst"], writes=[f"a2{i}"])
                P.op("act", lambda e: e.activation(out=Dst[i][0:nt, 0:nt], in_=a1[i][0:nt, 0:nt], func=AF.Exp, scale=-1.0), reads=[f"a1{i}"], writes=[f"Dst{i}"])
                P.op("act", lambda e: e.activation(out=DTt[i][0:nt, 0:nt], in_=a2[i][0:nt, 0:nt], func=AF.Exp), reads=[f"a2{i}"], writes=[f"DTt{i}"])
                yield
                b2, bk2, bt2 = nb()
                P.op("pe", lambda e: e.matmul(bk2[0:nt, 0:nt], kc, kc, start=True, stop=True), reads=[KT], writes=[bt2])
                P.op("dve", lambda e: e.scalar_tensor_tensor(out=Mm[i][0:nt, 0:nt], in0=bk2[0:nt, 0:nt], scalar=beta[0:nt, tt, h:h + 1], in1=Dst[i][0:nt, 0:nt],
                                                             op0=ALU.mult, op1=ALU.mult), reads=[bt2, "beta", f"Dst{i}"], writes=[f"Mm{i}"])
                yield
                b3, bk3, bt3 = nb()
                P.op("pe", lambda e: e.transpose(bk3[0:nt, 0:nt], Mm[i][0:nt, 0:nt], ident[0:nt, 0:nt]), reads=[f"Mm{i}", "cst"], writes=[bt3])
                P.op("dve", lambda e: e.tensor_tensor(out=YQ[0][i][0:nt, 0:nt], in0=ident[0:nt, 0:nt], in1=bk3[0:nt, 0:nt], op=ALU.subtract),
                     reads=[bt3, "cst"], writes=[f"YQ0{i}"])
                evac("act", Nm[i][0:nt, 0:nt], bk3[0:nt, 0:nt], [bt3, f"YQ0{i}"], [f"Nm{i}"])
                yield
                b4, bk4, bt4 = nb()
                P.op("pe", lambda e: e.matmul(bk4[0:nt, 0:nt], Mm[i][0:nt, 0:nt], Nm[i][0:nt, 0:nt], start=True, stop=True), reads=[f"Mm{i}", f"Nm{i}"], writes=[bt4])
                evac("act", YQ[0][i][0:nt, nt:2 * nt], bk4[0:nt, 0:nt], [bt4], [f"YQ0{i}"])
                b5, bk5, bt5 = nb()
                P.op("pe", lambda e: e.matmul(bk5[0:nt, 0:nt], Nm[i][0:nt, 0:nt], Mm[i][0:nt, 0:nt], start=True, stop=True), reads=[f"Mm{i}", f"Nm{i}"], writes=[bt5])
                evac("dve", QT[0][i][0:nt, 0:nt], bk5[0:nt, 0:nt], [bt5], [f"QT0{i}"])
                yield
                for r in range(5):
                    cur = r % 2; nx = 1 - cur
                    b6, bk6, bt6 = nb()
                    ncol = 2 * nt if r < 4 else nt
                    P.op("pe", lambda e: e.matmul(bk6[0:nt, 0:ncol], QT[cur][i][0:nt, 0:nt], YQ[cur][i][0:nt, 0:ncol], start=True, stop=True),
                         reads=[f"QT{cur}{i}", f"YQ{cur}{i}"], writes=[bt6])
                    P.op("dve", lambda e: e.tensor_tensor(out=YQ[nx][i][0:nt, 0:nt], in0=YQ[cur][i][0:nt, 0:nt], in1=bk6[0:nt, 0:nt], op=ALU.add),
                         reads=[bt6, f"YQ{cur}{i}"], writes=[f"YQ{nx}{i}"])
                    if r < 4:
                        evac("act", YQ[nx][i][0:nt, nt:2 * nt], bk6[0:nt, nt:2 * nt], [bt6], [f"YQ{nx}{i}"])
                        b7, bk7, bt7 = nb()
                        P.op("pe", lambda e: e.matmul(bk7[0:nt, 0:nt], YQ[cur][i][0:nt, nt:2 * nt], QT[cur][i][0:nt, 0:nt], start=True, stop=True),
                             reads=[f"QT{cur}{i}", f"YQ{cur}{i}"], writes=[bt7])
                        evac("act", QT[nx][i][0:nt, 0:nt], bk7[0:nt, 0:nt], [bt7], [f"QT{nx}{i}"])
                    yield
                Tt = YQ[1][i][0:nt, 0:nt]; TtT = f"YQ1{i}"
                b8, bk8, bt8 = nb()
                P.op("pe", lambda e: e.transpose(bk8[0:nt, 0:128], kc, ident), reads=[KT, "cst"], writes=[bt8])
                P.op("dve", lambda e: e.tensor_scalar(out=kbg[i][0:nt, :], in0=bk8[0:nt, 0:128], scalar1=bg[0:nt, tt, h:h + 1], scalar2=None, op0=ALU.mult),
                     reads=[bt8, "bg"], writes=[f"kbg{i}"])
                P.op("act", lambda e: e.activation(out=kdec[pp][i][0:nt, :], in_=bk8[0:nt, 0:128], func=AF.Copy, scale=kdsc[0:nt, tt, h:h + 1]),
                     reads=[bt8, "kdsc", f"kbg{i}"], writes=[f"kdec{pp}{i}"])
                b9, bk9, bt9 = nb()
                P.op("pe", lambda e: e.transpose(bk9[0:nt, 0:128], vc, ident), reads=[VT, "cst"], writes=[bt9])
                P.op("dve", lambda e: e.tensor_scalar(out=vb[i][0:nt, :], in0=bk9[0:nt, 0:128], scalar1=beta[0:nt, tt, h:h + 1], scalar2=None, op0=ALU.mult),
                     reads=[bt9, "beta"], writes=[f"vb{i}"])
                yield
                b10, bk10, bt10 = nb()
                P.op("pe", lambda e: e.matmul(bk10[0:nt, 0:128], Tt, vb[i][0:nt, :], start=True, stop=True), reads=[TtT, f"vb{i}"], writes=[bt10])
                evac("act", uu[pp][i][0:nt, :], bk10[0:nt, 0:128], [bt10], [f"uu{pp}{i}"])
                b11, bk11, bt11 = nb()
                P.op("pe", lambda e: e.matmul(bk11[:, 0:nt], kbg[i][0:nt, :], Tt, start=True, stop=True), reads=[TtT, f"kbg{i}"], writes=[bt11])
                evac("act", wT[pp][i][:, 0:nt], bk11[:, 0:nt], [bt11], [f"wT{pp}{i}"])
                yield
                b12, bk12, bt12 = nb()
                P.op("pe", lambda e: e.matmul(bk12[0:nt, 0:nt], kc, qc, start=True, stop=True), reads=[KT, QTg], writes=[bt12])
                P.op("dve", lambda e: e.tensor_tensor(out=qkT[pp][i][0:nt, 0:nt], in0=bk12[0:nt, 0:nt], in1=DTt[i][0:nt, 0:nt], op=ALU.mult),
                     reads=[bt12, f"DTt{i}"], writes=[f"qkT{pp}{i}"])
                P.op("pool", lambda e: e.tensor_tensor(out=qdT[pp][i][:, 0:nt], in0=qc, in1=eg[:, 0:nt], op=ALU.mult), reads=[QTg, egt], writes=[f"qdT{pp}{i}"])
                yield

            def rec_full(i, h, tl):
                tt, lo, nt, tc0, chunks, pp = tile_params(tl)
                eg = egrow[pp][i]; egt = f"egrow{pp}{i}"
                for (r0, L) in chunks:
                    bA, bkA, btA = 3 + 2 * i, pb[3 + 2 * i], f"pb{3 + 2 * i}"
                    P.op("pe", lambda e: e.matmul(bkA[r0:r0 + L, 0:128], wT[pp][i][:, r0:r0 + L], Sb[i][:, :], start=True, stop=True),
                         reads=[f"wT{pp}{i}", f"Sb{i}"], writes=[btA])
                    bB, bkB, btB = 4 + 2 * i, pb[4 + 2 * i], f"pb{4 + 2 * i}"
                    P.op("pe", lambda e: e.matmul(bkB[r0:r0 + L, 0:128], qdT[pp][i][:, r0:r0 + L], Sb[i][:, :], start=True, stop=False),
                         reads=[f"qdT{pp}{i}", f"Sb{i}"], writes=[btB])
                    yield
                    P.op("dve", lambda e: e.tensor_tensor(out=vnew[i][r0:r0 + L, :], in0=uu[pp][i][r0:r0 + L, :], in1=bkA[r0:r0 + L, 0:128], op=ALU.subtract),
                         reads=[btA, f"uu{pp}{i}"], writes=[f"vnew{i}"])
                    yield
                    P.op("pe", lambda e: e.matmul(bkB[r0:r0 + L, 0:128], qkT[pp][i][r0:r0 + L, r0:r0 + L], vnew[i][r0:r0 + L, :], start=False, stop=True),
                         reads=[f"qkT{pp}{i}", f"vnew{i}"], writes=[btB])
                    bC, bkC, btC = bA, bkA, btA
                    P.op("pe", lambda e: e.matmul(bkC[:, 0:128], kdec[pp][i][r0:r0 + L, :], vnew[i][r0:r0 + L, :], start=True, stop=True),
                         reads=[f"kdec{pp}{i}", f"vnew{i}"], writes=[btC])
                    yield
                    P.op("dve", lambda e: e.scalar_tensor_tensor(out=S[i][:, :], in0=S[i][:, :], scalar=eg[:, r0 + L - 1:r0 + L], in1=bkC[:, 0:128],
                                                                 op0=ALU.mult, op1=ALU.add), reads=[btC, f"S{i}", egt], writes=[f"S{i}"])
                    evac("act", odn[i][r0:r0 + L, :], bkB[r0:r0 + L, 0:128], [btB], [f"odn{i}"])
                    yield
                    P.op("act", lambda e: e.copy(out=Sb[i][:, :], in_=S[i][:, :]), reads=[f"S{i}"], writes=[f"Sb{i}"])
                    yield
                P.op("dve", lambda e: e.bn_stats(out=st6[i][0:nt, :], in_=odn[i][0:nt, :]), reads=[f"odn{i}"], writes=[f"st6{i}"])
                P.op("dve", lambda e: e.bn_aggr(out=mvv[i][0:nt, :], in_=st6[i][0:nt, :]), reads=[f"st6{i}"], writes=[f"mvv{i}"])
                P.op("dve", lambda e: e.scalar_tensor_tensor(out=msq[i][0:nt, :], in0=mvv[i][0:nt, 0:1], scalar=mvv[i][0:nt, 0:1], in1=mvv[i][0:nt, 1:2], op0=ALU.mult, op1=ALU.add),
                     reads=[f"mvv{i}"], writes=[f"msq{i}"])
                yield
                P.op("act", lambda e: e.activation(out=msq[i][0:nt, :], in_=msq[i][0:nt, :], func=AF.Ln, bias=epsr[0:nt, 0:1]), reads=[f"msq{i}", "epsg"], writes=[f"msq{i}"])
                P.op("act", lambda e: e.activation(out=msq[i][0:nt, :], in_=msq[i][0:nt, :], func=AF.Exp, scale=-0.5), reads=[f"msq{i}"], writes=[f"msq{i}"])
                yield
                P.op("dve", lambda e: e.scalar_tensor_tensor(out=t3[i][0:nt, :], in0=odn[i][0:nt, :], scalar=msq[i][0:nt, 0:1], in1=sm[0:nt, 273:401], op0=ALU.mult, op1=ALU.mult),
                     reads=[f"odn{i}", f"msq{i}", "sm"], writes=[f"t3{i}"])
                yield
                bD, bkD, btD = 3 + 2 * i, pb[3 + 2 * i], f"pb{3 + 2 * i}"
                P.op("pe", lambda e: e.transpose(bkD[:, 0:nt], t3[i][0:nt, :], ident[0:nt, 0:nt]), reads=[f"t3{i}", "cst"], writes=[btD])
                pj = tt % 2
                P.op("dve", lambda e: e.tensor_tensor(out=oTs[pj][i][:, 0:nt], in0=bkD[:, 0:nt], in1=szT[bp][i][:, lo:lo + nt], op=ALU.mult),
                     reads=[btD, f"szT{bp}{i}"], writes=[f"oTs{pj}{i}"])
                P.op("sp", lambda e: e.dma_start(out=oscr_d[8 + h, :, tc0:tc0 + nt], in_=oTs[pj][i][:, 0:nt]),
                     reads=[f"oTs{pj}{i}"], writes=[f"oscr{8 + h}"], dma=f"d_oscd{pj}{i}", out=True)
                yield

            ntl = len(tiles)
            run([pre_chain(i, h, 0) for i, h in enumerate(hg)], bgen)
            for tl in range(ntl):
                gens = [rec_full(i, h, tl) for i, h in enumerate(hg)]
                if tl + 1 < ntl:
                    gens += [pre_chain(i, h, tl + 1) for i, h in enumerate(hg)]
                run(gens, bgen)
            run(bgen)


def _tile_w(w):
    k, n = w.shape
    return np.ascontiguousarray(w.reshape(k // 128, 128, n // 128, 128).transpose(2, 1, 0, 3))


def _constants():
    p = np.arange(128)
    same = (p[:, None] // 64) == (p[None, :] // 64)
    cst = np.zeros((128, 8, 128), np.float32)
    cst[:, 0, :] = np.eye(128)
    sw = np.zeros((128, 128), np.float32)
    for m in range(128):
        k = (m // 64) * 64 + ((m % 64) + 32) % 64
        sw[k, m] = 1.0
    cst[:, 1, :] = sw
    cst[:, 2, :] = (same & (p[:, None] <= p[None, :])).astype(np.float32)
    cst[:, 3, :] = same.astype(np.float32)
    cst[:, 4, :] = np.where(same & (p[:, None] > p[None, :]), 0.0, BIG)
    cst[:, 5, :] = np.where(same & (p[None, :] >= p[:, None]), 0.0, BIG)
    cst[:, 6, :] = 1.0
    pos = np.arange(T, dtype=np.float32)
    inv_freq = (10000.0 ** (-np.arange(0, 64, 2, dtype=np.float32) / 64)).astype(np.float32)
    ang = pos[None, :] * inv_freq[:, None]
    d = p % 64
    cos = np.cos(ang)[d % 32]
    sin = np.sin(ang)[d % 32] * np.where(d < 32, -1.0, 1.0)[:, None]
    rope = np.stack([cos, sin]).astype(np.float32)
    return cst, rope


def _host_inputs(inp):
    f = lambda a: np.ascontiguousarray(np.asarray(a, dtype=np.float32))
    w_in = f(inp["w_in"][0])
    cst, rope = _constants()
    shared = {
        "cst": cst, "rope": rope,
        "waq": _tile_w(w_in[:, 0:1024]), "wak": _tile_w(w_in[:, 1024:2048]),
        "wav": np.ascontiguousarray(w_in[:, 2048:3072].reshape(16, 128, 2, 512).transpose(2, 1, 0, 3)),
        "wdq": _tile_w(w_in[:, 3072:4096]), "wdk": _tile_w(w_in[:, 4096:5120]),
        "wdv": _tile_w(w_in[:, 5120:6144]), "wdz": _tile_w(w_in[:, 6144:7168]),
        "wba": np.ascontiguousarray(w_in[:, 7168:7184].reshape(16, 128, 16).transpose(1, 0, 2)),
        "cw": np.ascontiguousarray(f(inp["conv_qkv_w"][0]).T.reshape(3, 8, 128, 4).transpose(2, 0, 1, 3)),
        "wout": np.ascontiguousarray(f(inp["w_out"][0]).reshape(16, 128, D).transpose(1, 0, 2)),
        "lnp": np.ascontiguousarray(np.stack([np.broadcast_to(f(inp[k][0])[None, :], (128, D))
                                             for k in ("ln1_g", "ln1_b", "ln2_g", "ln2_b")])),
        "wg": _tile_w(f(inp["ffn_w_gate"][0])), "wu": _tile_w(f(inp["ffn_w_up"][0])),
        "wd": np.ascontiguousarray(f(inp["ffn_w_down"][0]).reshape(NFF, 128, 16, 128).transpose(2, 1, 0, 3)),
    }
    fc = np.concatenate([f(inp["ffn_conv_w"][0]), f(inp["ffn_conv_b"][0])[None, :]], axis=0)
    shared["fcw"] = np.ascontiguousarray(fc.T.reshape(NFF, 128, 4).transpose(1, 0, 2))
    sm = np.zeros((128, 401), np.float32)
    for i, k in enumerate(("lambda_q1", "lambda_k1", "lambda_q2", "lambda_k2")):
        sm[:, i * 64:(i + 1) * 64] = f(inp[k][0])[None, :]
    sm[:, 256:264] = f(inp["a_log"][0])[None, :]
    sm[:, 264:272] = f(inp["dt_bias"][0])[None, :]
    sm[:, 272] = f(inp["diff_norm_w"][0])
    sm[:, 273:401] = f(inp["delta_norm_w"][0])[None, :]
    shared["sm"] = sm
    x = f(inp["x"]); meta = f(inp["meta_tokens"])
    per_core = []
    for c in range(8):
        b, hf = c // 2, c % 2
        h = np.concatenate([meta, x[b]], axis=0)
        r0 = 14 + 1024 * hf
        per_core.append({"hT": np.ascontiguousarray(h.T), "hrow": np.ascontiguousarray(h[r0:r0 + NTOK])})
    return shared, per_core


_NC_CACHE = {}


def kernel(**inputs):
    shared, per_core = _host_inputs(inputs)
    if "nc" not in _NC_CACHE:
        _NC_CACHE["nc"] = build()
    nc = _NC_CACHE["nc"]
    in_maps = [{**shared, **pc} for pc in per_core]
    res = run_bass_kernel_spmd(nc, in_maps, core_ids=list(range(8)))
    out = np.empty((4, 2048, D), np.float32)
    for c in range(8):
        b, hf = c // 2, c % 2
        out[b, 1024 * hf:1024 * (hf + 1)] = res.results[c]["out"]
    return out
```

```python
import math
import numpy as np
from contextlib import ExitStack
import concourse.bass as bass
import concourse.mybir as mybir
from concourse.bass_utils import run_bass_kernel_spmd

F32 = mybir.dt.float32
BF16 = mybir.dt.bfloat16
AF = mybir.ActivationFunctionType
ALU = mybir.AluOpType

D = 2048
T = 2064
NMETA = 16
DFF = 5632
NFF = 44
NTOK = 1026
ALPHA = 2.0 ** 0.25
LAM_INIT = 0.8 - 0.6 * math.exp(0.0)
LN_EPS = 1e-5
RMS_EPS = 1e-6
BIG = 30000.0


class Prog:
    def __init__(self, nc, stack):
        self.nc = nc
        self.stack = stack
        self.sems = {}
        self.engh = {"pe": nc.tensor, "act": nc.scalar, "dve": nc.vector, "pool": nc.gpsimd, "sp": nc.sync}
        self.cnt = {}
        self.seen = {e: {} for e in self.engh}
        self.last_w = {}
        self.readers = {}
        self.out_tokens = []
        self.nops = 0
        import os as _os
        self.same = _os.environ.get("SAME", "1") == "1"

    def _sem(self, k):
        if k not in self.sems:
            self.sems[k] = self.stack.enter_context(self.nc.semaphore("s_" + str(k)))
        return self.sems[k]

    def op(self, eng, fn, reads=(), writes=(), dma=None, out=False):
        deps = {}
        for r in reads:
            t = self.last_w.get(r)
            if t is not None and deps.get(t[0], 0) < t[1]:
                deps[t[0]] = t[1]
        for w in writes:
            t = self.last_w.get(w)
            if t is not None and deps.get(t[0], 0) < t[1]:
                deps[t[0]] = t[1]
            for k, v in self.readers.get(w, {}).items():
                if deps.get(k, 0) < v:
                    deps[k] = v
        key, amt = (eng, 1) if dma is None else (dma, 16)
        self.cnt[key] = self.cnt.get(key, 0) + amt
        tok = (key, self.cnt[key])
        e = self.engh[eng]
        for k, v in deps.items():
            if k == eng and (eng == "pe" or not self.same):
                continue
            if self.seen[eng].get(k, 0) < v:
                self.seen[eng][k] = v
                e.wait_ge(self._sem(k), v)
        fn(e).then_inc(self._sem(key), amt)
        self.nops += 1
        for r in reads:
            d = self.readers.setdefault(r, {})
            if d.get(key, 0) < tok[1]:
                d[key] = tok[1]
        for w in writes:
            self.last_w[w] = tok
            self.readers[w] = {}
        if out:
            self.out_tokens.append(tok)
        return tok

    def join(self, names):
        toks = [self.last_w[n] for n in names]
        k = toks[0][0]
        assert all(t[0] == k for t in toks)
        m = max(t[1] for t in toks)
        for n in names:
            self.last_w[n] = (k, m)

    def barrier(self):
        for eng, e in self.engh.items():
            for k, v in self.cnt.items():
                if self.seen[eng].get(k, 0) < v:
                    self.seen[eng][k] = v
                    e.wait_ge(self._sem(k), v)

    def finish(self):
        fin = {}
        for k, v in self.out_tokens:
            fin[k] = max(fin.get(k, 0), v)
        for k, v in fin.items():
            self.engh["sp"].wait_ge(self._sem(k), v)


def build(stages=("att", "gdn", "post"), att_heads=range(8), dn_heads=range(8), oscr_input=False, dbg=False, lvl=9):
    nc = bass.Bass("TRN2", target_bir_lowering=False)

    def din(name, shape, dt=F32):
        return nc.dram_tensor(name, list(shape), dt, kind="ExternalInput").ap()

    hT_d = din("hT", [D, T])
    hrow_d = din("hrow", [NTOK, D])
    cst_d = din("cst", [128, 8, 128])
    rope_d = din("rope", [2, 128, T])
    waq_d = din("waq", [8, 128, 16, 128])
    wak_d = din("wak", [8, 128, 16, 128])
    wav_d = din("wav", [2, 128, 16, 512])
    wdq_d = din("wdq", [8, 128, 16, 128])
    wdk_d = din("wdk", [8, 128, 16, 128])
    wdv_d = din("wdv", [8, 128, 16, 128])
    wdz_d = din("wdz", [8, 128, 16, 128])
    wba_d = din("wba", [128, 16, 16])
    cw_d = din("cw", [128, 3, 8, 4])
    sm_d = din("sm", [128, 4 * 64 + 8 + 8 + 1 + 128])
    wout_d = din("wout", [128, 16, D])
    lnp_d = din("lnp", [4, 128, D])
    wg_d = din("wg", [NFF, 128, 16, 128])
    wu_d = din("wu", [NFF, 128, 16, 128])
    wd_d = din("wd", [16, 128, NFF, 128])
    fcw_d = din("fcw", [128, NFF, 4])
    out_d = nc.dram_tensor("out", [1024, D], F32, kind="ExternalOutput").ap()
    if oscr_input:
        oscr_d = din("oscr", [16, 128, T], BF16)
    else:
        oscr_d = nc.dram_tensor("oscr", [16, 128, T], BF16, kind="ExternalOutput" if dbg else "Internal").ap()
    h1s_d = nc.dram_tensor("h1s", [1024, D], F32, kind="ExternalOutput" if dbg else "Internal").ap()

    with ExitStack() as top:
        P = Prog(nc, top)

        def sbt(st, name, shape, dt=F32):
            return st.enter_context(nc.sbuf_tensor("sb_" + name, list(shape), dt))

        pb = [top.enter_context(nc.psum_tensor(f"pb{i}", [128, 512], F32)) for i in range(7)]
        pbh = top.enter_context(nc.psum_tensor("pbh", [128, 1024], BF16))

        cst = sbt(top, "cst", [128, 8, 128])
        P.op("sp", lambda e: e.dma_start(out=cst[:], in_=cst_d), writes=["cst"], dma="d_cst0")
        ident = cst[:, 0, :]
        identb = sbt(top, "identb", [128, 128], BF16)
        pswapb = sbt(top, "pswapb", [128, 128], BF16)
        onesb = sbt(top, "onesb", [128, 128], BF16)
        P.op("dve", lambda e: e.tensor_copy(out=identb[:], in_=cst[:, 0, :]), reads=["cst"], writes=["identb"])
        P.op("dve", lambda e: e.tensor_copy(out=pswapb[:], in_=cst[:, 1, :]), reads=["cst"], writes=["pswapb"])
        P.op("dve", lambda e: e.tensor_copy(out=onesb[:], in_=cst[:, 6, :]), reads=["cst"], writes=["onesb"])
        sm = sbt(top, "sm", [128, 401])
        P.op("sp", lambda e: e.dma_start(out=sm[:], in_=sm_d), writes=["sm"], dma="d_cst1")

        if "att" in stages or "gdn" in stages:
            with ExitStack() as mix:
                xT = sbt(mix, "xT", [128, 16, T], BF16)
                for c in range(16):
                    P.op("pool", lambda e, c=c: e.dma_start(out=xT[:, c, :], in_=hT_d[c * 128:(c + 1) * 128, :]),
                         writes=[f"xT{c}"], dma=f"d_xT{c // 4}")
                XT = [f"xT{c}" for c in range(16)]
                for g4 in range(4):
                    P.join(XT[4 * g4:4 * g4 + 4])
                if "att" in stages:
                    with ExitStack() as st:
                        emit_attention(nc, P, st, sbt, pb, pbh, xT, XT, cst, identb, pswapb, onesb, sm, rope_d,
                                       waq_d, wak_d, wav_d, oscr_d, list(att_heads), lvl=lvl)
                    P.barrier()
                if "gdn" in stages:
                    with ExitStack() as st:
                        emit_gdn(nc, P, st, sbt, pb, pbh, xT, XT, cst, sm, wdq_d, wdk_d, wdv_d, wdz_d, wba_d, cw_d,
                                 oscr_d, list(dn_heads))
                    P.barrier()
            P.barrier()

        if "post" in stages:
            with ExitStack() as st:
                emit_post(nc, P, st, sbt, pb, pbh, cst, identb, oscr_d, hrow_d, wout_d, lnp_d, wg_d, wu_d, wd_d, fcw_d,
                          h1s_d, out_d, oscr_input)
        P.barrier()
        P.finish()
    return nc


class Region:
    def __init__(self, tile, nwords):
        self.t = tile; self.n = nwords; self.off = 0

    def reset(self):
        self.off = 0

    def take(self, nelem, dt=F32):
        words = nelem if dt == F32 else (nelem + 1) // 2
        ap = self.t[:, self.off:self.off + words]
        self.off += words
        assert self.off <= self.n, (self.off, self.n)
        return ap if dt == F32 else ap.bitcast(dt)


def emit_post(nc, P, st, sbt, pb, pbh, cst, identb, oscr_d, hrow_d, wout_d, lnp_d, wg_d, wu_d, wd_d, fcw_d,
              h1s_d, out_d, oscr_input):
    ident = cst[:, 0, :]
    pid = nc.sync.partition_id()
    col0 = (pid % 2) * 1024 + 14
    ttiles = [(0, 2)] + [(2 + 128 * i, 128) for i in range(8)]

    RA = Region(sbt(st, "RA", [128, 22528]), 22528)
    RB = Region(sbt(st, "RB", [128, 16384]), 16384)
    RC = Region(sbt(st, "RC", [128, 8208]), 8208)
    fcw = sbt(st, "fcw", [128, NFF, 4])
    P.op("sp", lambda e: e.dma_start(out=fcw[:], in_=fcw_d), writes=["fcw"], dma="d_fcw")
    stats = sbt(st, "stats", [128, 4, 6]); mv = sbt(st, "mv", [128, 2]); rstd = sbt(st, "rstd", [128, 1])
    eps = sbt(st, "eps", [128, 1])
    P.op("dve", lambda e: e.memset(eps[:], LN_EPS), writes=["eps"])
    h1T = RC.take(16 * NTOK, BF16).rearrange("p (c t) -> p c t", c=16)

    def layer_norm_tile(y, n, gsb, bsb, tag):
        yv = y[0:n, :].rearrange("p (c f) -> p c f", c=4)
        for c in range(4):
            P.op("dve", lambda e, c=c: e.bn_stats(out=stats[0:n, c, :], in_=yv[:, c, :]), reads=[tag], writes=["stats"])
        P.op("dve", lambda e: e.bn_aggr(out=mv[0:n, :], in_=stats[0:n, :, :].rearrange("p c s -> p (c s)")),
             reads=["stats"], writes=["mv"])
        P.op("act", lambda e: e.activation(out=rstd[0:n, :], in_=mv[0:n, 1:2], func=AF.Sqrt, bias=eps[0:n, :]),
             reads=["mv", "eps"], writes=["rstd"])
        P.op("dve", lambda e: e.reciprocal(out=rstd[0:n, :], in_=rstd[0:n, :]), reads=["rstd"], writes=["rstd"])
        P.op("dve", lambda e: e.tensor_scalar(out=y[0:n, :], in0=y[0:n, :], scalar1=mv[0:n, 0:1], scalar2=rstd[0:n, 0:1],
                                              op0=ALU.subtract, op1=ALU.mult), reads=[tag, "mv", "rstd"], writes=[tag])
        P.op("pool", lambda e: e.tensor_tensor(out=y[0:n, :], in0=y[0:n, :], in1=gsb[0:n, :], op=ALU.mult),
             reads=[tag, "lng"], writes=[tag])
        P.op("pool", lambda e: e.tensor_tensor(out=y[0:n, :], in0=y[0:n, :], in1=bsb[0:n, :], op=ALU.add),
             reads=[tag, "lnb"], writes=[tag])

    RA.reset(); RB.reset()
    oTm = RA.take(16 * NTOK, BF16).rearrange("p (h t) -> p h t", h=16)
    lng = RA.take(D); lnb = RA.take(D)
    ybuf = [RA.take(D), RA.take(D)]
    h1b = [RA.take(D, BF16), RA.take(D, BF16)]
    wout = RB.take(16 * D, BF16).rearrange("p (h d) -> p h d", h=16)
    OTM = [f"oTm{hd}" for hd in range(16)]
    WOUT = [f"wout{hd}" for hd in range(16)]
    for hd in range(16):
        P.op("sp", lambda e, hd=hd: e.dma_start(out=oTm[:, hd, :], in_=oscr_d[hd, :, bass.ds(col0, NTOK)]),
             reads=[f"oscr{hd}"], writes=[OTM[hd]], dma="d_oTm")
    P.join(OTM)
    for hd in range(16):
        P.op("pool", lambda e, hd=hd: e.dma_start(out=wout[:, hd, :], in_=wout_d[:, hd, :]),
             writes=[WOUT[hd]], dma="d_wout")
    P.join(WOUT)
    P.op("sp", lambda e: e.dma_start(out=lng, in_=lnp_d[0]), writes=["lng"], dma="d_lng")
    P.op("sp", lambda e: e.dma_start(out=lnb, in_=lnp_d[1]), writes=["lnb"], dma="d_lnb")
    pend_tr = [None]
    for ti, (r0, n) in enumerate(ttiles):
        b = ti % 2
        y = ybuf[b]; ytag = f"y{b}"
        P.op("sp", lambda e, y=y, r0=r0, n=n: e.dma_start(out=y[0:n, :], in_=hrow_d[r0:r0 + n, :]),
             writes=[ytag], dma=f"d_y{b}")
        for db in range(4):
            acc = pb[db]
            for hd in range(16):
                P.op("pe", lambda e, acc=acc, hd=hd, r0=r0, n=n, db=db: e.matmul(
                    acc[0:n, :], oTm[:, hd, r0:r0 + n], wout[:, hd, db * 512:(db + 1) * 512],
                    start=(hd == 0), stop=(hd == 15)), reads=[OTM[hd], WOUT[hd]], writes=[f"pb{db}"])
            P.op("dve", lambda e, acc=acc, y=y, n=n, db=db: e.scalar_tensor_tensor(
                out=y[0:n, db * 512:(db + 1) * 512], in0=y[0:n, db * 512:(db + 1) * 512], scalar=ALPHA,
                in1=acc[0:n, :], op0=ALU.mult, op1=ALU.add), reads=[ytag, f"pb{db}"], writes=[ytag])
        if pend_tr[0] is not None:
            pend_tr[0](); pend_tr[0] = None
        layer_norm_tile(y, n, lng, lnb, ytag)
        hb = h1b[b]
        P.op("act", lambda e, hb=hb, y=y, n=n: e.copy(out=hb[0:n, :], in_=y[0:n, :]), reads=[ytag], writes=[f"h1b{b}"])
        if ti > 0:
            P.op("sp", lambda e, y=y, ti=ti: e.dma_start(out=h1s_d[(ti - 1) * 128:ti * 128, :], in_=y[:, :]),
                 reads=[ytag], writes=["h1s"], dma="d_h1s", out=True)
        def _tr(hb=hb, n=n, b=b, r0=r0):
            for cg in range(2):
                for cc in range(8):
                    c = cg * 8 + cc
                    P.op("pe", lambda e, hb=hb, n=n, c=c, cc=cc: e.transpose(
                        pbh[:, cc * 128:cc * 128 + n], hb[0:n, c * 128:(c + 1) * 128], identb[0:n, 0:n]),
                        reads=[f"h1b{b}", "identb"], writes=["pbh"])
                src = pbh[:, :].rearrange("p (c t) -> p c t", c=8)
                if cg == 0:
                    P.op("act", lambda e, src=src, cg=cg, r0=r0, n=n: e.copy(
                        out=h1T[:, cg * 8:(cg + 1) * 8, r0:r0 + n], in_=src[:, :, 0:n]), reads=["pbh"], writes=["h1T"])
                else:
                    P.op("dve", lambda e, src=src, cg=cg, r0=r0, n=n: e.tensor_copy(
                        out=h1T[:, cg * 8:(cg + 1) * 8, r0:r0 + n], in_=src[:, :, 0:n]), reads=["pbh"], writes=["h1T"])
        pend_tr[0] = _tr
    if pend_tr[0] is not None:
        pend_tr[0](); pend_tr[0] = None
    P.barrier()

    RA.reset(); RB.reset()
    hidT = RA.take(NFF * 1024, BF16).rearrange("p (f t) -> p f t", f=NFF)
    NWB = 3
    wgb = [RB.take(16 * 128, BF16).rearrange("p (c n) -> p c n", c=16) for i in range(NWB)]
    wub = [RB.take(16 * 128, BF16).rearrange("p (c n) -> p c n", c=16) for i in range(NWB)]
    gsb = [RB.take(NTOK) for i in range(2)]
    cacc = [RB.take(1024) for i in range(2)]
    sil = [RB.take(1024) for i in range(2)]

    def load_w(f):
        s = f % NWB
        P.op("pool", lambda e: e.dma_start(out=wgb[s], in_=wg_d[f]), writes=[f"wgb{s}"], dma=f"d_wg{s}")
        P.op("pool", lambda e: e.dma_start(out=wub[s], in_=wu_d[f]), writes=[f"wub{s}"], dma=f"d_wu{s}")

    load_w(0); load_w(1)
    for f in range(NFF):
        s = f % NWB; b = f % 2
        if f + 2 < NFF:
            load_w(f + 2)
        for half in range(2):
            for c in range(16):
                P.op("pe", lambda e, c=c, half=half, s=s: e.matmul(
                    pb[half][:, :], wgb[s][:, c, :], h1T[:, c, 2 + half * 512:2 + (half + 1) * 512],
                    start=(c == 0), stop=(c == 15)), reads=[f"wgb{s}", "h1T"], writes=[f"pb{half}"])
        for c in range(16):
            P.op("pe", lambda e, c=c, s=s: e.matmul(pb[4][:, 0:2], wgb[s][:, c, :], h1T[:, c, 0:2],
                                                    start=(c == 0), stop=(c == 15)),
                 reads=[f"wgb{s}", "h1T"], writes=["pb4"])
        for half in range(2):
            for c in range(16):
                P.op("pe", lambda e, c=c, half=half, s=s: e.matmul(
                    pb[2 + half][:, :], wub[s][:, c, :], h1T[:, c, 2 + half * 512:2 + (half + 1) * 512],
                    start=(c == 0), stop=(c == 15)), reads=[f"wub{s}", "h1T"], writes=[f"pb{2 + half}"])
        g = gsb[b]; ca = cacc[b]; sl = sil[b]
        P.op("act", lambda e, g=g: e.copy(out=g[:, 0:2], in_=pb[4][:, 0:2]), reads=["pb4"], writes=[f"gsb{b}"])
        for half in range(2):
            P.op("act", lambda e, g=g, half=half: e.copy(out=g[:, 2 + half * 512:2 + (half + 1) * 512], in_=pb[half][:, :]),
                 reads=[f"pb{half}"], writes=[f"gsb{b}"])
        P.op("dve", lambda e, g=g, ca=ca, f=f: e.tensor_scalar(
            out=ca[:, :], in0=g[:, 2:1026], scalar1=fcw[:, f, 2:3], scalar2=fcw[:, f, 3:4], op0=ALU.mult, op1=ALU.add),
            reads=[f"gsb{b}", "fcw"], writes=[f"cacc{b}"])
        P.op("dve", lambda e, g=g, ca=ca, f=f: e.scalar_tensor_tensor(
            out=ca[:, :], in0=g[:, 1:1025], scalar=fcw[:, f, 1:2], in1=ca[:, :], op0=ALU.mult, op1=ALU.add),
            reads=[f"gsb{b}", "fcw", f"cacc{b}"], writes=[f"cacc{b}"])
        P.op("dve", lambda e, g=g, ca=ca, f=f: e.scalar_tensor_tensor(
            out=ca[:, :], in0=g[:, 0:1024], scalar=fcw[:, f, 0:1], in1=ca[:, :], op0=ALU.mult, op1=ALU.add),
            reads=[f"gsb{b}", "fcw", f"cacc{b}"], writes=[f"cacc{b}"])
        P.op("act", lambda e, ca=ca, sl=sl: e.activation(out=sl[:, :], in_=ca[:, :], func=AF.Silu),
             reads=[f"cacc{b}"], writes=[f"sil{b}"])
        for half in range(2):
            P.op("dve", lambda e, sl=sl, half=half, f=f: e.tensor_tensor(
                out=hidT[:, f, half * 512:(half + 1) * 512], in0=sl[:, half * 512:(half + 1) * 512],
                in1=pb[2 + half][:, :], op=ALU.mult), reads=[f"sil{b}", f"pb{2 + half}"], writes=[f"hid{f}"])
    P.barrier()
    HID = [f"hid{f}" for f in range(NFF)]

    RB.reset(); RC.reset()
    y2 = RB.take(8 * D).rearrange("p (t d) -> p t d", t=8)
    Y2 = [f"y2_{tt}" for tt in range(8)]
    for tt in range(8):
        P.op("sp", lambda e, tt=tt: e.dma_start(out=y2[:, tt, :], in_=h1s_d[tt * 128:(tt + 1) * 128, :]),
             reads=["h1s"], writes=[Y2[tt]], dma="d_y2")
    P.join(Y2)
    wdb = [RC.take(NFF * 128, BF16).rearrange("p (f n) -> p f n", f=NFF) for i in range(2)]
    fsb = [RC.take(1024) for i in range(2)]

    def load_wd(dt):
        s = dt % 2
        P.op("pool", lambda e: e.dma_start(out=wdb[s], in_=wd_d[dt]), writes=[f"wdb{s}"], dma=f"d_wd{s}")

    load_wd(0)
    for dt in range(16):
        s = dt % 2
        if dt + 1 < 16:
            load_wd(dt + 1)
        for half in range(2):
            for f in range(NFF):
                P.op("pe", lambda e, half=half, f=f, s=s: e.matmul(
                    pb[half + 4 * s][:, :], wdb[s][:, f, :], hidT[:, f, half * 512:(half + 1) * 512],
                    start=(f == 0), stop=(f == NFF - 1)), reads=[f"wdb{s}", HID[f]], writes=[f"pb{half + 4 * s}"])
        fs = fsb[s]
        for half in range(2):
            P.op("act", lambda e, fs=fs, half=half, s=s: e.copy(out=fs[:, half * 512:(half + 1) * 512], in_=pb[half + 4 * s][:, :]),
                 reads=[f"pb{half + 4 * s}"], writes=[f"fsb{s}"])
        for tg in range(2):
            bank = pb[2 + tg]
            for k in range(4):
                tt = tg * 4 + k
                P.op("pe", lambda e, bank=bank, k=k, tt=tt, fs=fs: e.transpose(
                    bank[:, k * 128:(k + 1) * 128], fs[:, tt * 128:(tt + 1) * 128], ident),
                    reads=[f"fsb{s}", "cst"], writes=[f"pb{2 + tg}"])
            src = bank[:, :].rearrange("p (k n) -> p k n", k=4)
            P.op("dve", lambda e, src=src, tg=tg, dt=dt: e.scalar_tensor_tensor(
                out=y2[:, tg * 4:(tg + 1) * 4, dt * 128:(dt + 1) * 128], in0=y2[:, tg * 4:(tg + 1) * 4, dt * 128:(dt + 1) * 128],
                scalar=ALPHA, in1=src, op0=ALU.mult, op1=ALU.add),
                reads=[f"pb{2 + tg}"] + Y2[tg * 4:(tg + 1) * 4], writes=Y2[tg * 4:(tg + 1) * 4])
    P.barrier()
    RA.reset()
    lng = RA.take(D); lnb = RA.take(D)
    P.op("sp", lambda e: e.dma_start(out=lng, in_=lnp_d[2]), writes=["lng"], dma="d_lng")
    P.op("sp", lambda e: e.dma_start(out=lnb, in_=lnp_d[3]), writes=["lnb"], dma="d_lnb")
    for tt in range(8):
        layer_norm_tile(y2[:, tt, :], 128, lng, lnb, Y2[tt])
        P.op("sp", lambda e, tt=tt: e.dma_start(out=out_d[tt * 128:(tt + 1) * 128, :], in_=y2[:, tt, :]),
             reads=[Y2[tt]], dma="d_out", out=True)


def emit_attention(nc, P, st, sbt, pb, pbh, xT, XT, cst, identb, pswapb, onesb, sm, rope_d, waq_d, wak_d, wav_d,
                   oscr_d, heads, lvl=9):
    AXX = mybir.AxisListType.X
    onesf = cst[:, 6, :]
    cosT = sbt(st, "cosT", [128, T]); sinT = sbt(st, "sinT", [128, T])
    P.op("sp", lambda e: e.dma_start(out=cosT[:], in_=rope_d[0]), writes=["cosT"], dma="d_cos")
    P.op("sp", lambda e: e.dma_start(out=sinT[:], in_=rope_d[1]), writes=["sinT"], dma="d_sin")
    prod = sbt(st, "lprod", [128, 2, 64]); ls = sbt(st, "lsum", [128, 2]); neglam = sbt(st, "neglam", [128, 1])
    nws = sbt(st, "nws", [128, 1]); epsr = sbt(st, "epsr", [128, 1])
    P.op("dve", lambda e: e.tensor_tensor(out=prod[:, 0, :], in0=sm[:, 0:64], in1=sm[:, 64:128], op=ALU.mult), reads=["sm"], writes=["lprod"])
    P.op("dve", lambda e: e.tensor_tensor(out=prod[:, 1, :], in0=sm[:, 128:192], in1=sm[:, 192:256], op=ALU.mult), reads=["sm", "lprod"], writes=["lprod"])
    P.op("dve", lambda e: e.reduce_sum(out=ls[:, :], in_=prod[:, :, :], axis=AXX), reads=["lprod"], writes=["lsum"])
    P.op("act", lambda e: e.activation(out=ls[:, :], in_=ls[:, :], func=AF.Exp), reads=["lsum"], writes=["lsum"])
    P.op("dve", lambda e: e.tensor_tensor(out=neglam[:, :], in0=ls[:, 1:2], in1=ls[:, 0:1], op=ALU.subtract), reads=["lsum"], writes=["neglam"])
    P.op("dve", lambda e: e.tensor_scalar(out=neglam[:, :], in0=neglam[:, :], scalar1=-LAM_INIT, scalar2=None, op0=ALU.add), reads=["neglam"], writes=["neglam"])
    P.op("dve", lambda e: e.tensor_scalar(out=nws[:, :], in0=sm[:, 272:273], scalar1=1.0 - LAM_INIT, scalar2=None, op0=ALU.mult), reads=["sm"], writes=["nws"])
    P.op("dve", lambda e: e.memset(epsr[:], RMS_EPS), writes=["epsr"])

    wv = sbt(st, "wv", [128, 16, 512], BF16)
    vtm = sbt(st, "vtm", [128, 17, 512], BF16)
    wqb = [sbt(st, f"wqb{i}", [128, 16, 128], BF16) for i in range(2)]
    wkb = [sbt(st, f"wkb{i}", [128, 16, 128], BF16) for i in range(2)]
    qT = sbt(st, "qT", [128, T], BF16); kT = sbt(st, "kT", [128, T], BF16)
    qb = [sbt(st, f"qb{i}", [128, 512], BF16) for i in range(2)]
    t1 = [sbt(st, f"t1_{i}", [128, 512]) for i in range(2)]
    t2 = [sbt(st, f"t2_{i}", [128, 512]) for i in range(2)]
    ptb = [sbt(st, f"ptb{i}", [128, 2, 256], BF16) for i in range(3)]
    rZ = sbt(st, "rZ", [128, 2, 256]); Aa = sbt(st, "Aa", [128, 2, 256])
    od = sbt(st, "od", [128, 256]); sq = sbt(st, "sq", [128, 256]); rs = sbt(st, "rs", [128, 256])
    oTb = [sbt(st, f"oTb{i}", [128, T], BF16) for i in range(2)]
    P.op("pool", lambda e: e.memset(vtm[:], 0.0), writes=["vtm"])

    blocks = [(0, 16)] + [(16 + 512 * j, 512) for j in range(4)]
    ttiles = [(0, 16)] + [(16 + 128 * j, 128) for j in range(16)]
    cnt = {"acc": 0, "qb": 0, "st": 0, "pt": 0}
    cur_group = [None]

    def load_qk(hi):
        h = heads[hi]; par = hi % 2
        P.op("pool", lambda e: e.dma_start(out=wqb[par][:], in_=waq_d[h]), writes=[f"wqb{par}"], dma=f"d_wq{par}")
        P.op("pool", lambda e: e.dma_start(out=wkb[par][:], in_=wak_d[h]), writes=[f"wkb{par}"], dma=f"d_wk{par}")

    load_qk(0)
    for hi, h in enumerate(heads):
        par = hi % 2
        if hi + 1 < len(heads):
            load_qk(hi + 1)
        g = h // 4
        if cur_group[0] != g:
            cur_group[0] = g
            P.op("pool", lambda e, g=g: e.dma_start(out=wv[:], in_=wav_d[g]), writes=["wv"], dma="d_wv")
            for tt, (c0, n) in enumerate(ttiles):
                bi = 4 + cnt["acc"] % 2; cnt["acc"] += 1
                acc = pb[bi]
                for c in range(16):
                    P.op("pe", lambda e, acc=acc, c=c, c0=c0, n=n: e.matmul(acc[0:n, :], xT[:, c, c0:c0 + n], wv[:, c, :],
                                                                           start=(c == 0), stop=(c == 15)),
                         reads=[XT[c], "wv"], writes=[f"pb{bi}"])
                P.op("act", lambda e, acc=acc, tt=tt, n=n: e.copy(out=vtm[0:n, tt, :], in_=acc[0:n, :]), reads=[f"pb{bi}"], writes=["vtm"])
        hc = (h % 4) * 128
        if lvl < 1:
            continue
        for which, wsb, wtag, dst, dtag in (("q", wqb[par], f"wqb{par}", qT, "qT"), ("k", wkb[par], f"wkb{par}", kT, "kT")):
            for (c0, n) in blocks:
                bi = 4 + cnt["acc"] % 2; cnt["acc"] += 1
                acc = pb[bi]
                qi = cnt["qb"] % 2; cnt["qb"] += 1
                for c in range(16):
                    P.op("pe", lambda e, acc=acc, c=c, c0=c0, n=n, wsb=wsb: e.matmul(acc[:, 0:n], wsb[:, c, :], xT[:, c, c0:c0 + n],
                                                                                    start=(c == 0), stop=(c == 15)),
                         reads=[XT[c], wtag], writes=[f"pb{bi}"])
                P.op("act", lambda e, acc=acc, qi=qi, n=n: e.copy(out=qb[qi][:, 0:n], in_=acc[:, 0:n]), reads=[f"pb{bi}"], writes=[f"qb{qi}"])
                P.op("pe", lambda e, qi=qi, n=n: e.matmul(pb[6][:, 0:n], pswapb[:, :], qb[qi][:, 0:n], start=True, stop=True),
                     reads=["pswapb", f"qb{qi}"], writes=["pb6"])
                P.op("dve", lambda e, qi=qi, n=n, c0=c0: e.tensor_tensor(out=t1[qi][:, 0:n], in0=pb[6][:, 0:n], in1=sinT[:, c0:c0 + n], op=ALU.mult),
                     reads=["pb6", "sinT"], writes=[f"t1_{qi}"])
                P.op("dve", lambda e, acc=acc, qi=qi, n=n, c0=c0: e.tensor_tensor(out=t2[qi][:, 0:n], in0=acc[:, 0:n], in1=cosT[:, c0:c0 + n], op=ALU.mult),
                     reads=[f"pb{bi}", "cosT"], writes=[f"t2_{qi}"])
                P.op("pool", lambda e, qi=qi, n=n, c0=c0, dst=dst: e.tensor_tensor(out=dst[:, c0:c0 + n], in0=t1[qi][:, 0:n], in1=t2[qi][:, 0:n], op=ALU.add),
                     reads=[f"t1_{qi}", f"t2_{qi}"], writes=[dtag])
        if lvl < 2:
            continue
        ob = oTb[par]; otag = f"oTb{par}"
        groups = [(0, 16, [(0, 16, 0, 0, None)])]
        for gq in range(8):
            kts = [(0, 16, 0, 0, None)]
            for j in range(2 * gq + 2):
                qlo = 0 if j <= 2 * gq else 128
                dl = 0 if j == 2 * gq else (128 if j == 2 * gq + 1 else None)
                kts.append((16 + 128 * j, 128, j + 1, qlo, dl))
            groups.append((16 + 256 * gq, 256, kts))
        if lvl < 3:
            groups = groups[:1]
        elif lvl < 4:
            groups = groups[:2]
        pending = [None]
        for (qc0, nq, kts) in groups:
            O = pb[2][:, 0:2 * nq].rearrange("p (c n) -> p c n", c=2)
            Z = pb[3][:, 0:2 * nq].rearrange("p (c n) -> p c n", c=2)
            slots = {}

            def emit_scores(ki, qc0=qc0, nq=nq, kts=kts, slots=slots):
                kc0, nk, vt, qlo, dl = kts[ki]
                si = cnt["st"] % 2; cnt["st"] += 1
                pi = cnt["pt"] % 3; cnt["pt"] += 1
                slots[ki] = pi
                sbk = (0, 1) if si == 0 else (4, 5)
                PT = ptb[pi]
                for cc in range(2):
                    P.op("pe", lambda e, cc=cc: e.matmul(
                        pb[sbk[cc]][0:nk, qlo:nq], kT[cc * 64:(cc + 1) * 64, kc0:kc0 + nk], qT[cc * 64:(cc + 1) * 64, qc0 + qlo:qc0 + nq],
                        start=True, stop=True), reads=["kT", "qT"], writes=[f"pb{sbk[cc]}"])
                for cc in range(2):
                    P.op("act", lambda e, cc=cc: e.activation(
                        out=PT[0:nk, cc, qlo:nq], in_=pb[sbk[cc]][0:nk, qlo:nq], func=AF.Exp, scale=0.125), reads=[f"pb{sbk[cc]}"], writes=[f"ptb{pi}"])
                if dl is not None:
                    P.op("dve", lambda e: e.memset(PT[64:128, :, dl:dl + 64], 0.0), reads=[f"ptb{pi}"], writes=[f"ptb{pi}"])

            def emit_av(ki, qc0=qc0, nq=nq, kts=kts, slots=slots, O=O, Z=Z):
                kc0, nk, vt, qlo, dl = kts[ki]
                pi = slots[ki]; PT = ptb[pi]
                first = ki == 0; last = ki == len(kts) - 1
                lastflat = ki == len(kts) - 2
                if qlo == 0 and nq == 256:
                    PTf = PT[0:nk, :, :].rearrange("p c n -> p (c n)")
                    P.op("pe", lambda e: e.matmul(pb[2][:, 0:512], vtm[0:nk, vt, hc:hc + 128], PTf, start=first, stop=lastflat),
                         reads=["vtm", f"ptb{pi}"], writes=["pb2"])
                    P.op("pe", lambda e: e.matmul(pb[3][:, 0:512], onesb[0:nk, :], PTf, start=first, stop=lastflat),
                         reads=["onesb", f"ptb{pi}"], writes=["pb3"])
                else:
                    for cc in range(2):
                        P.op("pe", lambda e, cc=cc: e.matmul(
                            O[:, cc, qlo:nq], vtm[0:nk, vt, hc:hc + 128], PT[0:nk, cc, qlo:nq], start=(first and cc == 0), stop=(last and cc == 1),
                            skip_group_check=True), reads=["vtm", f"ptb{pi}"], writes=["pb2"])
                        P.op("pe", lambda e, cc=cc: e.matmul(
                            Z[:, cc, qlo:nq], onesb[0:nk, :], PT[0:nk, cc, qlo:nq], start=(first and cc == 0), stop=(last and cc == 1),
                            skip_group_check=True), reads=["onesb", f"ptb{pi}"], writes=["pb3"])

            nk_ = len(kts)
            emit_scores(0)
            if nk_ > 1:
                emit_scores(1)
            if pending[0] is not None:
                pending[0](); pending[0] = None
            for ki in range(nk_):
                if ki + 2 < nk_:
                    emit_scores(ki + 2)
                emit_av(ki)
            P.op("dve", lambda e, Z=Z, nq=nq: e.reciprocal(out=rZ[:, :, 0:nq], in_=Z), reads=["pb3"], writes=["rZ"])
            P.op("dve", lambda e, O=O, nq=nq: e.tensor_tensor(out=Aa[:, :, 0:nq], in0=O, in1=rZ[:, :, 0:nq], op=ALU.mult),
                 reads=["pb2", "rZ"], writes=["Aa"])
            P.op("dve", lambda e, nq=nq: e.scalar_tensor_tensor(out=od[:, 0:nq], in0=Aa[:, 1, 0:nq], scalar=neglam[:, 0:1], in1=Aa[:, 0, 0:nq],
                                                                 op0=ALU.mult, op1=ALU.add), reads=["Aa", "neglam"], writes=["od"])
            P.op("dve", lambda e, nq=nq: e.tensor_tensor(out=sq[:, 0:nq], in0=od[:, 0:nq], in1=od[:, 0:nq], op=ALU.mult), reads=["od"], writes=["sq"])

            def post2(nq=nq, qc0=qc0, ob=ob, otag=otag):
                P.op("pe", lambda e: e.matmul(pb[6][:, 0:nq], onesf, sq[:, 0:nq], start=True, stop=True), reads=["cst", "sq"], writes=["pb6"])
                P.op("act", lambda e: e.activation(out=rs[:, 0:nq], in_=pb[6][:, 0:nq], func=AF.Ln, bias=epsr[:, 0:1], scale=1.0 / 128),
                     reads=["pb6", "epsr"], writes=["rs"])
                P.op("act", lambda e: e.activation(out=rs[:, 0:nq], in_=rs[:, 0:nq], func=AF.Exp, scale=-0.5), reads=["rs"], writes=["rs"])
                P.op("dve", lambda e: e.scalar_tensor_tensor(out=ob[:, qc0:qc0 + nq], in0=od[:, 0:nq], scalar=nws[:, 0:1], in1=rs[:, 0:nq],
                                                             op0=ALU.mult, op1=ALU.mult), reads=["od", "nws", "rs"], writes=[otag])
            pending[0] = post2
        if pending[0] is not None:
            pending[0](); pending[0] = None
        P.op("sp", lambda e, ob=ob, h=h: e.dma_start(out=oscr_d[h], in_=ob[:, :]), reads=[otag], writes=[f"oscr{h}"], dma=f"d_oscr{par}", out=True)


def emit_gdn(nc, P, st, sbt, pb, pbh, xT, XT, cst, sm, wdq_d, wdk_d, wdv_d, wdz_d, wba_d, cw_d, oscr_d, heads, GS=2):
    ident = cst[:, 0, :]; U = cst[:, 2, :]; Bm = cst[:, 3, :]; NM1 = cst[:, 4, :]; NM2 = cst[:, 5, :]; onesf = cst[:, 6, :]
    ttiles = [(0, 16)] + [(16 + 128 * j, 128) for j in range(16)]
    rot = {"b": 0, "u": 0}

    def nb():
        i = rot["b"] % 3; rot["b"] += 1
        return i, pb[i], f"pb{i}"

    def uid(p):
        rot["u"] += 1
        return f"{p}{rot['u']}"

    def evac(eng, out_ap, in_ap, reads, writes):
        if eng == "act":
            P.op("act", lambda e: e.copy(out=out_ap, in_=in_ap), reads=reads, writes=writes)
        else:
            P.op("dve", lambda e: e.tensor_copy(out=out_ap, in_=in_ap), reads=reads, writes=writes)

    wba = sbt(st, "wba", [128, 16, 16], BF16)
    P.op("pool", lambda e: e.dma_start(out=wba[:], in_=wba_d), writes=["wba"], dma="d_wba")
    cw = sbt(st, "cw", [128, 3, 8, 4])
    P.op("sp", lambda e: e.dma_start(out=cw[:], in_=cw_d), writes=["cw"], dma="d_cw")
    ba = sbt(st, "ba", [128, 17, 16]); beta = sbt(st, "beta", [128, 17, 8]); gg = sbt(st, "gg", [128, 17, 8])
    gcl = sbt(st, "gcl", [128, 17, 16]); bg = sbt(st, "bg", [128, 17, 8]); kdsc = sbt(st, "kdsc", [128, 17, 8])
    nea = sbt(st, "nea", [128, 8]); tA = sbt(st, "tA", [128, 17]); tB = sbt(st, "tB", [128, 17]); tC = sbt(st, "tC", [128, 17, 8])
    epsr = sbt(st, "epsg", [128, 1])
    P.op("dve", lambda e: e.memset(epsr[:], RMS_EPS), writes=["epsg"])
    P.op("dve", lambda e: e.memset(ba[:], 0.0), writes=["ba"])
    P.op("dve", lambda e: e.memset(gcl[:], 0.0), writes=["gcl"])
    for tt, (c0, n) in enumerate(ttiles):
        bi, bank, btag = nb()
        for c in range(16):
            P.op("pe", lambda e, bank=bank, c=c, c0=c0, n=n: e.matmul(bank[0:n, 0:16], xT[:, c, c0:c0 + n], wba[:, c, :],
                                                                     start=(c == 0), stop=(c == 15)), reads=[XT[c], "wba"], writes=[btag])
        evac("act", ba[0:n, tt, :], bank[0:n, 0:16], [btag], ["ba"])
    P.op("act", lambda e: e.activation(out=beta[:, :, :], in_=ba[:, :, 0:8], func=AF.Sigmoid), reads=["ba"], writes=["beta"])
    P.op("act", lambda e: e.activation(out=nea[:, :], in_=sm[:, 256:264], func=AF.Exp), reads=["sm"], writes=["nea"])
    P.op("dve", lambda e: e.tensor_scalar(out=nea[:, :], in0=nea[:, :], scalar1=-1.0, scalar2=None, op0=ALU.mult), reads=["nea"], writes=["nea"])
    for h in range(8):
        P.op("dve", lambda e, h=h: e.tensor_scalar(out=tA[:, :], in0=ba[:, :, 8 + h], scalar1=sm[:, 264 + h:265 + h], scalar2=None, op0=ALU.add),
             reads=["ba", "sm"], writes=["tA"])
        P.op("act", lambda e: e.activation(out=tB[:, :], in_=tA[:, :], func=AF.Abs), reads=["tA"], writes=["tB"])
        P.op("act", lambda e: e.activation(out=tB[:, :], in_=tB[:, :], func=AF.Exp, scale=-1.0), reads=["tB"], writes=["tB"])
        P.op("act", lambda e: e.activation(out=tB[:, :], in_=tB[:, :], func=AF.Ln, bias=1.0), reads=["tB"], writes=["tB"])
        P.op("dve", lambda e: e.scalar_tensor_tensor(out=tA[:, :], in0=tA[:, :], scalar=0.0, in1=tB[:, :], op0=ALU.max, op1=ALU.add),
             reads=["tA", "tB"], writes=["tA"])
        P.op("dve", lambda e, h=h: e.tensor_scalar(out=gg[:, :, h], in0=tA[:, :], scalar1=nea[:, h:h + 1], scalar2=None, op0=ALU.mult),
             reads=["tA", "nea"], writes=["gg"])
    P.op("dve", lambda e: e.tensor_scalar(out=gg[:, 0, :], in0=gg[:, 0, :], scalar1=U[:, 15:16], scalar2=None, op0=ALU.mult), reads=["gg", "cst"], writes=["gg"])
    for tt, (c0, n) in enumerate(ttiles):
        bi, bank, btag = nb()
        P.op("pe", lambda e, bank=bank, n=n, tt=tt: e.matmul(bank[0:n, 0:8], U[0:n, 0:n], gg[0:n, tt, :], start=True, stop=True),
             reads=["cst", "gg"], writes=[btag])
        P.op("pe", lambda e, bank=bank, n=n, tt=tt: e.matmul(bank[0:n, 8:16], Bm[0:n, 0:n], gg[0:n, tt, :], start=True, stop=True),
             reads=["cst", "gg"], writes=[btag])
        evac("act", gcl[0:n, tt, :], bank[0:n, 0:16], [btag], ["gcl"])
    P.op("act", lambda e: e.activation(out=tC[:, :, :], in_=gcl[:, :, 0:8], func=AF.Exp), reads=["gcl"], writes=["tC"])
    P.op("dve", lambda e: e.tensor_tensor(out=bg[:, :, :], in0=beta[:, :, :], in1=tC[:, :, :], op=ALU.mult), reads=["beta", "tC"], writes=["bg"])
    P.op("dve", lambda e: e.tensor_tensor(out=tC[:, :, :], in0=gcl[:, :, 8:16], in1=gcl[:, :, 0:8], op=ALU.subtract), reads=["gcl", "tC"], writes=["tC"])
    P.op("act", lambda e: e.activation(out=kdsc[:, :, :], in_=tC[:, :, :], func=AF.Exp), reads=["tC"], writes=["kdsc"])

    def mk(name, shape, dt=F32):
        return [sbt(st, f"{name}_{i}", shape, dt) for i in range(GS)]

    wts = {k: [sbt(st, f"g{k}_{i}", [128, 16, 128], BF16) for i in range(GS)] for k in ("q", "k", "v", "z")}
    pre = {k: mk("pre" + k, [128, 515]) for k in ("q", "k", "v")}
    cvo = {k: [mk(f"cvo{k}{p}", [128, 512]) for p in range(2)] for k in ("q", "k", "v")}
    cacc = mk("gcacc", [128, 512]); rn = mk("grn", [128, 512])
    S = mk("S", [128, 128])
    gb = mk("gb", [128, 128]); xx = mk("xx", [128, 128]); a1 = mk("a1", [128, 128]); a2 = mk("a2", [128, 128])
    Dst = mk("Dst", [128, 128]); DTt = mk("DTt", [128, 128]); egrow = [mk(f"egrow{p}", [128, 128]) for p in range(2)]
    Mm = mk("Mm", [128, 128]); Nm = mk("Nm", [128, 128])
    YQ = [mk(f"YQ{j}", [128, 256]) for j in range(2)]
    QT = [mk(f"QT{j}", [128, 128]) for j in range(2)]
    kbg = mk("kbg", [128, 128]); kdec = [mk(f"kdec{p}", [128, 128], BF16) for p in range(2)]; vb = mk("vb", [128, 128]); uu = [mk(f"uu{p}", [128, 128]) for p in range(2)]
    wT = [mk(f"wT{p}", [128, 128], BF16) for p in range(2)]; qkT = [mk(f"qkT{p}", [128, 128], BF16) for p in range(2)]
    qdT = [mk(f"qdT{p}", [128, 128], BF16) for p in range(2)]; szT = [mk(f"szT{p}", [128, 512]) for p in range(2)]
    vnew = mk("vnew", [128, 128], BF16); odn = mk("odn", [128, 128]); t3 = mk("t3", [128, 128]); Sb = mk("Sb", [128, 128], BF16)
    st6 = mk("st6", [128, 6]); mvv = mk("mvv", [128, 2]); msq = mk("msq", [128, 1])
    oTs = [mk(f"oTs{j}", [128, 128], BF16) for j in range(2)]
    wsrc = {"q": wdq_d, "k": wdk_d, "v": wdv_d, "z": wdz_d}

    groups = [heads[i:i + GS] for i in range(0, len(heads), GS)]

    def load_weights(gi):
        for i, h in enumerate(groups[gi]):
            for k in ("q", "k", "v", "z"):
                P.op("pool", lambda e, i=i, h=h, k=k: e.dma_start(out=wts[k][i][:], in_=wsrc[k][h]),
                     writes=[f"gw{k}{i}"], dma=f"d_gw{k}{i}")

    import os as _os
    GL = int(_os.environ.get("GDN_LVL", "9"))
    if GL < 1:
        return
    blocks = [(0, 16, [0])] + [(16 + 512 * j, 512, [1 + 4 * j + t for t in range(4)]) for j in range(4)]
    import itertools as _it

    def blk_chain(i, h, bidx):
        c0, n, tiles_ = blocks[bidx]; bp = bidx % 2
        for pi, k in enumerate(("q", "k", "v")):
            w = wts[k][i]; wtag = f"gw{k}{i}"
            pr = pre[k][i]; ptag = f"pre{k}{i}"; co = cvo[k][bp][i]; ctag = f"cvo{k}{bp}{i}"
            nprev = blocks[bidx - 1][1] if bidx > 0 else 0
            bi, bank, btag = nb()
            for c in range(16):
                P.op("pe", lambda e, c=c: e.matmul(bank[:, 0:n], w[:, c, :], xT[:, c, c0:c0 + n], start=(c == 0), stop=(c == 15)), reads=[XT[c], wtag], writes=[btag])
            if bidx == 0:
                P.op("pool", lambda e: e.memset(pr[:, 0:3], 0.0), writes=[ptag])
            else:
                P.op("pool", lambda e: e.tensor_copy(out=pr[:, 0:3], in_=pr[:, nprev:nprev + 3]), reads=[ptag], writes=[ptag])
            evac("act", pr[:, 3:3 + n], bank[:, 0:n], [btag], [ptag])
            yield
            ca = cacc[i]; catag = f"gcacc{i}"
            P.op("dve", lambda e: e.tensor_scalar(out=ca[:, 0:n], in0=pr[:, 3:3 + n], scalar1=cw[:, pi, h, 3:4], scalar2=None, op0=ALU.mult), reads=[ptag, "cw"], writes=[catag])
            yield
            for j in range(3):
                P.op("dve", lambda e, j=j: e.scalar_tensor_tensor(out=ca[:, 0:n], in0=pr[:, j:j + n], scalar=cw[:, pi, h, j:j + 1], in1=ca[:, 0:n], op0=ALU.mult, op1=ALU.add),
                     reads=[ptag, "cw", catag], writes=[catag])
                yield
            P.op("act", lambda e: e.activation(out=co[:, 0:n], in_=ca[:, 0:n], func=AF.Silu), reads=[catag], writes=[ctag])
            yield
            if k in ("q", "k"):
                r_ = rn[i]
                P.op("dve", lambda e: e.tensor_tensor(out=ca[:, 0:n], in0=co[:, 0:n], in1=co[:, 0:n], op=ALU.mult), reads=[ctag], writes=[catag])
                yield
                b2, bank2, btag2 = nb()
                P.op("pe", lambda e: e.matmul(bank2[:, 0:n], onesf, ca[:, 0:n], start=True, stop=True), reads=["cst", catag], writes=[btag2])
                P.op("act", lambda e: e.activation(out=r_[:, 0:n], in_=bank2[:, 0:n], func=AF.Ln, bias=epsr[:, 0:1]), reads=[btag2, "epsg"], writes=[f"grn{i}"])
                P.op("act", lambda e: e.activation(out=r_[:, 0:n], in_=r_[:, 0:n], func=AF.Exp, scale=-0.5), reads=[f"grn{i}"], writes=[f"grn{i}"])
                yield
                if k == "k":
                    P.op("dve", lambda e: e.tensor_tensor(out=co[:, 0:n], in0=co[:, 0:n], in1=r_[:, 0:n], op=ALU.mult), reads=[ctag, f"grn{i}"], writes=[ctag])
                else:
                    P.op("dve", lambda e: e.scalar_tensor_tensor(out=co[:, 0:n], in0=co[:, 0:n], scalar=128.0 ** -0.5, in1=r_[:, 0:n], op0=ALU.mult, op1=ALU.mult),
                         reads=[ctag, f"grn{i}"], writes=[ctag])
                yield
        wz = wts["z"][i]
        bz, bankz, btz = nb()
        for c in range(16):
            P.op("pe", lambda e, c=c: e.matmul(bankz[:, 0:n], wz[:, c, :], xT[:, c, c0:c0 + n], start=(c == 0), stop=(c == 15)), reads=[XT[c], f"gwz{i}"], writes=[btz])
        P.op("act", lambda e: e.activation(out=szT[bp][i][:, 0:n], in_=bankz[:, 0:n], func=AF.Silu), reads=[btz], writes=[f"szT{bp}{i}"])
        yield

    def run(fg, bg=()):
        fg = list(fg)
        while fg:
            nxt = []
            for g in fg:
                try:
                    next(g); nxt.append(g)
                except StopIteration:
                    pass
            fg = nxt
            for g in list(bg):
                try:
                    next(g)
                except StopIteration:
                    bg.remove(g)

    for gi, hg in enumerate(groups):
        load_weights(gi)
        for i, h in enumerate(hg):
            P.op("dve", lambda e, i=i: e.memset(S[i][:], 0.0), writes=[f"S{i}"])
            P.op("dve", lambda e, i=i: e.memset(Sb[i][:], 0.0), writes=[f"Sb{i}"])
        run([blk_chain(i, h, 0) for i, h in enumerate(hg)])
        for bidx, (c0, n, tiles) in enumerate(blocks):
            bp = bidx % 2
            bgen = [blk_chain(i, h, bidx + 1) for i, h in enumerate(hg)] if bidx + 1 < len(blocks) else []

            import itertools as _it

            def tile_params(tl):
                tt = tiles[tl]; lo = 128 * tl; nt = 16 if tt == 0 else 128
                return tt, lo, nt, c0 + lo, ([(0, 16)] if tt == 0 else [(0, 64), (64, 64)]), tl % 2

            def pre_chain(i, h, tl):
                tt, lo, nt, tc0, chunks, pp = tile_params(tl)
                kc = cvo["k"][bp][i][:, lo:lo + nt]; qc = cvo["q"][bp][i][:, lo:lo + nt]; vc = cvo["v"][bp][i][:, lo:lo + nt]
                KT = f"cvok{bp}{i}"; QTg = f"cvoq{bp}{i}"; VT = f"cvov{bp}{i}"
                eg = egrow[pp][i]; egt = f"egrow{pp}{i}"
                P.op("dve", lambda e: e.tensor_scalar(out=gb[i][0:nt, :], in0=onesf[0:nt, :], scalar1=gg[0:nt, tt, h:h + 1], scalar2=None, op0=ALU.mult),
                     reads=["cst", "gg"], writes=[f"gb{i}"])
                b1, bk1, bt1 = nb()
                P.op("pe", lambda e: e.matmul(bk1[:, 0:nt], gb[i][0:nt, :], U[0:nt, 0:nt], start=True, stop=True), reads=[f"gb{i}", "cst"], writes=[bt1])
                P.op("dve", lambda e: e.tensor_scalar(out=xx[i][0:nt, 0:nt], in0=bk1[0:nt, 0:nt], scalar1=gcl[0:nt, tt, h:h + 1], scalar2=None, op0=ALU.subtract),
                     reads=[bt1, "gcl"], writes=[f"xx{i}"])
                P.op("act", lambda e: e.activation(out=eg[:, 0:nt], in_=bk1[:, 0:nt], func=AF.Exp), reads=[bt1, f"xx{i}"], writes=[egt])
                yield
                P.op("pool", lambda e: e.tensor_tensor(out=a1[i][0:nt, 0:nt], in0=xx[i][0:nt, 0:nt], in1=NM1[0:nt, 0:nt], op=ALU.add), reads=[f"xx{i}", "cst"], writes=[f"a1{i}"])
                P.op("pool", lambda e: e.tensor_tensor(out=a2[i][0:nt, 0:nt], in0=xx[i][0:nt, 0:nt], in1=NM2[0:nt, 0:nt], op=ALU.subtract), reads=[f"xx{i}", "cst"], writes=[f"a2{i}"])
                P.op("act", lambda e: e.activation(out=Dst[i][0:nt, 0:nt], in_=a1[i][0:nt, 0:nt], func=AF.Exp, scale=-1.0), reads=[f"a1{i}"], writes=[f"Dst{i}"])
                P.op("act", lambda e: e.activation(out=DTt[i][0:nt, 0:nt], in_=a2[i][0:nt, 0:nt], func=AF.Exp), reads=[f"a2{i}"], writes=[f"DTt{i}"])
                yield
                b2, bk2, bt2 = nb()
                P.op("pe", lambda e: e.matmul(bk2[0:nt, 0:nt], kc, kc, start=True, stop=True), reads=[KT], writes=[bt2])
                P.op("dve", lambda e: e.scalar_tensor_tensor(out=Mm[i][0:nt, 0:nt], in0=bk2[0:nt, 0:nt], scalar=beta[0:nt, tt, h:h + 1], in1=Dst[i][0:nt, 0:nt],
                                                             op0=ALU.mult, op1=ALU.mult), reads=[bt2, "beta", f"Dst{i}"], writes=[f"Mm{i}"])
                yield
                b3, bk3, bt3 = nb()
                P.op("pe", lambda e: e.transpose(bk3[0:nt, 0:nt], Mm[i][0:nt, 0:nt], ident[0:nt, 0:nt]), reads=[f"Mm{i}", "cst"], writes=[bt3])
                P.op("dve", lambda e: e.tensor_tensor(out=YQ[0][i][0:nt, 0:nt], in0=ident[0:nt, 0:nt], in1=bk3[0:nt, 0:nt], op=ALU.subtract),
                     reads=[bt3, "cst"], writes=[f"YQ0{i}"])
                evac("act", Nm[i][0:nt, 0:nt], bk3[0:nt, 0:nt], [bt3, f"YQ0{i}"], [f"Nm{i}"])
                yield
                b4, bk4, bt4 = nb()
                P.op("pe", lambda e: e.matmul(bk4[0:nt, 0:nt], Mm[i][0:nt, 0:nt], Nm[i][0:nt, 0:nt], start=True, stop=True), reads=[f"Mm{i}", f"Nm{i}"], writes=[bt4])
                evac("act", YQ[0][i][0:nt, nt:2 * nt], bk4[0:nt, 0:nt], [bt4], [f"YQ0{i}"])
                b5, bk5, bt5 = nb()
                P.op("pe", lambda e: e.matmul(bk5[0:nt, 0:nt], Nm[i][0:nt, 0:nt], Mm[i][0:nt, 0:nt], start=True, stop=True), reads=[f"Mm{i}", f"Nm{i}"], writes=[bt5])
                evac("dve", QT[0][i][0:nt, 0:nt], bk5[0:nt, 0:nt], [bt5], [f"QT0{i}"])
                yield
                for r in range(5):
                    cur = r % 2; nx = 1 - cur
                    b6, bk6, bt6 = nb()
                    ncol = 2 * nt if r < 4 else nt
                    P.op("pe", lambda e: e.matmul(bk6[0:nt, 0:ncol], QT[cur][i][0:nt, 0:nt], YQ[cur][i][0:nt, 0:ncol], start=True, stop=True),
                         reads=[f"QT{cur}{i}", f"YQ{cur}{i}"], writes=[bt6])
                    P.op("dve", lambda e: e.tensor_tensor(out=YQ[nx][i][0:nt, 0:nt], in0=YQ[cur][i][0:nt, 0:nt], in1=bk6[0:nt, 0:nt], op=ALU.add),
                         reads=[bt6, f"YQ{cur}{i}"], writes=[f"YQ{nx}{i}"])
                    if r < 4:
                        evac("act", YQ[nx][i][0:nt, nt:2 * nt], bk6[0:nt, nt:2 * nt], [bt6], [f"YQ{nx}{i}"])
                        b7, bk7, bt7 = nb()
                        P.op("pe", lambda e: e.matmul(bk7[0:nt, 0:nt], YQ[cur][i][0:nt, nt:2 * nt], QT[cur][i][0:nt, 0:nt], start=True, stop=True),
                             reads=[f"QT{cur}{i}", f"YQ{cur}{i}"], writes=[bt7])
                        evac("act", QT[nx][i][0:nt, 0:nt], bk7[0:nt, 0:nt], [bt7], [f"QT{nx}{i}"])
                    yield
                Tt = YQ[1][i][0:nt, 0:nt]; TtT = f"YQ1{i}"
                b8, bk8, bt8 = nb()
                P.op("pe", lambda e: e.transpose(bk8[0:nt, 0:128], kc, ident), reads=[KT, "cst"], writes=[bt8])
                P.op("dve", lambda e: e.tensor_scalar(out=kbg[i][0:nt, :], in0=bk8[0:nt, 0:128], scalar1=bg[0:nt, tt, h:h + 1], scalar2=None, op0=ALU.mult),
                     reads=[bt8, "bg"], writes=[f"kbg{i}"])
                P.op("act", lambda e: e.activation(out=kdec[pp][i][0:nt, :], in_=bk8[0:nt, 0:128], func=AF.Copy, scale=kdsc[0:nt, tt, h:h + 1]),
                     reads=[bt8, "kdsc", f"kbg{i}"], writes=[f"kdec{pp}{i}"])
                b9, bk9, bt9 = nb()
                P.op("pe", lambda e: e.transpose(bk9[0:nt, 0:128], vc, ident), reads=[VT, "cst"], writes=[bt9])
                P.op("dve", lambda e: e.tensor_scalar(out=vb[i][0:nt, :], in0=bk9[0:nt, 0:128], scalar1=beta[0:nt, tt, h:h + 1], scalar2=None, op0=ALU.mult),
                     reads=[bt9, "beta"], writes=[f"vb{i}"])
                yield
                b10, bk10, bt10 = nb()
                P.op("pe", lambda e: e.matmul(bk10[0:nt, 0:128], Tt, vb[i][0:nt, :], start=True, stop=True), reads=[TtT, f"vb{i}"], writes=[bt10])
                evac("act", uu[pp][i][0:nt, :], bk10[0:nt, 0:128], [bt10], [f"uu{pp}{i}"])
                b11, bk11, bt11 = nb()
                P.op("pe", lambda e: e.matmul(bk11[:, 0:nt], kbg[i][0:nt, :], Tt, start=True, stop=True), reads=[TtT, f"kbg{i}"], writes=[bt11])
                evac("act", wT[pp][i][:, 0:nt], bk11[:, 0:nt], [bt11], [f"wT{pp}{i}"])
                yield
                b12, bk12, bt12 = nb()
                P.op("pe", lambda e: e.matmul(bk12[0:nt, 0:nt], kc, qc, start=True, stop=True), reads=[KT, QTg], writes=[bt12])
                P.op("dve", lambda e: e.tensor_tensor(out=qkT[pp][i][0:nt, 0:nt], in0=bk12[0:nt, 0:nt], in1=DTt[i][0:nt, 0:nt], op=ALU.mult),
                     reads=[bt12, f"DTt{i}"], writes=[f"qkT{pp}{i}"])
                P.op("pool", lambda e: e.tensor_tensor(out=qdT[pp][i][:, 0:nt], in0=qc, in1=eg[:, 0:nt], op=ALU.mult), reads=[QTg, egt], writes=[f"qdT{pp}{i}"])
                yield

            def rec_full(i, h, tl):
                tt, lo, nt, tc0, chunks, pp = tile_params(tl)
                eg = egrow[pp][i]; egt = f"egrow{pp}{i}"
                for (r0, L) in chunks:
                    bA, bkA, btA = 3 + 2 * i, pb[3 + 2 * i], f"pb{3 + 2 * i}"
                    P.op("pe", lambda e: e.matmul(bkA[r0:r0 + L, 0:128], wT[pp][i][:, r0:r0 + L], Sb[i][:, :], start=True, stop=True),
                         reads=[f"wT{pp}{i}", f"Sb{i}"], writes=[btA])
                    bB, bkB, btB = 4 + 2 * i, pb[4 + 2 * i], f"pb{4 + 2 * i}"
                    P.op("pe", lambda e: e.matmul(bkB[r0:r0 + L, 0:128], qdT[pp][i][:, r0:r0 + L], Sb[i][:, :], start=True, stop=False),
                         reads=[f"qdT{pp}{i}", f"Sb{i}"], writes=[btB])
                    yield
                    P.op("dve", lambda e: e.tensor_tensor(out=vnew[i][r0:r0 + L, :], in0=uu[pp][i][r0:r0 + L, :], in1=bkA[r0:r0 + L, 0:128], op=ALU.subtract),
                         reads=[btA, f"uu{pp}{i}"], writes=[f"vnew{i}"])
                    yield
                    P.op("pe", lambda e: e.matmul(bkB[r0:r0 + L, 0:128], qkT[pp][i][r0:r0 + L, r0:r0 + L], vnew[i][r0:r0 + L, :], start=False, stop=True),
                         reads=[f"qkT{pp}{i}", f"vnew{i}"], writes=[btB])
                    bC, bkC, btC = bA, bkA, btA
                    P.op("pe", lambda e: e.matmul(bkC[:, 0:128], kdec[pp][i][r0:r0 + L, :], vnew[i][r0:r0 + L, :], start=True, stop=True),
                         reads=[f"kdec{pp}{i}", f"vnew{i}"], writes=[btC])
                    yield
                    P.op("dve", lambda e: e.scalar_tensor_tensor(out=S[i][:, :], in0=S[i][:, :], scalar=eg[:, r0 + L - 1:r0 + L], in1=bkC[:, 0:128],
                                                                 op0=ALU.mult, op1=ALU.add), reads=[btC, f"S{i}", egt], writes=[f"S{i}"])
                    evac("act", odn[i][r0:r0 + L, :], bkB[r0:r0 + L, 0:128], [btB], [f"odn{i}"])
                    yield
                    P.op("act", lambda e: e.copy(out=Sb[i][:, :], in_=S[i][:, :]), reads=[f"S{i}"], writes=[f"Sb{i}"])
                    yield
                P.op("dve", lambda e: e.bn_stats(out=st6[i][0:nt, :], in_=odn[i][0:nt, :]), reads=[f"odn{i}"], writes=[f"st6{i}"])
                P.op("dve", lambda e: e.bn_aggr(out=mvv[i][0:nt, :], in_=st6[i][0:nt, :]), reads=[f"st6{i}"], writes=[f"mvv{i}"])
                P.op("dve", lambda e: e.scalar_tensor_tensor(out=msq[i][0:nt, :], in0=mvv[i][0:nt, 0:1], scalar=mvv[i][0:nt, 0:1], in1=mvv[i][0:nt, 1:2], op0=ALU.mult, op1=ALU.add),
                     reads=[f"mvv{i}"], writes=[f"msq{i}"])
                yield
                P.op("act", lambda e: e.activation(out=msq[i][0:nt, :], in_=msq[i][0:nt, :], func=AF.Ln, bias=epsr[0:nt, 0:1]), reads=[f"msq{i}", "epsg"], writes=[f"msq{i}"])
                P.op("act", lambda e: e.activation(out=msq[i][0:nt, :], in_=msq[i][0:nt, :], func=AF.Exp, scale=-0.5), reads=[f"msq{i}"], writes=[f"msq{i}"])
                yield
                P.op("dve", lambda e: e.scalar_tensor_tensor(out=t3[i][0:nt, :], in0=odn[i][0:nt, :], scalar=msq[i][0:nt, 0:1], in1=sm[0:nt, 273:401], op0=ALU.mult, op1=ALU.mult),
                     reads=[f"odn{i}", f"msq{i}", "sm"], writes=[f"t3{i}"])
                yield
                bD, bkD, btD = 3 + 2 * i, pb[3 + 2 * i], f"pb{3 + 2 * i}"
                P.op("pe", lambda e: e.transpose(bkD[:, 0:nt], t3[i][0:nt, :], ident[0:nt, 0:nt]), reads=[f"t3{i}", "cst"], writes=[btD])
                pj = tt % 2
                P.op("dve", lambda e: e.tensor_tensor(out=oTs[pj][i][:, 0:nt], in0=bkD[:, 0:nt], in1=szT[bp][i][:, lo:lo + nt], op=ALU.mult),
                     reads=[btD, f"szT{bp}{i}"], writes=[f"oTs{pj}{i}"])
                P.op("sp", lambda e: e.dma_start(out=oscr_d[8 + h, :, tc0:tc0 + nt], in_=oTs[pj][i][:, 0:nt]),
                     reads=[f"oTs{pj}{i}"], writes=[f"oscr{8 + h}"], dma=f"d_oscd{pj}{i}", out=True)
                yield

            ntl = len(tiles)
            run([pre_chain(i, h, 0) for i, h in enumerate(hg)], bgen)
            for tl in range(ntl):
                gens = [rec_full(i, h, tl) for i, h in enumerate(hg)]
                if tl + 1 < ntl:
                    gens += [pre_chain(i, h, tl + 1) for i, h in enumerate(hg)]
                run(gens, bgen)
            run(bgen)


def _tile_w(w):
    k, n = w.shape
    return np.ascontiguousarray(w.reshape(k // 128, 128, n // 128, 128).transpose(2, 1, 0, 3))


def _constants():
    p = np.arange(128)
    same = (p[:, None] // 64) == (p[None, :] // 64)
    cst = np.zeros((128, 8, 128), np.float32)
    cst[:, 0, :] = np.eye(128)
    sw = np.zeros((128, 128), np.float32)
    for m in range(128):
        k = (m // 64) * 64 + ((m % 64) + 32) % 64
        sw[k, m] = 1.0
    cst[:, 1, :] = sw
    cst[:, 2, :] = (same & (p[:, None] <= p[None, :])).astype(np.float32)
    cst[:, 3, :] = same.astype(np.float32)
    cst[:, 4, :] = np.where(same & (p[:, None] > p[None, :]), 0.0, BIG)
    cst[:, 5, :] = np.where(same & (p[None, :] >= p[:, None]), 0.0, BIG)
    cst[:, 6, :] = 1.0
    pos = np.arange(T, dtype=np.float32)
    inv_freq = (10000.0 ** (-np.arange(0, 64, 2, dtype=np.float32) / 64)).astype(np.float32)
    ang = pos[None, :] * inv_freq[:, None]
    d = p % 64
    cos = np.cos(ang)[d % 32]
    sin = np.sin(ang)[d % 32] * np.where(d < 32, -1.0, 1.0)[:, None]
    rope = np.stack([cos, sin]).astype(np.float32)
    return cst, rope


def _host_inputs(inp):
    f = lambda a: np.ascontiguousarray(np.asarray(a, dtype=np.float32))
    w_in = f(inp["w_in"][0])
    cst, rope = _constants()
    shared = {
        "cst": cst, "rope": rope,
        "waq": _tile_w(w_in[:, 0:1024]), "wak": _tile_w(w_in[:, 1024:2048]),
        "wav": np.ascontiguousarray(w_in[:, 2048:3072].reshape(16, 128, 2, 512).transpose(2, 1, 0, 3)),
        "wdq": _tile_w(w_in[:, 3072:4096]), "wdk": _tile_w(w_in[:, 4096:5120]),
        "wdv": _tile_w(w_in[:, 5120:6144]), "wdz": _tile_w(w_in[:, 6144:7168]),
        "wba": np.ascontiguousarray(w_in[:, 7168:7184].reshape(16, 128, 16).transpose(1, 0, 2)),
        "cw": np.ascontiguousarray(f(inp["conv_qkv_w"][0]).T.reshape(3, 8, 128, 4).transpose(2, 0, 1, 3)),
        "wout": np.ascontiguousarray(f(inp["w_out"][0]).reshape(16, 128, D).transpose(1, 0, 2)),
        "lnp": np.ascontiguousarray(np.stack([np.broadcast_to(f(inp[k][0])[None, :], (128, D))
                                             for k in ("ln1_g", "ln1_b", "ln2_g", "ln2_b")])),
        "wg": _tile_w(f(inp["ffn_w_gate"][0])), "wu": _tile_w(f(inp["ffn_w_up"][0])),
        "wd": np.ascontiguousarray(f(inp["ffn_w_down"][0]).reshape(NFF, 128, 16, 128).transpose(2, 1, 0, 3)),
    }
    fc = np.concatenate([f(inp["ffn_conv_w"][0]), f(inp["ffn_conv_b"][0])[None, :]], axis=0)
    shared["fcw"] = np.ascontiguousarray(fc.T.reshape(NFF, 128, 4).transpose(1, 0, 2))
    sm = np.zeros((128, 401), np.float32)
    for i, k in enumerate(("lambda_q1", "lambda_k1", "lambda_q2", "lambda_k2")):
        sm[:, i * 64:(i + 1) * 64] = f(inp[k][0])[None, :]
    sm[:, 256:264] = f(inp["a_log"][0])[None, :]
    sm[:, 264:272] = f(inp["dt_bias"][0])[None, :]
    sm[:, 272] = f(inp["diff_norm_w"][0])
    sm[:, 273:401] = f(inp["delta_norm_w"][0])[None, :]
    shared["sm"] = sm
    x = f(inp["x"]); meta = f(inp["meta_tokens"])
    per_core = []
    for c in range(8):
        b, hf = c // 2, c % 2
        h = np.concatenate([meta, x[b]], axis=0)
        r0 = 14 + 1024 * hf
        per_core.append({"hT": np.ascontiguousarray(h.T), "hrow": np.ascontiguousarray(h[r0:r0 + NTOK])})
    return shared, per_core


_NC_CACHE = {}


def kernel(**inputs):
    shared, per_core = _host_inputs(inputs)
    if "nc" not in _NC_CACHE:
        _NC_CACHE["nc"] = build()
    nc = _NC_CACHE["nc"]
    in_maps = [{**shared, **pc} for pc in per_core]
    res = run_bass_kernel_spmd(nc, in_maps, core_ids=list(range(8)))
    out = np.empty((4, 2048, D), np.float32)
    for c in range(8):
        b, hf = c // 2, c % 2
        out[b, 1024 * hf:1024 * (hf + 1)] = res.results[c]["out"]
    return out
```

```python
import math
import numpy as np
from contextlib import ExitStack
import concourse.bass as bass
import concourse.mybir as mybir
from concourse.bass_utils import run_bass_kernel_spmd

F32 = mybir.dt.float32
BF16 = mybir.dt.bfloat16
AF = mybir.ActivationFunctionType
ALU = mybir.AluOpType

D = 2048
T = 2064
NMETA = 16
DFF = 5632
NFF = 44
NTOK = 1026
ALPHA = 2.0 ** 0.25
LAM_INIT = 0.8 - 0.6 * math.exp(0.0)
LN_EPS = 1e-5
RMS_EPS = 1e-6
BIG = 30000.0


class Prog:
    def __init__(self, nc, stack):
        self.nc = nc
        self.stack = stack
        self.sems = {}
        self.engh = {"pe": nc.tensor, "act": nc.scalar, "dve": nc.vector, "pool": nc.gpsimd, "sp": nc.sync}
        self.cnt = {}
        self.seen = {e: {} for e in self.engh}
        self.last_w = {}
        self.readers = {}
        self.out_tokens = []
        self.nops = 0
        import os as _os
        self.same = _os.environ.get("SAME", "1") == "1"

    def _sem(self, k):
        if k not in self.sems:
            self.sems[k] = self.stack.enter_context(self.nc.semaphore("s_" + str(k)))
        return self.sems[k]

    def op(self, eng, fn, reads=(), writes=(), dma=None, out=False):
        deps = {}
        for r in reads:
            t = self.last_w.get(r)
            if t is not None and deps.get(t[0], 0) < t[1]:
                deps[t[0]] = t[1]
        for w in writes:
            t = self.last_w.get(w)
            if t is not None and deps.get(t[0], 0) < t[1]:
                deps[t[0]] = t[1]
            for k, v in self.readers.get(w, {}).items():
                if deps.get(k, 0) < v:
                    deps[k] = v
        key, amt = (eng, 1) if dma is None else (dma, 16)
        self.cnt[key] = self.cnt.get(key, 0) + amt
        tok = (key, self.cnt[key])
        e = self.engh[eng]
        for k, v in deps.items():
            if k == eng and (eng == "pe" or not self.same):
                continue
            if self.seen[eng].get(k, 0) < v:
                self.seen[eng][k] = v
                e.wait_ge(self._sem(k), v)
        fn(e).then_inc(self._sem(key), amt)
        self.nops += 1
        for r in reads:
            d = self.readers.setdefault(r, {})
            if d.get(key, 0) < tok[1]:
                d[key] = tok[1]
        for w in writes:
            self.last_w[w] = tok
            self.readers[w] = {}
        if out:
            self.out_tokens.append(tok)
        return tok

    def join(self, names):
        toks = [self.last_w[n] for n in names]
        k = toks[0][0]
        assert all(t[0] == k for t in toks)
        m = max(t[1] for t in toks)
        for n in names:
            self.last_w[n] = (k, m)

    def barrier(self):
        for eng, e in self.engh.items():
            for k, v in self.cnt.items():
                if self.seen[eng].get(k, 0) < v:
                    self.seen[eng][k] = v
                    e.wait_ge(self._sem(k), v)

    def finish(self):
        fin = {}
        for k, v in self.out_tokens:
            fin[k] = max(fin.get(k, 0), v)
        for k, v in fin.items():
            self.engh["sp"].wait_ge(self._sem(k), v)


def build(stages=("att", "gdn", "post"), att_heads=range(8), dn_heads=range(8), oscr_input=False, dbg=False, lvl=9):
    nc = bass.Bass("TRN2", target_bir_lowering=False)

    def din(name, shape, dt=F32):
        return nc.dram_tensor(name, list(shape), dt, kind="ExternalInput").ap()

    hT_d = din("hT", [D, T])
    hrow_d = din("hrow", [NTOK, D])
    cst_d = din("cst", [128, 8, 128])
    rope_d = din("rope", [2, 128, T])
    waq_d = din("waq", [8, 128, 16, 128])
    wak_d = din("wak", [8, 128, 16, 128])
    wav_d = din("wav", [2, 128, 16, 512])
    wdq_d = din("wdq", [8, 128, 16, 128])
    wdk_d = din("wdk", [8, 128, 16, 128])
    wdv_d = din("wdv", [8, 128, 16, 128])
    wdz_d = din("wdz", [8, 128, 16, 128])
    wba_d = din("wba", [128, 16, 16])
    cw_d = din("cw", [128, 3, 8, 4])
    sm_d = din("sm", [128, 4 * 64 + 8 + 8 + 1 + 128])
    wout_d = din("wout", [128, 16, D])
    lnp_d = din("lnp", [4, 128, D])
    wg_d = din("wg", [NFF, 128, 16, 128])
    wu_d = din("wu", [NFF, 128, 16, 128])
    wd_d = din("wd", [16, 128, NFF, 128])
    fcw_d = din("fcw", [128, NFF, 4])
    out_d = nc.dram_tensor("out", [1024, D], F32, kind="ExternalOutput").ap()
    if oscr_input:
        oscr_d = din("oscr", [16, 128, T], BF16)
    else:
        oscr_d = nc.dram_tensor("oscr", [16, 128, T], BF16, kind="ExternalOutput" if dbg else "Internal").ap()
    h1s_d = nc.dram_tensor("h1s", [1024, D], F32, kind="ExternalOutput" if dbg else "Internal").ap()

    with ExitStack() as top:
        P = Prog(nc, top)

        def sbt(st, name, shape, dt=F32):
            return st.enter_context(nc.sbuf_tensor("sb_" + name, list(shape), dt))

        pb = [top.enter_context(nc.psum_tensor(f"pb{i}", [128, 512], F32)) for i in range(7)]
        pbh = top.enter_context(nc.psum_tensor("pbh", [128, 1024], BF16))

        cst = sbt(top, "cst", [128, 8, 128])
        P.op("sp", lambda e: e.dma_start(out=cst[:], in_=cst_d), writes=["cst"], dma="d_cst0")
        ident = cst[:, 0, :]
        identb = sbt(top, "identb", [128, 128], BF16)
        pswapb = sbt(top, "pswapb", [128, 128], BF16)
        onesb = sbt(top, "onesb", [128, 128], BF16)
        P.op("dve", lambda e: e.tensor_copy(out=identb[:], in_=cst[:, 0, :]), reads=["cst"], writes=["identb"])
        P.op("dve", lambda e: e.tensor_copy(out=pswapb[:], in_=cst[:, 1, :]), reads=["cst"], writes=["pswapb"])
        P.op("dve", lambda e: e.tensor_copy(out=onesb[:], in_=cst[:, 6, :]), reads=["cst"], writes=["onesb"])
        sm = sbt(top, "sm", [128, 401])
        P.op("sp", lambda e: e.dma_start(out=sm[:], in_=sm_d), writes=["sm"], dma="d_cst1")

        if "att" in stages or "gdn" in stages:
            with ExitStack() as mix:
                xT = sbt(mix, "xT", [128, 16, T], BF16)
                for c in range(16):
                    P.op("pool", lambda e, c=c: e.dma_start(out=xT[:, c, :], in_=hT_d[c * 128:(c + 1) * 128, :]),
                         writes=[f"xT{c}"], dma=f"d_xT{c // 4}")
                XT = [f"xT{c}" for c in range(16)]
                for g4 in range(4):
                    P.join(XT[4 * g4:4 * g4 + 4])
                if "att" in stages:
                    with ExitStack() as st:
                        emit_attention(nc, P, st, sbt, pb, pbh, xT, XT, cst, identb, pswapb, onesb, sm, rope_d,
                                       waq_d, wak_d, wav_d, oscr_d, list(att_heads), lvl=lvl)
                    P.barrier()
                if "gdn" in stages:
                    with ExitStack() as st:
                        emit_gdn(nc, P, st, sbt, pb, pbh, xT, XT, cst, sm, wdq_d, wdk_d, wdv_d, wdz_d, wba_d, cw_d,
                                 oscr_d, list(dn_heads))
                    P.barrier()
            P.barrier()

        if "post" in stages:
            with ExitStack() as st:
                emit_post(nc, P, st, sbt, pb, pbh, cst, identb, oscr_d, hrow_d, wout_d, lnp_d, wg_d, wu_d, wd_d, fcw_d,
                          h1s_d, out_d, oscr_input)
        P.barrier()
        P.finish()
    return nc


class Region:
    def __init__(self, tile, nwords):
        self.t = tile; self.n = nwords; self.off = 0

    def reset(self):
        self.off = 0

    def take(self, nelem, dt=F32):
        words = nelem if dt == F32 else (nelem + 1) // 2
        ap = self.t[:, self.off:self.off + words]
        self.off += words
        assert self.off <= self.n, (self.off, self.n)
        return ap if dt == F32 else ap.bitcast(dt)


def emit_post(nc, P, st, sbt, pb, pbh, cst, identb, oscr_d, hrow_d, wout_d, lnp_d, wg_d, wu_d, wd_d, fcw_d,
              h1s_d, out_d, oscr_input):
    ident = cst[:, 0, :]
    pid = nc.sync.partition_id()
    col0 = (pid % 2) * 1024 + 14
    ttiles = [(0, 2)] + [(2 + 128 * i, 128) for i in range(8)]

    RA = Region(sbt(st, "RA", [128, 22528]), 22528)
    RB = Region(sbt(st, "RB", [128, 16384]), 16384)
    RC = Region(sbt(st, "RC", [128, 8208]), 8208)
    fcw = sbt(st, "fcw", [128, NFF, 4])
    P.op("sp", lambda e: e.dma_start(out=fcw[:], in_=fcw_d), writes=["fcw"], dma="d_fcw")
    stats = sbt(st, "stats", [128, 4, 6]); mv = sbt(st, "mv", [128, 2]); rstd = sbt(st, "rstd", [128, 1])
    eps = sbt(st, "eps", [128, 1])
    P.op("dve", lambda e: e.memset(eps[:], LN_EPS), writes=["eps"])
    h1T = RC.take(16 * NTOK, BF16).rearrange("p (c t) -> p c t", c=16)

    def layer_norm_tile(y, n, gsb, bsb, tag):
        yv = y[0:n, :].rearrange("p (c f) -> p c f", c=4)
        for c in range(4):
            P.op("dve", lambda e, c=c: e.bn_stats(out=stats[0:n, c, :], in_=yv[:, c, :]), reads=[tag], writes=["stats"])
        P.op("dve", lambda e: e.bn_aggr(out=mv[0:n, :], in_=stats[0:n, :, :].rearrange("p c s -> p (c s)")),
             reads=["stats"], writes=["mv"])
        P.op("act", lambda e: e.activation(out=rstd[0:n, :], in_=mv[0:n, 1:2], func=AF.Sqrt, bias=eps[0:n, :]),
             reads=["mv", "eps"], writes=["rstd"])
        P.op("dve", lambda e: e.reciprocal(out=rstd[0:n, :], in_=rstd[0:n, :]), reads=["rstd"], writes=["rstd"])
        P.op("dve", lambda e: e.tensor_scalar(out=y[0:n, :], in0=y[0:n, :], scalar1=mv[0:n, 0:1], scalar2=rstd[0:n, 0:1],
                                              op0=ALU.subtract, op1=ALU.mult), reads=[tag, "mv", "rstd"], writes=[tag])
        P.op("pool", lambda e: e.tensor_tensor(out=y[0:n, :], in0=y[0:n, :], in1=gsb[0:n, :], op=ALU.mult),
             reads=[tag, "lng"], writes=[tag])
        P.op("pool", lambda e: e.tensor_tensor(out=y[0:n, :], in0=y[0:n, :], in1=bsb[0:n, :], op=ALU.add),
             reads=[tag, "lnb"], writes=[tag])

    RA.reset(); RB.reset()
    oTm = RA.take(16 * NTOK, BF16).rearrange("p (h t) -> p h t", h=16)
    lng = RA.take(D); lnb = RA.take(D)
    ybuf = [RA.take(D), RA.take(D)]
    h1b = [RA.take(D, BF16), RA.take(D, BF16)]
    wout = RB.take(16 * D, BF16).rearrange("p (h d) -> p h d", h=16)
    OTM = [f"oTm{hd}" for hd in range(16)]
    WOUT = [f"wout{hd}" for hd in range(16)]
    for hd in range(16):
        P.op("sp", lambda e, hd=hd: e.dma_start(out=oTm[:, hd, :], in_=oscr_d[hd, :, bass.ds(col0, NTOK)]),
             reads=[f"oscr{hd}"], writes=[OTM[hd]], dma="d_oTm")
    P.join(OTM)
    for hd in range(16):
        P.op("pool", lambda e, hd=hd: e.dma_start(out=wout[:, hd, :], in_=wout_d[:, hd, :]),
             writes=[WOUT[hd]], dma="d_wout")
    P.join(WOUT)
    P.op("sp", lambda e: e.dma_start(out=lng, in_=lnp_d[0]), writes=["lng"], dma="d_lng")
    P.op("sp", lambda e: e.dma_start(out=lnb, in_=lnp_d[1]), writes=["lnb"], dma="d_lnb")
    for ti, (r0, n) in enumerate(ttiles):
        b = ti % 2
        y = ybuf[b]; ytag = f"y{b}"
        P.op("sp", lambda e, y=y, r0=r0, n=n: e.dma_start(out=y[0:n, :], in_=hrow_d[r0:r0 + n, :]),
             writes=[ytag], dma=f"d_y{b}")
        for db in range(4):
            acc = pb[db]
            for hd in range(16):
                P.op("pe", lambda e, acc=acc, hd=hd, r0=r0, n=n, db=db: e.matmul(
                    acc[0:n, :], oTm[:, hd, r0:r0 + n], wout[:, hd, db * 512:(db + 1) * 512],
                    start=(hd == 0), stop=(hd == 15)), reads=[OTM[hd], WOUT[hd]], writes=[f"pb{db}"])
            P.op("dve", lambda e, acc=acc, y=y, n=n, db=db: e.scalar_tensor_tensor(
                out=y[0:n, db * 512:(db + 1) * 512], in0=y[0:n, db * 512:(db + 1) * 512], scalar=ALPHA,
                in1=acc[0:n, :], op0=ALU.mult, op1=ALU.add), reads=[ytag, f"pb{db}"], writes=[ytag])
        layer_norm_tile(y, n, lng, lnb, ytag)
        hb = h1b[b]
        P.op("act", lambda e, hb=hb, y=y, n=n: e.copy(out=hb[0:n, :], in_=y[0:n, :]), reads=[ytag], writes=[f"h1b{b}"])
        if ti > 0:
            P.op("sp", lambda e, y=y, ti=ti: e.dma_start(out=h1s_d[(ti - 1) * 128:ti * 128, :], in_=y[:, :]),
                 reads=[ytag], writes=["h1s"], dma="d_h1s", out=True)
        for cg in range(2):
            for cc in range(8):
                c = cg * 8 + cc
                P.op("pe", lambda e, hb=hb, n=n, c=c, cc=cc: e.transpose(
                    pbh[:, cc * 128:cc * 128 + n], hb[0:n, c * 128:(c + 1) * 128], identb[0:n, 0:n]),
                    reads=[f"h1b{b}", "identb"], writes=["pbh"])
            src = pbh[:, :].rearrange("p (c t) -> p c t", c=8)
            if cg == 0:
                P.op("act", lambda e, src=src, cg=cg, r0=r0, n=n: e.copy(
                    out=h1T[:, cg * 8:(cg + 1) * 8, r0:r0 + n], in_=src[:, :, 0:n]), reads=["pbh"], writes=["h1T"])
            else:
                P.op("dve", lambda e, src=src, cg=cg, r0=r0, n=n: e.tensor_copy(
                    out=h1T[:, cg * 8:(cg + 1) * 8, r0:r0 + n], in_=src[:, :, 0:n]), reads=["pbh"], writes=["h1T"])
    P.barrier()

    RA.reset(); RB.reset()
    hidT = RA.take(NFF * 1024, BF16).rearrange("p (f t) -> p f t", f=NFF)
    NWB = 3
    wgb = [RB.take(16 * 128, BF16).rearrange("p (c n) -> p c n", c=16) for i in range(NWB)]
    wub = [RB.take(16 * 128, BF16).rearrange("p (c n) -> p c n", c=16) for i in range(NWB)]
    gsb = [RB.take(NTOK) for i in range(2)]
    cacc = [RB.take(1024) for i in range(2)]
    sil = [RB.take(1024) for i in range(2)]

    def load_w(f):
        s = f % NWB
        P.op("pool", lambda e: e.dma_start(out=wgb[s], in_=wg_d[f]), writes=[f"wgb{s}"], dma=f"d_wg{s}")
        P.op("pool", lambda e: e.dma_start(out=wub[s], in_=wu_d[f]), writes=[f"wub{s}"], dma=f"d_wu{s}")

    load_w(0); load_w(1)
    for f in range(NFF):
        s = f % NWB; b = f % 2
        if f + 2 < NFF:
            load_w(f + 2)
        for half in range(2):
            for c in range(16):
                P.op("pe", lambda e, c=c, half=half, s=s: e.matmul(
                    pb[half][:, :], wgb[s][:, c, :], h1T[:, c, 2 + half * 512:2 + (half + 1) * 512],
                    start=(c == 0), stop=(c == 15)), reads=[f"wgb{s}", "h1T"], writes=[f"pb{half}"])
        for c in range(16):
            P.op("pe", lambda e, c=c, s=s: e.matmul(pb[4][:, 0:2], wgb[s][:, c, :], h1T[:, c, 0:2],
                                                    start=(c == 0), stop=(c == 15)),
                 reads=[f"wgb{s}", "h1T"], writes=["pb4"])
        for half in range(2):
            for c in range(16):
                P.op("pe", lambda e, c=c, half=half, s=s: e.matmul(
                    pb[2 + half][:, :], wub[s][:, c, :], h1T[:, c, 2 + half * 512:2 + (half + 1) * 512],
                    start=(c == 0), stop=(c == 15)), reads=[f"wub{s}", "h1T"], writes=[f"pb{2 + half}"])
        g = gsb[b]; ca = cacc[b]; sl = sil[b]
        P.op("act", lambda e, g=g: e.copy(out=g[:, 0:2], in_=pb[4][:, 0:2]), reads=["pb4"], writes=[f"gsb{b}"])
        for half in range(2):
            P.op("act", lambda e, g=g, half=half: e.copy(out=g[:, 2 + half * 512:2 + (half + 1) * 512], in_=pb[half][:, :]),
                 reads=[f"pb{half}"], writes=[f"gsb{b}"])
        P.op("dve", lambda e, g=g, ca=ca, f=f: e.tensor_scalar(
            out=ca[:, :], in0=g[:, 2:1026], scalar1=fcw[:, f, 2:3], scalar2=fcw[:, f, 3:4], op0=ALU.mult, op1=ALU.add),
            reads=[f"gsb{b}", "fcw"], writes=[f"cacc{b}"])
        P.op("dve", lambda e, g=g, ca=ca, f=f: e.scalar_tensor_tensor(
            out=ca[:, :], in0=g[:, 1:1025], scalar=fcw[:, f, 1:2], in1=ca[:, :], op0=ALU.mult, op1=ALU.add),
            reads=[f"gsb{b}", "fcw", f"cacc{b}"], writes=[f"cacc{b}"])
        P.op("dve", lambda e, g=g, ca=ca, f=f: e.scalar_tensor_tensor(
            out=ca[:, :], in0=g[:, 0:1024], scalar=fcw[:, f, 0:1], in1=ca[:, :], op0=ALU.mult, op1=ALU.add),
            reads=[f"gsb{b}", "fcw", f"cacc{b}"], writes=[f"cacc{b}"])
        P.op("act", lambda e, ca=ca, sl=sl: e.activation(out=sl[:, :], in_=ca[:, :], func=AF.Silu),
             reads=[f"cacc{b}"], writes=[f"sil{b}"])
        for half in range(2):
            P.op("dve", lambda e, sl=sl, half=half, f=f: e.tensor_tensor(
                out=hidT[:, f, half * 512:(half + 1) * 512], in0=sl[:, half * 512:(half + 1) * 512],
                in1=pb[2 + half][:, :], op=ALU.mult), reads=[f"sil{b}", f"pb{2 + half}"], writes=[f"hid{f}"])
    P.barrier()
    HID = [f"hid{f}" for f in range(NFF)]

    RB.reset(); RC.reset()
    y2 = RB.take(8 * D).rearrange("p (t d) -> p t d", t=8)
    Y2 = [f"y2_{tt}" for tt in range(8)]
    for tt in range(8):
        P.op("sp", lambda e, tt=tt: e.dma_start(out=y2[:, tt, :], in_=h1s_d[tt * 128:(tt + 1) * 128, :]),
             reads=["h1s"], writes=[Y2[tt]], dma="d_y2")
    P.join(Y2)
    wdb = [RC.take(NFF * 128, BF16).rearrange("p (f n) -> p f n", f=NFF) for i in range(2)]
    fsb = [RC.take(1024) for i in range(2)]

    def load_wd(dt):
        s = dt % 2
        P.op("pool", lambda e: e.dma_start(out=wdb[s], in_=wd_d[dt]), writes=[f"wdb{s}"], dma=f"d_wd{s}")

    load_wd(0)
    for dt in range(16):
        s = dt % 2
        if dt + 1 < 16:
            load_wd(dt + 1)
        for half in range(2):
            for f in range(NFF):
                P.op("pe", lambda e, half=half, f=f, s=s: e.matmul(
                    pb[half][:, :], wdb[s][:, f, :], hidT[:, f, half * 512:(half + 1) * 512],
                    start=(f == 0), stop=(f == NFF - 1)), reads=[f"wdb{s}", HID[f]], writes=[f"pb{half}"])
        fs = fsb[s]
        for half in range(2):
            P.op("act", lambda e, fs=fs, half=half: e.copy(out=fs[:, half * 512:(half + 1) * 512], in_=pb[half][:, :]),
                 reads=[f"pb{half}"], writes=[f"fsb{s}"])
        for tg in range(2):
            bank = pb[2 + tg]
            for k in range(4):
                tt = tg * 4 + k
                P.op("pe", lambda e, bank=bank, k=k, tt=tt, fs=fs: e.transpose(
                    bank[:, k * 128:(k + 1) * 128], fs[:, tt * 128:(tt + 1) * 128], ident),
                    reads=[f"fsb{s}", "cst"], writes=[f"pb{2 + tg}"])
            src = bank[:, :].rearrange("p (k n) -> p k n", k=4)
            P.op("dve", lambda e, src=src, tg=tg, dt=dt: e.scalar_tensor_tensor(
                out=y2[:, tg * 4:(tg + 1) * 4, dt * 128:(dt + 1) * 128], in0=y2[:, tg * 4:(tg + 1) * 4, dt * 128:(dt + 1) * 128],
                scalar=ALPHA, in1=src, op0=ALU.mult, op1=ALU.add),
                reads=[f"pb{2 + tg}"] + Y2[tg * 4:(tg + 1) * 4], writes=Y2[tg * 4:(tg + 1) * 4])
    P.barrier()
    RA.reset()
    lng = RA.take(D); lnb = RA.take(D)
    P.op("sp", lambda e: e.dma_start(out=lng, in_=lnp_d[2]), writes=["lng"], dma="d_lng")
    P.op("sp", lambda e: e.dma_start(out=lnb, in_=lnp_d[3]), writes=["lnb"], dma="d_lnb")
    for tt in range(8):
        layer_norm_tile(y2[:, tt, :], 128, lng, lnb, Y2[tt])
        P.op("sp", lambda e, tt=tt: e.dma_start(out=out_d[tt * 128:(tt + 1) * 128, :], in_=y2[:, tt, :]),
             reads=[Y2[tt]], dma="d_out", out=True)


def emit_attention(nc, P, st, sbt, pb, pbh, xT, XT, cst, identb, pswapb, onesb, sm, rope_d, waq_d, wak_d, wav_d,
                   oscr_d, heads, lvl=9):
    AXX = mybir.AxisListType.X
    onesf = cst[:, 6, :]
    cosT = sbt(st, "cosT", [128, T]); sinT = sbt(st, "sinT", [128, T])
    P.op("sp", lambda e: e.dma_start(out=cosT[:], in_=rope_d[0]), writes=["cosT"], dma="d_cos")
    P.op("sp", lambda e: e.dma_start(out=sinT[:], in_=rope_d[1]), writes=["sinT"], dma="d_sin")
    prod = sbt(st, "lprod", [128, 2, 64]); ls = sbt(st, "lsum", [128, 2]); neglam = sbt(st, "neglam", [128, 1])
    nws = sbt(st, "nws", [128, 1]); epsr = sbt(st, "epsr", [128, 1])
    P.op("dve", lambda e: e.tensor_tensor(out=prod[:, 0, :], in0=sm[:, 0:64], in1=sm[:, 64:128], op=ALU.mult), reads=["sm"], writes=["lprod"])
    P.op("dve", lambda e: e.tensor_tensor(out=prod[:, 1, :], in0=sm[:, 128:192], in1=sm[:, 192:256], op=ALU.mult), reads=["sm", "lprod"], writes=["lprod"])
    P.op("dve", lambda e: e.reduce_sum(out=ls[:, :], in_=prod[:, :, :], axis=AXX), reads=["lprod"], writes=["lsum"])
    P.op("act", lambda e: e.activation(out=ls[:, :], in_=ls[:, :], func=AF.Exp), reads=["lsum"], writes=["lsum"])
    P.op("dve", lambda e: e.tensor_tensor(out=neglam[:, :], in0=ls[:, 1:2], in1=ls[:, 0:1], op=ALU.subtract), reads=["lsum"], writes=["neglam"])
    P.op("dve", lambda e: e.tensor_scalar(out=neglam[:, :], in0=neglam[:, :], scalar1=-LAM_INIT, scalar2=None, op0=ALU.add), reads=["neglam"], writes=["neglam"])
    P.op("dve", lambda e: e.tensor_scalar(out=nws[:, :], in0=sm[:, 272:273], scalar1=1.0 - LAM_INIT, scalar2=None, op0=ALU.mult), reads=["sm"], writes=["nws"])
    P.op("dve", lambda e: e.memset(epsr[:], RMS_EPS), writes=["epsr"])

    wv = sbt(st, "wv", [128, 16, 512], BF16)
    vtm = sbt(st, "vtm", [128, 17, 512], BF16)
    wqb = [sbt(st, f"wqb{i}", [128, 16, 128], BF16) for i in range(2)]
    wkb = [sbt(st, f"wkb{i}", [128, 16, 128], BF16) for i in range(2)]
    qT = sbt(st, "qT", [128, T], BF16); kT = sbt(st, "kT", [128, T], BF16)
    qb = [sbt(st, f"qb{i}", [128, 512], BF16) for i in range(2)]
    t1 = [sbt(st, f"t1_{i}", [128, 512]) for i in range(2)]
    t2 = [sbt(st, f"t2_{i}", [128, 512]) for i in range(2)]
    ptb = [sbt(st, f"ptb{i}", [128, 2, 256], BF16) for i in range(3)]
    rZ = sbt(st, "rZ", [128, 2, 256]); Aa = sbt(st, "Aa", [128, 2, 256])
    od = sbt(st, "od", [128, 256]); sq = sbt(st, "sq", [128, 256]); rs = sbt(st, "rs", [128, 256])
    oTb = [sbt(st, f"oTb{i}", [128, T], BF16) for i in range(2)]
    P.op("pool", lambda e: e.memset(vtm[:], 0.0), writes=["vtm"])

    blocks = [(0, 16)] + [(16 + 512 * j, 512) for j in range(4)]
    ttiles = [(0, 16)] + [(16 + 128 * j, 128) for j in range(16)]
    cnt = {"acc": 0, "qb": 0, "st": 0, "pt": 0}
    cur_group = [None]

    def load_qk(hi):
        h = heads[hi]; par = hi % 2
        P.op("pool", lambda e: e.dma_start(out=wqb[par][:], in_=waq_d[h]), writes=[f"wqb{par}"], dma=f"d_wq{par}")
        P.op("pool", lambda e: e.dma_start(out=wkb[par][:], in_=wak_d[h]), writes=[f"wkb{par}"], dma=f"d_wk{par}")

    load_qk(0)
    for hi, h in enumerate(heads):
        par = hi % 2
        if hi + 1 < len(heads):
            load_qk(hi + 1)
        g = h // 4
        if cur_group[0] != g:
            cur_group[0] = g
            P.op("pool", lambda e, g=g: e.dma_start(out=wv[:], in_=wav_d[g]), writes=["wv"], dma="d_wv")
            for tt, (c0, n) in enumerate(ttiles):
                bi = 4 + cnt["acc"] % 2; cnt["acc"] += 1
                acc = pb[bi]
                for c in range(16):
                    P.op("pe", lambda e, acc=acc, c=c, c0=c0, n=n: e.matmul(acc[0:n, :], xT[:, c, c0:c0 + n], wv[:, c, :],
                                                                           start=(c == 0), stop=(c == 15)),
                         reads=[XT[c], "wv"], writes=[f"pb{bi}"])
                P.op("act", lambda e, acc=acc, tt=tt, n=n: e.copy(out=vtm[0:n, tt, :], in_=acc[0:n, :]), reads=[f"pb{bi}"], writes=["vtm"])
        hc = (h % 4) * 128
        if lvl < 1:
            continue
        for which, wsb, wtag, dst, dtag in (("q", wqb[par], f"wqb{par}", qT, "qT"), ("k", wkb[par], f"wkb{par}", kT, "kT")):
            for (c0, n) in blocks:
                bi = 4 + cnt["acc"] % 2; cnt["acc"] += 1
                acc = pb[bi]
                qi = cnt["qb"] % 2; cnt["qb"] += 1
                for c in range(16):
                    P.op("pe", lambda e, acc=acc, c=c, c0=c0, n=n, wsb=wsb: e.matmul(acc[:, 0:n], wsb[:, c, :], xT[:, c, c0:c0 + n],
                                                                                    start=(c == 0), stop=(c == 15)),
                         reads=[XT[c], wtag], writes=[f"pb{bi}"])
                P.op("act", lambda e, acc=acc, qi=qi, n=n: e.copy(out=qb[qi][:, 0:n], in_=acc[:, 0:n]), reads=[f"pb{bi}"], writes=[f"qb{qi}"])
                P.op("pe", lambda e, qi=qi, n=n: e.matmul(pb[6][:, 0:n], pswapb[:, :], qb[qi][:, 0:n], start=True, stop=True),
                     reads=["pswapb", f"qb{qi}"], writes=["pb6"])
                P.op("dve", lambda e, qi=qi, n=n, c0=c0: e.tensor_tensor(out=t1[qi][:, 0:n], in0=pb[6][:, 0:n], in1=sinT[:, c0:c0 + n], op=ALU.mult),
                     reads=["pb6", "sinT"], writes=[f"t1_{qi}"])
                P.op("dve", lambda e, acc=acc, qi=qi, n=n, c0=c0: e.tensor_tensor(out=t2[qi][:, 0:n], in0=acc[:, 0:n], in1=cosT[:, c0:c0 + n], op=ALU.mult),
                     reads=[f"pb{bi}", "cosT"], writes=[f"t2_{qi}"])
                P.op("pool", lambda e, qi=qi, n=n, c0=c0, dst=dst: e.tensor_tensor(out=dst[:, c0:c0 + n], in0=t1[qi][:, 0:n], in1=t2[qi][:, 0:n], op=ALU.add),
                     reads=[f"t1_{qi}", f"t2_{qi}"], writes=[dtag])
        if lvl < 2:
            continue
        ob = oTb[par]; otag = f"oTb{par}"
        groups = [(0, 16, [(0, 16, 0, 0, None)])]
        for gq in range(8):
            kts = [(0, 16, 0, 0, None)]
            for j in range(2 * gq + 2):
                qlo = 0 if j <= 2 * gq else 128
                dl = 0 if j == 2 * gq else (128 if j == 2 * gq + 1 else None)
                kts.append((16 + 128 * j, 128, j + 1, qlo, dl))
            groups.append((16 + 256 * gq, 256, kts))
        if lvl < 3:
            groups = groups[:1]
        elif lvl < 4:
            groups = groups[:2]
        pending = [None]
        for (qc0, nq, kts) in groups:
            O = pb[2][:, 0:2 * nq].rearrange("p (c n) -> p c n", c=2)
            Z = pb[3][:, 0:2 * nq].rearrange("p (c n) -> p c n", c=2)
            slots = {}

            def emit_scores(ki, qc0=qc0, nq=nq, kts=kts, slots=slots):
                kc0, nk, vt, qlo, dl = kts[ki]
                si = cnt["st"] % 2; cnt["st"] += 1
                pi = cnt["pt"] % 3; cnt["pt"] += 1
                slots[ki] = pi
                sbk = (0, 1) if si == 0 else (4, 5)
                PT = ptb[pi]
                for cc in range(2):
                    P.op("pe", lambda e, cc=cc: e.matmul(
                        pb[sbk[cc]][0:nk, qlo:nq], kT[cc * 64:(cc + 1) * 64, kc0:kc0 + nk], qT[cc * 64:(cc + 1) * 64, qc0 + qlo:qc0 + nq],
                        start=True, stop=True), reads=["kT", "qT"], writes=[f"pb{sbk[cc]}"])
                for cc in range(2):
                    P.op("act", lambda e, cc=cc: e.activation(
                        out=PT[0:nk, cc, qlo:nq], in_=pb[sbk[cc]][0:nk, qlo:nq], func=AF.Exp, scale=0.125), reads=[f"pb{sbk[cc]}"], writes=[f"ptb{pi}"])
                if dl is not None:
                    P.op("dve", lambda e: e.memset(PT[64:128, :, dl:dl + 64], 0.0), reads=[f"ptb{pi}"], writes=[f"ptb{pi}"])

            def emit_av(ki, qc0=qc0, nq=nq, kts=kts, slots=slots, O=O, Z=Z):
                kc0, nk, vt, qlo, dl = kts[ki]
                pi = slots[ki]; PT = ptb[pi]
                first = ki == 0; last = ki == len(kts) - 1
                lastflat = ki == len(kts) - 2
                if qlo == 0 and nq == 256:
                    PTf = PT[0:nk, :, :].rearrange("p c n -> p (c n)")
                    P.op("pe", lambda e: e.matmul(pb[2][:, 0:512], vtm[0:nk, vt, hc:hc + 128], PTf, start=first, stop=lastflat),
                         reads=["vtm", f"ptb{pi}"], writes=["pb2"])
                    P.op("pe", lambda e: e.matmul(pb[3][:, 0:512], onesb[0:nk, :], PTf, start=first, stop=lastflat),
                         reads=["onesb", f"ptb{pi}"], writes=["pb3"])
                else:
                    for cc in range(2):
                        P.op("pe", lambda e, cc=cc: e.matmul(
                            O[:, cc, qlo:nq], vtm[0:nk, vt, hc:hc + 128], PT[0:nk, cc, qlo:nq], start=(first and cc == 0), stop=(last and cc == 1),
                            skip_group_check=True), reads=["vtm", f"ptb{pi}"], writes=["pb2"])
                        P.op("pe", lambda e, cc=cc: e.matmul(
                            Z[:, cc, qlo:nq], onesb[0:nk, :], PT[0:nk, cc, qlo:nq], start=(first and cc == 0), stop=(last and cc == 1),
                            skip_group_check=True), reads=["onesb", f"ptb{pi}"], writes=["pb3"])

            nk_ = len(kts)
            emit_scores(0)
            if nk_ > 1:
                emit_scores(1)
            if pending[0] is not None:
                pending[0](); pending[0] = None
            for ki in range(nk_):
                if ki + 2 < nk_:
                    emit_scores(ki + 2)
                emit_av(ki)
            P.op("dve", lambda e, Z=Z, nq=nq: e.reciprocal(out=rZ[:, :, 0:nq], in_=Z), reads=["pb3"], writes=["rZ"])
            P.op("dve", lambda e, O=O, nq=nq: e.tensor_tensor(out=Aa[:, :, 0:nq], in0=O, in1=rZ[:, :, 0:nq], op=ALU.mult),
                 reads=["pb2", "rZ"], writes=["Aa"])
            P.op("dve", lambda e, nq=nq: e.scalar_tensor_tensor(out=od[:, 0:nq], in0=Aa[:, 1, 0:nq], scalar=neglam[:, 0:1], in1=Aa[:, 0, 0:nq],
                                                                 op0=ALU.mult, op1=ALU.add), reads=["Aa", "neglam"], writes=["od"])
            P.op("dve", lambda e, nq=nq: e.tensor_tensor(out=sq[:, 0:nq], in0=od[:, 0:nq], in1=od[:, 0:nq], op=ALU.mult), reads=["od"], writes=["sq"])

            def post2(nq=nq, qc0=qc0, ob=ob, otag=otag):
                P.op("pe", lambda e: e.matmul(pb[6][:, 0:nq], onesf, sq[:, 0:nq], start=True, stop=True), reads=["cst", "sq"], writes=["pb6"])
                P.op("act", lambda e: e.activation(out=rs[:, 0:nq], in_=pb[6][:, 0:nq], func=AF.Ln, bias=epsr[:, 0:1], scale=1.0 / 128),
                     reads=["pb6", "epsr"], writes=["rs"])
                P.op("act", lambda e: e.activation(out=rs[:, 0:nq], in_=rs[:, 0:nq], func=AF.Exp, scale=-0.5), reads=["rs"], writes=["rs"])
                P.op("dve", lambda e: e.scalar_tensor_tensor(out=ob[:, qc0:qc0 + nq], in0=od[:, 0:nq], scalar=nws[:, 0:1], in1=rs[:, 0:nq],
                                                             op0=ALU.mult, op1=ALU.mult), reads=["od", "nws", "rs"], writes=[otag])
            pending[0] = post2
        if pending[0] is not None:
            pending[0](); pending[0] = None
        P.op("sp", lambda e, ob=ob, h=h: e.dma_start(out=oscr_d[h], in_=ob[:, :]), reads=[otag], writes=[f"oscr{h}"], dma=f"d_oscr{par}", out=True)


def emit_gdn(nc, P, st, sbt, pb, pbh, xT, XT, cst, sm, wdq_d, wdk_d, wdv_d, wdz_d, wba_d, cw_d, oscr_d, heads, GS=2):
    ident = cst[:, 0, :]; U = cst[:, 2, :]; Bm = cst[:, 3, :]; NM1 = cst[:, 4, :]; NM2 = cst[:, 5, :]; onesf = cst[:, 6, :]
    ttiles = [(0, 16)] + [(16 + 128 * j, 128) for j in range(16)]
    rot = {"b": 0, "u": 0}

    def nb():
        i = rot["b"] % 3; rot["b"] += 1
        return i, pb[i], f"pb{i}"

    def uid(p):
        rot["u"] += 1
        return f"{p}{rot['u']}"

    def evac(eng, out_ap, in_ap, reads, writes):
        if eng == "act":
            P.op("act", lambda e: e.copy(out=out_ap, in_=in_ap), reads=reads, writes=writes)
        else:
            P.op("dve", lambda e: e.tensor_copy(out=out_ap, in_=in_ap), reads=reads, writes=writes)

    wba = sbt(st, "wba", [128, 16, 16], BF16)
    P.op("pool", lambda e: e.dma_start(out=wba[:], in_=wba_d), writes=["wba"], dma="d_wba")
    cw = sbt(st, "cw", [128, 3, 8, 4])
    P.op("sp", lambda e: e.dma_start(out=cw[:], in_=cw_d), writes=["cw"], dma="d_cw")
    ba = sbt(st, "ba", [128, 17, 16]); beta = sbt(st, "beta", [128, 17, 8]); gg = sbt(st, "gg", [128, 17, 8])
    gcl = sbt(st, "gcl", [128, 17, 16]); bg = sbt(st, "bg", [128, 17, 8]); kdsc = sbt(st, "kdsc", [128, 17, 8])
    nea = sbt(st, "nea", [128, 8]); tA = sbt(st, "tA", [128, 17]); tB = sbt(st, "tB", [128, 17]); tC = sbt(st, "tC", [128, 17, 8])
    epsr = sbt(st, "epsg", [128, 1])
    P.op("dve", lambda e: e.memset(epsr[:], RMS_EPS), writes=["epsg"])
    P.op("dve", lambda e: e.memset(ba[:], 0.0), writes=["ba"])
    P.op("dve", lambda e: e.memset(gcl[:], 0.0), writes=["gcl"])
    for tt, (c0, n) in enumerate(ttiles):
        bi, bank, btag = nb()
        for c in range(16):
            P.op("pe", lambda e, bank=bank, c=c, c0=c0, n=n: e.matmul(bank[0:n, 0:16], xT[:, c, c0:c0 + n], wba[:, c, :],
                                                                     start=(c == 0), stop=(c == 15)), reads=[XT[c], "wba"], writes=[btag])
        evac("act", ba[0:n, tt, :], bank[0:n, 0:16], [btag], ["ba"])
    P.op("act", lambda e: e.activation(out=beta[:, :, :], in_=ba[:, :, 0:8], func=AF.Sigmoid), reads=["ba"], writes=["beta"])
    P.op("act", lambda e: e.activation(out=nea[:, :], in_=sm[:, 256:264], func=AF.Exp), reads=["sm"], writes=["nea"])
    P.op("dve", lambda e: e.tensor_scalar(out=nea[:, :], in0=nea[:, :], scalar1=-1.0, scalar2=None, op0=ALU.mult), reads=["nea"], writes=["nea"])
    for h in range(8):
        P.op("dve", lambda e, h=h: e.tensor_scalar(out=tA[:, :], in0=ba[:, :, 8 + h], scalar1=sm[:, 264 + h:265 + h], scalar2=None, op0=ALU.add),
             reads=["ba", "sm"], writes=["tA"])
        P.op("act", lambda e: e.activation(out=tB[:, :], in_=tA[:, :], func=AF.Abs), reads=["tA"], writes=["tB"])
        P.op("act", lambda e: e.activation(out=tB[:, :], in_=tB[:, :], func=AF.Exp, scale=-1.0), reads=["tB"], writes=["tB"])
        P.op("act", lambda e: e.activation(out=tB[:, :], in_=tB[:, :], func=AF.Ln, bias=1.0), reads=["tB"], writes=["tB"])
        P.op("dve", lambda e: e.scalar_tensor_tensor(out=tA[:, :], in0=tA[:, :], scalar=0.0, in1=tB[:, :], op0=ALU.max, op1=ALU.add),
             reads=["tA", "tB"], writes=["tA"])
        P.op("dve", lambda e, h=h: e.tensor_scalar(out=gg[:, :, h], in0=tA[:, :], scalar1=nea[:, h:h + 1], scalar2=None, op0=ALU.mult),
             reads=["tA", "nea"], writes=["gg"])
    P.op("dve", lambda e: e.tensor_scalar(out=gg[:, 0, :], in0=gg[:, 0, :], scalar1=U[:, 15:16], scalar2=None, op0=ALU.mult), reads=["gg", "cst"], writes=["gg"])
    for tt, (c0, n) in enumerate(ttiles):
        bi, bank, btag = nb()
        P.op("pe", lambda e, bank=bank, n=n, tt=tt: e.matmul(bank[0:n, 0:8], U[0:n, 0:n], gg[0:n, tt, :], start=True, stop=True),
             reads=["cst", "gg"], writes=[btag])
        P.op("pe", lambda e, bank=bank, n=n, tt=tt: e.matmul(bank[0:n, 8:16], Bm[0:n, 0:n], gg[0:n, tt, :], start=True, stop=True),
             reads=["cst", "gg"], writes=[btag])
        evac("act", gcl[0:n, tt, :], bank[0:n, 0:16], [btag], ["gcl"])
    P.op("act", lambda e: e.activation(out=tC[:, :, :], in_=gcl[:, :, 0:8], func=AF.Exp), reads=["gcl"], writes=["tC"])
    P.op("dve", lambda e: e.tensor_tensor(out=bg[:, :, :], in0=beta[:, :, :], in1=tC[:, :, :], op=ALU.mult), reads=["beta", "tC"], writes=["bg"])
    P.op("dve", lambda e: e.tensor_tensor(out=tC[:, :, :], in0=gcl[:, :, 8:16], in1=gcl[:, :, 0:8], op=ALU.subtract), reads=["gcl", "tC"], writes=["tC"])
    P.op("act", lambda e: e.activation(out=kdsc[:, :, :], in_=tC[:, :, :], func=AF.Exp), reads=["tC"], writes=["kdsc"])

    def mk(name, shape, dt=F32):
        return [sbt(st, f"{name}_{i}", shape, dt) for i in range(GS)]

    wts = {k: [sbt(st, f"g{k}_{i}", [128, 16, 128], BF16) for i in range(GS)] for k in ("q", "k", "v", "z")}
    pre = {k: mk("pre" + k, [128, 515]) for k in ("q", "k", "v")}
    cvo = {k: [mk(f"cvo{k}{p}", [128, 512]) for p in range(2)] for k in ("q", "k", "v")}
    cacc = mk("gcacc", [128, 512]); rn = mk("grn", [128, 512])
    S = mk("S", [128, 128])
    gb = mk("gb", [128, 128]); xx = mk("xx", [128, 128]); a1 = mk("a1", [128, 128]); a2 = mk("a2", [128, 128])
    Dst = mk("Dst", [128, 128]); DTt = mk("DTt", [128, 128]); egrow = [mk(f"egrow{p}", [128, 128]) for p in range(2)]
    Mm = mk("Mm", [128, 128]); Nm = mk("Nm", [128, 128])
    YQ = [mk(f"YQ{j}", [128, 256]) for j in range(2)]
    QT = [mk(f"QT{j}", [128, 128]) for j in range(2)]
    kbg = mk("kbg", [128, 128]); kdec = [mk(f"kdec{p}", [128, 128], BF16) for p in range(2)]; vb = mk("vb", [128, 128]); uu = [mk(f"uu{p}", [128, 128]) for p in range(2)]
    wT = [mk(f"wT{p}", [128, 128], BF16) for p in range(2)]; qkT = [mk(f"qkT{p}", [128, 128], BF16) for p in range(2)]
    qdT = [mk(f"qdT{p}", [128, 128], BF16) for p in range(2)]; szT = [mk(f"szT{p}", [128, 512]) for p in range(2)]
    vnew = mk("vnew", [128, 128], BF16); odn = mk("odn", [128, 128]); t3 = mk("t3", [128, 128]); Sb = mk("Sb", [128, 128], BF16)
    st6 = mk("st6", [128, 6]); mvv = mk("mvv", [128, 2]); msq = mk("msq", [128, 1])
    oTs = [mk(f"oTs{j}", [128, 128], BF16) for j in range(2)]
    wsrc = {"q": wdq_d, "k": wdk_d, "v": wdv_d, "z": wdz_d}

    groups = [heads[i:i + GS] for i in range(0, len(heads), GS)]

    def load_weights(gi):
        for i, h in enumerate(groups[gi]):
            for k in ("q", "k", "v", "z"):
                P.op("pool", lambda e, i=i, h=h, k=k: e.dma_start(out=wts[k][i][:], in_=wsrc[k][h]),
                     writes=[f"gw{k}{i}"], dma=f"d_gw{k}{i}")

    import os as _os
    GL = int(_os.environ.get("GDN_LVL", "9"))
    if GL < 1:
        return
    blocks = [(0, 16, [0])] + [(16 + 512 * j, 512, [1 + 4 * j + t for t in range(4)]) for j in range(4)]
    import itertools as _it

    def blk_chain(i, h, bidx):
        c0, n, tiles_ = blocks[bidx]; bp = bidx % 2
        for pi, k in enumerate(("q", "k", "v")):
            w = wts[k][i]; wtag = f"gw{k}{i}"
            pr = pre[k][i]; ptag = f"pre{k}{i}"; co = cvo[k][bp][i]; ctag = f"cvo{k}{bp}{i}"
            nprev = blocks[bidx - 1][1] if bidx > 0 else 0
            bi, bank, btag = nb()
            for c in range(16):
                P.op("pe", lambda e, c=c: e.matmul(bank[:, 0:n], w[:, c, :], xT[:, c, c0:c0 + n], start=(c == 0), stop=(c == 15)), reads=[XT[c], wtag], writes=[btag])
            if bidx == 0:
                P.op("pool", lambda e: e.memset(pr[:, 0:3], 0.0), writes=[ptag])
            else:
                P.op("pool", lambda e: e.tensor_copy(out=pr[:, 0:3], in_=pr[:, nprev:nprev + 3]), reads=[ptag], writes=[ptag])
            evac("act", pr[:, 3:3 + n], bank[:, 0:n], [btag], [ptag])
            yield
            ca = cacc[i]; catag = f"gcacc{i}"
            P.op("dve", lambda e: e.tensor_scalar(out=ca[:, 0:n], in0=pr[:, 3:3 + n], scalar1=cw[:, pi, h, 3:4], scalar2=None, op0=ALU.mult), reads=[ptag, "cw"], writes=[catag])
            yield
            for j in range(3):
                P.op("dve", lambda e, j=j: e.scalar_tensor_tensor(out=ca[:, 0:n], in0=pr[:, j:j + n], scalar=cw[:, pi, h, j:j + 1], in1=ca[:, 0:n], op0=ALU.mult, op1=ALU.add),
                     reads=[ptag, "cw", catag], writes=[catag])
                yield
            P.op("act", lambda e: e.activation(out=co[:, 0:n], in_=ca[:, 0:n], func=AF.Silu), reads=[catag], writes=[ctag])
            yield
            if k in ("q", "k"):
                r_ = rn[i]
                P.op("dve", lambda e: e.tensor_tensor(out=ca[:, 0:n], in0=co[:, 0:n], in1=co[:, 0:n], op=ALU.mult), reads=[ctag], writes=[catag])
                yield
                b2, bank2, btag2 = nb()
                P.op("pe", lambda e: e.matmul(bank2[:, 0:n], onesf, ca[:, 0:n], start=True, stop=True), reads=["cst", catag], writes=[btag2])
                P.op("act", lambda e: e.activation(out=r_[:, 0:n], in_=bank2[:, 0:n], func=AF.Ln, bias=epsr[:, 0:1]), reads=[btag2, "epsg"], writes=[f"grn{i}"])
                P.op("act", lambda e: e.activation(out=r_[:, 0:n], in_=r_[:, 0:n], func=AF.Exp, scale=-0.5), reads=[f"grn{i}"], writes=[f"grn{i}"])
                yield
                if k == "k":
                    P.op("dve", lambda e: e.tensor_tensor(out=co[:, 0:n], in0=co[:, 0:n], in1=r_[:, 0:n], op=ALU.mult), reads=[ctag, f"grn{i}"], writes=[ctag])
                else:
                    P.op("dve", lambda e: e.scalar_tensor_tensor(out=co[:, 0:n], in0=co[:, 0:n], scalar=128.0 ** -0.5, in1=r_[:, 0:n], op0=ALU.mult, op1=ALU.mult),
                         reads=[ctag, f"grn{i}"], writes=[ctag])
                yield
        wz = wts["z"][i]
        bz, bankz, btz = nb()
        for c in range(16):
            P.op("pe", lambda e, c=c: e.matmul(bankz[:, 0:n], wz[:, c, :], xT[:, c, c0:c0 + n], start=(c == 0), stop=(c == 15)), reads=[XT[c], f"gwz{i}"], writes=[btz])
        P.op("act", lambda e: e.activation(out=szT[bp][i][:, 0:n], in_=bankz[:, 0:n], func=AF.Silu), reads=[btz], writes=[f"szT{bp}{i}"])
        yield

    def run(fg, bg=()):
        fg = list(fg)
        while fg:
            nxt = []
            for g in fg:
                try:
                    next(g); nxt.append(g)
                except StopIteration:
                    pass
            fg = nxt
            for g in list(bg):
                try:
                    next(g)
                except StopIteration:
                    bg.remove(g)

    for gi, hg in enumerate(groups):
        load_weights(gi)
        for i, h in enumerate(hg):
            P.op("dve", lambda e, i=i: e.memset(S[i][:], 0.0), writes=[f"S{i}"])
            P.op("dve", lambda e, i=i: e.memset(Sb[i][:], 0.0), writes=[f"Sb{i}"])
        run([blk_chain(i, h, 0) for i, h in enumerate(hg)])
        for bidx, (c0, n, tiles) in enumerate(blocks):
            bp = bidx % 2
            bgen = [blk_chain(i, h, bidx + 1) for i, h in enumerate(hg)] if bidx + 1 < len(blocks) else []

            import itertools as _it

            def tile_params(tl):
                tt = tiles[tl]; lo = 128 * tl; nt = 16 if tt == 0 else 128
                return tt, lo, nt, c0 + lo, ([(0, 16)] if tt == 0 else [(0, 64), (64, 64)]), tl % 2

            def pre_chain(i, h, tl):
                tt, lo, nt, tc0, chunks, pp = tile_params(tl)
                kc = cvo["k"][bp][i][:, lo:lo + nt]; qc = cvo["q"][bp][i][:, lo:lo + nt]; vc = cvo["v"][bp][i][:, lo:lo + nt]
                KT = f"cvok{bp}{i}"; QTg = f"cvoq{bp}{i}"; VT = f"cvov{bp}{i}"
                eg = egrow[pp][i]; egt = f"egrow{pp}{i}"
                P.op("dve", lambda e: e.tensor_scalar(out=gb[i][0:nt, :], in0=onesf[0:nt, :], scalar1=gg[0:nt, tt, h:h + 1], scalar2=None, op0=ALU.mult),
                     reads=["cst", "gg"], writes=[f"gb{i}"])
                b1, bk1, bt1 = nb()
                P.op("pe", lambda e: e.matmul(bk1[:, 0:nt], gb[i][0:nt, :], U[0:nt, 0:nt], start=True, stop=True), reads=[f"gb{i}", "cst"], writes=[bt1])
                P.op("dve", lambda e: e.tensor_scalar(out=xx[i][0:nt, 0:nt], in0=bk1[0:nt, 0:nt], scalar1=gcl[0:nt, tt, h:h + 1], scalar2=None, op0=ALU.subtract),
                     reads=[bt1, "gcl"], writes=[f"xx{i}"])
                P.op("act", lambda e: e.activation(out=eg[:, 0:nt], in_=bk1[:, 0:nt], func=AF.Exp), reads=[bt1, f"xx{i}"], writes=[egt])
                yield
                P.op("pool", lambda e: e.tensor_tensor(out=a1[i][0:nt, 0:nt], in0=xx[i][0:nt, 0:nt], in1=NM1[0:nt, 0:nt], op=ALU.add), reads=[f"xx{i}", "cst"], writes=[f"a1{i}"])
                P.op("pool", lambda e: e.tensor_tensor(out=a2[i][0:nt, 0:nt], in0=xx[i][0:nt, 0:nt], in1=NM2[0:nt, 0:nt], op=ALU.subtract), reads=[f"xx{i}", "cst"], writes=[f"a2{i}"])
                P.op("act", lambda e: e.activation(out=Dst[i][0:nt, 0:nt], in_=a1[i][0:nt, 0:nt], func=AF.Exp, scale=-1.0), reads=[f"a1{i}"], writes=[f"Dst{i}"])
                P.op("act", lambda e: e.activation(out=DTt[i][0:nt, 0:nt], in_=a2[i][0:nt, 0:nt], func=AF.Exp), reads=[f"a2{i}"], writes=[f"DTt{i}"])
                yield
                b2, bk2, bt2 = nb()
                P.op("pe", lambda e: e.matmul(bk2[0:nt, 0:nt], kc, kc, start=True, stop=True), reads=[KT], writes=[bt2])
                P.op("dve", lambda e: e.scalar_tensor_tensor(out=Mm[i][0:nt, 0:nt], in0=bk2[0:nt, 0:nt], scalar=beta[0:nt, tt, h:h + 1], in1=Dst[i][0:nt, 0:nt],
                                                             op0=ALU.mult, op1=ALU.mult), reads=[bt2, "beta", f"Dst{i}"], writes=[f"Mm{i}"])
                yield
                b3, bk3, bt3 = nb()
                P.op("pe", lambda e: e.transpose(bk3[0:nt, 0:nt], Mm[i][0:nt, 0:nt], ident[0:nt, 0:nt]), reads=[f"Mm{i}", "cst"], writes=[bt3])
                P.op("dve", lambda e: e.tensor_tensor(out=YQ[0][i][0:nt, 0:nt], in0=ident[0:nt, 0:nt], in1=bk3[0:nt, 0:nt], op=ALU.subtract),
                     reads=[bt3, "cst"], writes=[f"YQ0{i}"])
                evac("act", Nm[i][0:nt, 0:nt], bk3[0:nt, 0:nt], [bt3, f"YQ0{i}"], [f"Nm{i}"])
                yield
                b4, bk4, bt4 = nb()
                P.op("pe", lambda e: e.matmul(bk4[0:nt, 0:nt], Mm[i][0:nt, 0:nt], Nm[i][0:nt, 0:nt], start=True, stop=True), reads=[f"Mm{i}", f"Nm{i}"], writes=[bt4])
                evac("act", YQ[0][i][0:nt, nt:2 * nt], bk4[0:nt, 0:nt], [bt4], [f"YQ0{i}"])
                b5, bk5, bt5 = nb()
                P.op("pe", lambda e: e.matmul(bk5[0:nt, 0:nt], Nm[i][0:nt, 0:nt], Mm[i][0:nt, 0:nt], start=True, stop=True), reads=[f"Mm{i}", f"Nm{i}"], writes=[bt5])
                evac("dve", QT[0][i][0:nt, 0:nt], bk5[0:nt, 0:nt], [bt5], [f"QT0{i}"])
                yield
                for r in range(5):
                    cur = r % 2; nx = 1 - cur
                    b6, bk6, bt6 = nb()
                    ncol = 2 * nt if r < 4 else nt
                    P.op("pe", lambda e: e.matmul(bk6[0:nt, 0:ncol], QT[cur][i][0:nt, 0:nt], YQ[cur][i][0:nt, 0:ncol], start=True, stop=True),
                         reads=[f"QT{cur}{i}", f"YQ{cur}{i}"], writes=[bt6])
                    P.op("dve", lambda e: e.tensor_tensor(out=YQ[nx][i][0:nt, 0:nt], in0=YQ[cur][i][0:nt, 0:nt], in1=bk6[0:nt, 0:nt], op=ALU.add),
                         reads=[bt6, f"YQ{cur}{i}"], writes=[f"YQ{nx}{i}"])
                    if r < 4:
                        evac("act", YQ[nx][i][0:nt, nt:2 * nt], bk6[0:nt, nt:2 * nt], [bt6], [f"YQ{nx}{i}"])
                        b7, bk7, bt7 = nb()
                        P.op("pe", lambda e: e.matmul(bk7[0:nt, 0:nt], YQ[cur][i][0:nt, nt:2 * nt], QT[cur][i][0:nt, 0:nt], start=True, stop=True),
                             reads=[f"QT{cur}{i}", f"YQ{cur}{i}"], writes=[bt7])
                        evac("act", QT[nx][i][0:nt, 0:nt], bk7[0:nt, 0:nt], [bt7], [f"QT{nx}{i}"])
                    yield
                Tt = YQ[1][i][0:nt, 0:nt]; TtT = f"YQ1{i}"
                b8, bk8, bt8 = nb()
                P.op("pe", lambda e: e.transpose(bk8[0:nt, 0:128], kc, ident), reads=[KT, "cst"], writes=[bt8])
                P.op("dve", lambda e: e.tensor_scalar(out=kbg[i][0:nt, :], in0=bk8[0:nt, 0:128], scalar1=bg[0:nt, tt, h:h + 1], scalar2=None, op0=ALU.mult),
                     reads=[bt8, "bg"], writes=[f"kbg{i}"])
                P.op("act", lambda e: e.activation(out=kdec[pp][i][0:nt, :], in_=bk8[0:nt, 0:128], func=AF.Copy, scale=kdsc[0:nt, tt, h:h + 1]),
                     reads=[bt8, "kdsc", f"kbg{i}"], writes=[f"kdec{pp}{i}"])
                b9, bk9, bt9 = nb()
                P.op("pe", lambda e: e.transpose(bk9[0:nt, 0:128], vc, ident), reads=[VT, "cst"], writes=[bt9])
                P.op("dve", lambda e: e.tensor_scalar(out=vb[i][0:nt, :], in0=bk9[0:nt, 0:128], scalar1=beta[0:nt, tt, h:h + 1], scalar2=None, op0=ALU.mult),
                     reads=[bt9, "beta"], writes=[f"vb{i}"])
                yield
                b10, bk10, bt10 = nb()
                P.op("pe", lambda e: e.matmul(bk10[0:nt, 0:128], Tt, vb[i][0:nt, :], start=True, stop=True), reads=[TtT, f"vb{i}"], writes=[bt10])
                evac("act", uu[pp][i][0:nt, :], bk10[0:nt, 0:128], [bt10], [f"uu{pp}{i}"])
                b11, bk11, bt11 = nb()
                P.op("pe", lambda e: e.matmul(bk11[:, 0:nt], kbg[i][0:nt, :], Tt, start=True, stop=True), reads=[TtT, f"kbg{i}"], writes=[bt11])
                evac("act", wT[pp][i][:, 0:nt], bk11[:, 0:nt], [bt11], [f"wT{pp}{i}"])
                yield
                b12, bk12, bt12 = nb()
                P.op("pe", lambda e: e.matmul(bk12[0:nt, 0:nt], kc, qc, start=True, stop=True), reads=[KT, QTg], writes=[bt12])
                P.op("dve", lambda e: e.tensor_tensor(out=qkT[pp][i][0:nt, 0:nt], in0=bk12[0:nt, 0:nt], in1=DTt[i][0:nt, 0:nt], op=ALU.mult),
                     reads=[bt12, f"DTt{i}"], writes=[f"qkT{pp}{i}"])
                P.op("pool", lambda e: e.tensor_tensor(out=qdT[pp][i][:, 0:nt], in0=qc, in1=eg[:, 0:nt], op=ALU.mult), reads=[QTg, egt], writes=[f"qdT{pp}{i}"])
                yield

            def rec_full(i, h, tl):
                tt, lo, nt, tc0, chunks, pp = tile_params(tl)
                eg = egrow[pp][i]; egt = f"egrow{pp}{i}"
                for (r0, L) in chunks:
                    bA, bkA, btA = 3 + 2 * i, pb[3 + 2 * i], f"pb{3 + 2 * i}"
                    P.op("pe", lambda e: e.matmul(bkA[r0:r0 + L, 0:128], wT[pp][i][:, r0:r0 + L], Sb[i][:, :], start=True, stop=True),
                         reads=[f"wT{pp}{i}", f"Sb{i}"], writes=[btA])
                    bB, bkB, btB = 4 + 2 * i, pb[4 + 2 * i], f"pb{4 + 2 * i}"
                    P.op("pe", lambda e: e.matmul(bkB[r0:r0 + L, 0:128], qdT[pp][i][:, r0:r0 + L], Sb[i][:, :], start=True, stop=False),
                         reads=[f"qdT{pp}{i}", f"Sb{i}"], writes=[btB])
                    yield
                    P.op("dve", lambda e: e.tensor_tensor(out=vnew[i][r0:r0 + L, :], in0=uu[pp][i][r0:r0 + L, :], in1=bkA[r0:r0 + L, 0:128], op=ALU.subtract),
                         reads=[btA, f"uu{pp}{i}"], writes=[f"vnew{i}"])
                    yield
                    P.op("pe", lambda e: e.matmul(bkB[r0:r0 + L, 0:128], qkT[pp][i][r0:r0 + L, r0:r0 + L], vnew[i][r0:r0 + L, :], start=False, stop=True),
                         reads=[f"qkT{pp}{i}", f"vnew{i}"], writes=[btB])
                    bC, bkC, btC = bA, bkA, btA
                    P.op("pe", lambda e: e.matmul(bkC[:, 0:128], kdec[pp][i][r0:r0 + L, :], vnew[i][r0:r0 + L, :], start=True, stop=True),
                         reads=[f"kdec{pp}{i}", f"vnew{i}"], writes=[btC])
                    yield
                    P.op("dve", lambda e: e.scalar_tensor_tensor(out=Sb[i][:, :], in0=S[i][:, :], scalar=eg[:, r0 + L - 1:r0 + L], in1=bkC[:, 0:128],
                                                                 op0=ALU.mult, op1=ALU.add), reads=[btC, f"S{i}", egt], writes=[f"Sb{i}"])
                    P.op("dve", lambda e: e.scalar_tensor_tensor(out=S[i][:, :], in0=S[i][:, :], scalar=eg[:, r0 + L - 1:r0 + L], in1=bkC[:, 0:128],
                                                                 op0=ALU.mult, op1=ALU.add), reads=[btC, f"S{i}", egt], writes=[f"S{i}"])
                    evac("act", odn[i][r0:r0 + L, :], bkB[r0:r0 + L, 0:128], [btB], [f"odn{i}"])
                    yield
                P.op("dve", lambda e: e.bn_stats(out=st6[i][0:nt, :], in_=odn[i][0:nt, :]), reads=[f"odn{i}"], writes=[f"st6{i}"])
                P.op("dve", lambda e: e.bn_aggr(out=mvv[i][0:nt, :], in_=st6[i][0:nt, :]), reads=[f"st6{i}"], writes=[f"mvv{i}"])
                P.op("dve", lambda e: e.scalar_tensor_tensor(out=msq[i][0:nt, :], in0=mvv[i][0:nt, 0:1], scalar=mvv[i][0:nt, 0:1], in1=mvv[i][0:nt, 1:2], op0=ALU.mult, op1=ALU.add),
                     reads=[f"mvv{i}"], writes=[f"msq{i}"])
                yield
                P.op("act", lambda e: e.activation(out=msq[i][0:nt, :], in_=msq[i][0:nt, :], func=AF.Ln, bias=epsr[0:nt, 0:1]), reads=[f"msq{i}", "epsg"], writes=[f"msq{i}"])
                P.op("act", lambda e: e.activation(out=msq[i][0:nt, :], in_=msq[i][0:nt, :], func=AF.Exp, scale=-0.5), reads=[f"msq{i}"], writes=[f"msq{i}"])
                yield
                P.op("dve", lambda e: e.scalar_tensor_tensor(out=t3[i][0:nt, :], in0=odn[i][0:nt, :], scalar=msq[i][0:nt, 0:1], in1=sm[0:nt, 273:401], op0=ALU.mult, op1=ALU.mult),
                     reads=[f"odn{i}", f"msq{i}", "sm"], writes=[f"t3{i}"])
                yield
                bD, bkD, btD = 3 + 2 * i, pb[3 + 2 * i], f"pb{3 + 2 * i}"
                P.op("pe", lambda e: e.transpose(bkD[:, 0:nt], t3[i][0:nt, :], ident[0:nt, 0:nt]), reads=[f"t3{i}", "cst"], writes=[btD])
                pj = tt % 2
                P.op("dve", lambda e: e.tensor_tensor(out=oTs[pj][i][:, 0:nt], in0=bkD[:, 0:nt], in1=szT[bp][i][:, lo:lo + nt], op=ALU.mult),
                     reads=[btD, f"szT{bp}{i}"], writes=[f"oTs{pj}{i}"])
                P.op("sp", lambda e: e.dma_start(out=oscr_d[8 + h, :, tc0:tc0 + nt], in_=oTs[pj][i][:, 0:nt]),
                     reads=[f"oTs{pj}{i}"], writes=[f"oscr{8 + h}"], dma=f"d_oscd{pj}{i}", out=True)
                yield

            ntl = len(tiles)
            run([pre_chain(i, h, 0) for i, h in enumerate(hg)], bgen)
            for tl in range(ntl):
                gens = [rec_full(i, h, tl) for i, h in enumerate(hg)]
                if tl + 1 < ntl:
                    gens += [pre_chain(i, h, tl + 1) for i, h in enumerate(hg)]
                run(gens, bgen)
            run(bgen)


def _tile_w(w):
    k, n = w.shape
    return np.ascontiguousarray(w.reshape(k // 128, 128, n // 128, 128).transpose(2, 1, 0, 3))


def _constants():
    p = np.arange(128)
    same = (p[:, None] // 64) == (p[None, :] // 64)
    cst = np.zeros((128, 8, 128), np.float32)
    cst[:, 0, :] = np.eye(128)
    sw = np.zeros((128, 128), np.float32)
    for m in range(128):
        k = (m // 64) * 64 + ((m % 64) + 32) % 64
        sw[k, m] = 1.0
    cst[:, 1, :] = sw
    cst[:, 2, :] = (same & (p[:, None] <= p[None, :])).astype(np.float32)
    cst[:, 3, :] = same.astype(np.float32)
    cst[:, 4, :] = np.where(same & (p[:, None] > p[None, :]), 0.0, BIG)
    cst[:, 5, :] = np.where(same & (p[None, :] >= p[:, None]), 0.0, BIG)
    cst[:, 6, :] = 1.0
    pos = np.arange(T, dtype=np.float32)
    inv_freq = (10000.0 ** (-np.arange(0, 64, 2, dtype=np.float32) / 64)).astype(np.float32)
    ang = pos[None, :] * inv_freq[:, None]
    d = p % 64
    cos = np.cos(ang)[d % 32]
    sin = np.sin(ang)[d % 32] * np.where(d < 32, -1.0, 1.0)[:, None]
    rope = np.stack([cos, sin]).astype(np.float32)
    return cst, rope


def _host_inputs(inp):
    f = lambda a: np.ascontiguousarray(np.asarray(a, dtype=np.float32))
    w_in = f(inp["w_in"][0])
    cst, rope = _constants()
    shared = {
        "cst": cst, "rope": rope,
        "waq": _tile_w(w_in[:, 0:1024]), "wak": _tile_w(w_in[:, 1024:2048]),
        "wav": np.ascontiguousarray(w_in[:, 2048:3072].reshape(16, 128, 2, 512).transpose(2, 1, 0, 3)),
        "wdq": _tile_w(w_in[:, 3072:4096]), "wdk": _tile_w(w_in[:, 4096:5120]),
        "wdv": _tile_w(w_in[:, 5120:6144]), "wdz": _tile_w(w_in[:, 6144:7168]),
        "wba": np.ascontiguousarray(w_in[:, 7168:7184].reshape(16, 128, 16).transpose(1, 0, 2)),
        "cw": np.ascontiguousarray(f(inp["conv_qkv_w"][0]).T.reshape(3, 8, 128, 4).transpose(2, 0, 1, 3)),
        "wout": np.ascontiguousarray(f(inp["w_out"][0]).reshape(16, 128, D).transpose(1, 0, 2)),
        "lnp": np.ascontiguousarray(np.stack([np.broadcast_to(f(inp[k][0])[None, :], (128, D))
                                             for k in ("ln1_g", "ln1_b", "ln2_g", "ln2_b")])),
        "wg": _tile_w(f(inp["ffn_w_gate"][0])), "wu": _tile_w(f(inp["ffn_w_up"][0])),
        "wd": np.ascontiguousarray(f(inp["ffn_w_down"][0]).reshape(NFF, 128, 16, 128).transpose(2, 1, 0, 3)),
    }
    fc = np.concatenate([f(inp["ffn_conv_w"][0]), f(inp["ffn_conv_b"][0])[None, :]], axis=0)
    shared["fcw"] = np.ascontiguousarray(fc.T.reshape(NFF, 128, 4).transpose(1, 0, 2))
    sm = np.zeros((128, 401), np.float32)
    for i, k in enumerate(("lambda_q1", "lambda_k1", "lambda_q2", "lambda_k2")):
        sm[:, i * 64:(i + 1) * 64] = f(inp[k][0])[None, :]
    sm[:, 256:264] = f(inp["a_log"][0])[None, :]
    sm[:, 264:272] = f(inp["dt_bias"][0])[None, :]
    sm[:, 272] = f(inp["diff_norm_w"][0])
    sm[:, 273:401] = f(inp["delta_norm_w"][0])[None, :]
    shared["sm"] = sm
    x = f(inp["x"]); meta = f(inp["meta_tokens"])
    per_core = []
    for c in range(8):
        b, hf = c // 2, c % 2
        h = np.concatenate([meta, x[b]], axis=0)
        r0 = 14 + 1024 * hf
        per_core.append({"hT": np.ascontiguousarray(h.T), "hrow": np.ascontiguousarray(h[r0:r0 + NTOK])})
    return shared, per_core


_NC_CACHE = {}


def kernel(**inputs):
    shared, per_core = _host_inputs(inputs)
    if "nc" not in _NC_CACHE:
        _NC_CACHE["nc"] = build()
    nc = _NC_CACHE["nc"]
    in_maps = [{**shared, **pc} for pc in per_core]
    res = run_bass_kernel_spmd(nc, in_maps, core_ids=list(range(8)))
    out = np.empty((4, 2048, D), np.float32)
    for c in range(8):
        b, hf = c // 2, c % 2
        out[b, 1024 * hf:1024 * (hf + 1)] = res.results[c]["out"]
    return out
```
